# Optimizing a Trainium2 kernel written in Bass

```python
import math
import jax, jax.numpy as jnp
from jax import lax
import numpy as np

D_MODEL = 1024
BATCH = 8
SEQ = 4096
DEPTH = 2

CHUNK = 64
MEM_LEN = 256
EPS = 1e-6
A_WIDTH = 512
A_GROUPS = 4
A_GROUP_DIM = A_WIDTH // A_GROUPS
GMLP_BLOCK = 128
B_WIDTH = 512
CONV_WIDTH = 31
MIX_WIDTH = A_WIDTH + B_WIDTH
IN_WIDTH = 2 * A_WIDTH + 2 * B_WIDTH
C_WIDTH = 512
C_GROUP_CH = 16
C_GROUPS = C_WIDTH // C_GROUP_CH
C_STATE = 64
DT_MIN = 1e-3
DT_MAX = 1e-1
CA_HEADS = 4
CA_HEAD_DIM = D_MODEL // CA_HEADS
FFN_HIDDEN = -(-8 * D_MODEL // (3 * 256)) * 256
N_EVEN = (DEPTH + 1) // 2
N_ODD = DEPTH // 2

kernel_name = "chunk_causal_hybrid_gmlp_conformer_s5_trunk"


def rmsnorm(x, g):
    xf = x.astype(jnp.float32)
    y = xf * lax.rsqrt(jnp.mean(xf * xf, axis=-1, keepdims=True) + EPS)
    return (y * g.astype(jnp.float32)).astype(x.dtype)


def layernorm(x, g=None, b=None):
    xf = x.astype(jnp.float32)
    mu = jnp.mean(xf, axis=-1, keepdims=True)
    xc = xf - mu
    y = xc * lax.rsqrt(jnp.mean(xc * xc, axis=-1, keepdims=True) + EPS)
    if g is not None:
        y = y * g.astype(jnp.float32) + b.astype(jnp.float32)
    return y.astype(x.dtype)


def gmlp_spatial_gate(u, v, w_s, b_s):
    bn, s, _ = v.shape
    v = layernorm(v)
    v = v.reshape(bn, s // GMLP_BLOCK, GMLP_BLOCK, A_GROUPS, A_GROUP_DIM)
    chunk_id = jnp.arange(GMLP_BLOCK) // CHUNK
    mask = chunk_id[None, :] <= chunk_id[:, None]
    w = jnp.where(mask[None], w_s, jnp.zeros_like(w_s))
    sg = jnp.einsum('gij,bnjgc->bnigc', w, v) + b_s.T[None, None, :, :, None]
    return u * sg.reshape(bn, s, A_WIDTH)


def conformer_conv(a, g, conv_w, conv_b, ln_g, ln_b):
    h = a * jax.nn.sigmoid(g)
    h = lax.conv_general_dilated(
        h, conv_w[:, None, :].astype(h.dtype), window_strides=(1,),
        padding=[(CONV_WIDTH - 1, 0)], dimension_numbers=('NWC', 'WIO', 'NWC'),
        feature_group_count=B_WIDTH) + conv_b
    h = layernorm(h, ln_g, ln_b)
    return jax.nn.silu(h)


def _complex_affine_combine(e1, e2):
    a1r, a1i, b1r, b1i = e1
    a2r, a2i, b2r, b2i = e2
    ar = a1r * a2r - a1i * a2i
    ai = a1r * a2i + a1i * a2r
    br = a2r * b1r - a2i * b1i + b2r
    bi = a2r * b1i + a2i * b1r + b2i
    return (ar, ai, br, bi)


def s5_layer(u, lam_re, lam_im, log_dt, b_re, b_im, c_re, c_im, d_skip):
    bn, s, _ = u.shape
    f32 = jnp.float32
    uf = u.astype(f32)
    dt = jnp.exp(log_dt.astype(f32))[:, None]
    lr = lam_re.astype(f32)
    li = lam_im.astype(f32)
    mag = jnp.exp(lr * dt)
    ar = mag * jnp.cos(li * dt)
    ai = mag * jnp.sin(li * dt)
    den = lr * lr + li * li
    qr = ((ar - 1.0) * lr + ai * li) / den
    qi = (ai * lr - (ar - 1.0) * li) / den
    br_ = b_re.astype(f32)
    bi_ = b_im.astype(f32)
    bbr = qr[..., None] * br_ - qi[..., None] * bi_
    bbi = qr[..., None] * bi_ + qi[..., None] * br_
    ug = uf.reshape(bn, s, C_GROUPS, C_GROUP_CH).transpose(1, 0, 2, 3)
    bu_r = jnp.einsum('sbgc,gpc->sbgp', ug, bbr)
    bu_i = jnp.einsum('sbgc,gpc->sbgp', ug, bbi)
    a_r = jnp.broadcast_to(ar[None, None], (s, 1, C_GROUPS, C_STATE))
    a_i = jnp.broadcast_to(ai[None, None], (s, 1, C_GROUPS, C_STATE))
    _, _, xr, xi = lax.associative_scan(_complex_affine_combine, (a_r, a_i, bu_r, bu_i), axis=0)
    y = (jnp.einsum('sbgp,gcp->sbgc', xr, c_re.astype(f32))
         - jnp.einsum('sbgp,gcp->sbgc', xi, c_im.astype(f32)))
    y = y.transpose(1, 0, 2, 3).reshape(bn, s, C_WIDTH) + d_skip.astype(f32) * uf
    return y.astype(u.dtype)


def cross_attention(xn, memn, wq, wk, wv, wo):
    bn, s, _ = xn.shape
    m = memn.shape[1]
    q = (xn @ wq).reshape(bn, s, CA_HEADS, CA_HEAD_DIM)
    k = (memn @ wk).reshape(bn, m, CA_HEADS, CA_HEAD_DIM)
    v = (memn @ wv).reshape(bn, m, CA_HEADS, CA_HEAD_DIM)
    sc = jnp.einsum('bshd,bmhd->bhsm', q, k).astype(jnp.float32) * (CA_HEAD_DIM ** -0.5)
    p = jax.nn.softmax(sc, axis=-1).astype(v.dtype)
    o = jnp.einsum('bhsm,bmhd->bshd', p, v).reshape(bn, s, D_MODEL)
    return o @ wo


def swiglu(xn, wg, wu, wd):
    return (jax.nn.silu(xn @ wg) * (xn @ wu)) @ wd


def setup_inputs(seed: int = 0) -> dict:
    key = jax.random.key(seed)
    ks = iter(jax.random.split(key, 48))

    def nrm(shape, scale):
        return jax.random.normal(next(ks), shape, jnp.float32) * scale

    def gain(shape):
        return 1.0 + nrm(shape, 0.02)

    d, h = D_MODEL, FFN_HIDDEN
    inp = {}
    inp['x'] = nrm((BATCH, SEQ, d), 1.0)
    inp['mem'] = nrm((BATCH, MEM_LEN, d), 1.0)
    inp['e_norm'] = gain((N_EVEN, d))
    inp['e_w_in'] = nrm((N_EVEN, d, IN_WIDTH), d ** -0.5)
    inp['e_gmlp_w'] = nrm((N_EVEN, A_GROUPS, GMLP_BLOCK, GMLP_BLOCK), 0.5 * GMLP_BLOCK ** -0.5)
    inp['e_gmlp_b'] = gain((N_EVEN, A_GROUPS, GMLP_BLOCK))
    inp['e_conv_w'] = nrm((N_EVEN, CONV_WIDTH, B_WIDTH), CONV_WIDTH ** -0.5)
    inp['e_conv_b'] = nrm((N_EVEN, B_WIDTH), 0.02)
    inp['e_conv_ln_g'] = gain((N_EVEN, B_WIDTH))
    inp['e_conv_ln_b'] = nrm((N_EVEN, B_WIDTH), 0.02)
    inp['e_w_out'] = nrm((N_EVEN, MIX_WIDTH, d), MIX_WIDTH ** -0.5)
    inp['o_norm'] = gain((N_ODD, d))
    inp['o_w_in'] = nrm((N_ODD, d, C_WIDTH), d ** -0.5)
    inp['o_lam_re'] = -0.5 + nrm((N_ODD, C_GROUPS, C_STATE), 0.01)
    inp['o_lam_im'] = (math.pi * jnp.arange(C_STATE, dtype=jnp.float32))[None, None, :] + nrm((N_ODD, C_GROUPS, C_STATE), 0.01)
    inp['o_log_dt'] = jax.random.uniform(next(ks), (N_ODD, C_GROUPS), jnp.float32, math.log(DT_MIN), math.log(DT_MAX))
    inp['o_b_re'] = nrm((N_ODD, C_GROUPS, C_STATE, C_GROUP_CH), (2 * C_GROUP_CH) ** -0.5)
    inp['o_b_im'] = nrm((N_ODD, C_GROUPS, C_STATE, C_GROUP_CH), (2 * C_GROUP_CH) ** -0.5)
    inp['o_c_re'] = nrm((N_ODD, C_GROUPS, C_GROUP_CH, C_STATE), (2 * C_STATE) ** -0.5)
    inp['o_c_im'] = nrm((N_ODD, C_GROUPS, C_GROUP_CH, C_STATE), (2 * C_STATE) ** -0.5)
    inp['o_d'] = gain((N_ODD, C_WIDTH))
    inp['o_w_out'] = nrm((N_ODD, C_WIDTH, 2 * d), C_WIDTH ** -0.5)
    inp['ca_norm'] = gain((DEPTH, d))
    inp['ca_mem_norm'] = gain((DEPTH, d))
    inp['ca_wq'] = nrm((DEPTH, d, d), d ** -0.5)
    inp['ca_wk'] = nrm((DEPTH, d, d), d ** -0.5)
    inp['ca_wv'] = nrm((DEPTH, d, d), d ** -0.5)
    inp['ca_wo'] = nrm((DEPTH, d, d), d ** -0.5)
    inp['ffn_norm'] = gain((DEPTH, d))
    inp['ffn_w_gate'] = nrm((DEPTH, d, h), d ** -0.5)
    inp['ffn_w_up'] = nrm((DEPTH, d, h), d ** -0.5)
    inp['ffn_w_down'] = nrm((DEPTH, h, d), h ** -0.5)
    inp['final_norm'] = gain((d,))
    return inp


def reference(x, mem, e_norm, e_w_in, e_gmlp_w, e_gmlp_b, e_conv_w, e_conv_b, e_conv_ln_g,
              e_conv_ln_b, e_w_out, o_norm, o_w_in, o_lam_re, o_lam_im, o_log_dt, o_b_re,
              o_b_im, o_c_re, o_c_im, o_d, o_w_out, ca_norm, ca_mem_norm, ca_wq, ca_wk, ca_wv,
              ca_wo, ffn_norm, ffn_w_gate, ffn_w_up, ffn_w_down, final_norm):
    for i in range(DEPTH):
        j = i // 2
        if i % 2 == 0:
            hn = rmsnorm(x, e_norm[j])
            proj = hn @ e_w_in[j]
            a_u, a_v, b_a, b_g = jnp.split(proj, [A_WIDTH, 2 * A_WIDTH, 2 * A_WIDTH + B_WIDTH], axis=-1)
            out_a = gmlp_spatial_gate(jax.nn.gelu(a_u), jax.nn.gelu(a_v), e_gmlp_w[j], e_gmlp_b[j])
            out_b = conformer_conv(b_a, b_g, e_conv_w[j], e_conv_b[j], e_conv_ln_g[j], e_conv_ln_b[j])
            mix = jnp.concatenate([out_a, out_b], axis=-1) @ e_w_out[j]
        else:
            hn = rmsnorm(x, o_norm[j])
            u = hn @ o_w_in[j]
            y = s5_layer(u, o_lam_re[j], o_lam_im[j], o_log_dt[j], o_b_re[j], o_b_im[j],
                         o_c_re[j], o_c_im[j], o_d[j])
            o = jax.nn.gelu(y) @ o_w_out[j]
            mix = o[..., :D_MODEL] * jax.nn.sigmoid(o[..., D_MODEL:])
        x = x + mix
        x = x + cross_attention(rmsnorm(x, ca_norm[i]), rmsnorm(mem, ca_mem_norm[i]),
                                ca_wq[i], ca_wk[i], ca_wv[i], ca_wo[i])
        x = x + swiglu(rmsnorm(x, ffn_norm[i]), ffn_w_gate[i], ffn_w_up[i], ffn_w_down[i])
    return rmsnorm(x, final_norm)
```

```python
import contextlib
import numpy as np
import concourse.bass as bass
import concourse.mybir as mybir
from concourse.bass_utils import run_bass_kernel_spmd

F32 = mybir.dt.float32
BF16 = mybir.dt.bfloat16
AF = mybir.ActivationFunctionType
ALU = mybir.AluOpType

D = 1024
SEQ = 4096
T = 512
DC = 8
H = 2816
HC = 22
MEM = 256
LS = 64
SLOT = 8448
NSLOT = 3
EPS = 1e-6
PI = float(np.pi)

ENGS = ("pe", "act", "dve", "pool", "sp")
NDMA_SEM = 12

G_E, G_O, G_CA, G_FFN, G_MEM, G_CB, G_LG, G_LB, G_OD, G_N = 0, 8, 16, 32, 48, 64, 68, 72, 76, 80


class Buf:
    __slots__ = ("name", "w", "r", "excl")

    def __init__(self, name="", excl=False):
        self.name = name
        self.w = None
        self.r = {}
        self.excl = excl


class Sched:
    def __init__(self, nc):
        self.nc = nc
        self.streams = {e: [] for e in ENGS}
        self.cnt = {e: 0 for e in ENGS}
        self.seen = {e: {} for e in ENGS}
        self.dma_cnt = {}
        self.dma_rr = {e: 0 for e in ENGS}

    def _need(self, eng, tok, out):
        if tok is None:
            return
        key, val, src = tok
        if self.seen[eng].get(key, 0) >= val:
            return
        out[key] = max(out.get(key, 0), val)

    def _deps(self, eng, reads, writes, same_eng_raw=True):
        need = {}
        for b in reads:
            if b.w is not None:
                if b.w[2] == eng and not same_eng_raw:
                    continue
                self._need(eng, b.w, need)
            if b.excl:
                for key, (val, src) in b.r.items():
                    if src != eng:
                        self._need(eng, (key, val, src), need)
        for b in writes:
            if b.w is not None and (b.w[2] != eng or eng != "pe"):
                self._need(eng, b.w, need)
            for key, (val, src) in b.r.items():
                if src != eng or eng != "pe":
                    self._need(eng, (key, val, src), need)
        for key, val in need.items():
            self.streams[eng].append(("wait", key, val))
            self.seen[eng][key] = val

    def _commit(self, tok, reads, writes):
        key, val, src = tok
        for b in reads:
            b.r[key] = (val, src)
        for b in writes:
            b.w = tok
            b.r = {}

    def op(self, eng, fn, reads=(), writes=()):
        self._deps(eng, reads, writes, same_eng_raw=(eng != "pe"))
        self.cnt[eng] += 1
        tok = (("e", eng), self.cnt[eng], eng)
        self.streams[eng].append(("op", fn))
        self._commit(tok, reads, writes)

    def dma(self, q, fn, reads=(), writes=()):
        i = self.dma_rr[q]
        self.dma_rr[q] = (i + 1) % NDMA_SEM
        key = ("d", q, i)
        prev = self.dma_cnt.get(key, 0)
        self._deps(q, reads, writes)
        if prev > 0 and self.seen[q].get(key, 0) < prev:
            self.streams[q].append(("wait", key, prev))
            self.seen[q][key] = prev
        val = prev + 16
        self.dma_cnt[key] = val
        self.streams[q].append(("dma", fn, key))
        self._commit((key, val, "dma"), reads, writes)

    def barrier(self):
        keys = {("e", e): self.cnt[e] for e in ENGS if self.cnt[e] > 0}
        keys.update(self.dma_cnt)
        for eng in ENGS:
            for key, val in keys.items():
                if key == ("e", "pe") and eng == "pe":
                    continue
                if self.seen[eng].get(key, 0) < val:
                    self.streams[eng].append(("wait", key, val))
                    self.seen[eng][key] = val

    def final_wait(self, eng, bufs):
        need = {}
        for b in bufs:
            self._need(eng, b.w, need)
        for key, val in need.items():
            self.streams[eng].append(("wait", key, val))
            self.seen[eng][key] = val

    def emit(self):
        nc = self.nc
        with contextlib.ExitStack() as st:
            sems = {}
            for e in ENGS:
                sems[("e", e)] = st.enter_context(nc.semaphore("s_" + e))
            for key in self.dma_cnt:
                sems[key] = st.enter_context(nc.semaphore("d_%s%d" % (key[1], key[2])))
            block = st.enter_context(nc.Block())

            def run(stream, ename):
                def body(h):
                    for item in stream:
                        if item[0] == "wait":
                            h.wait_ge(sems[item[1]], item[2])
                        elif item[0] == "op":
                            item[1](h).then_inc(sems[("e", ename)], 1)
                        else:
                            item[1](h).then_inc(sems[item[2]], 16)
                return body

            block.tensor(run(self.streams["pe"], "pe"))
            block.scalar(run(self.streams["act"], "act"))
            block.vector(run(self.streams["dve"], "dve"))
            block.gpsimd(run(self.streams["pool"], "pool"))
            block.sync(run(self.streams["sp"], "sp"))


PARAM_SHAPES = {
    "e_norm": [1, 1024], "e_w_in": [1, 1024, 2048], "e_gmlp_w": [1, 4, 128, 128], "e_gmlp_b": [1, 4, 128],
    "e_conv_w": [1, 31, 512], "e_conv_b": [1, 512], "e_conv_ln_g": [1, 512], "e_conv_ln_b": [1, 512],
    "e_w_out": [1, 1024, 1024], "o_norm": [1, 1024], "o_w_in": [1, 1024, 512], "o_lam_re": [1, 32, 64],
    "o_lam_im": [1, 32, 64], "o_log_dt": [1, 32], "o_b_re": [1, 32, 64, 16], "o_b_im": [1, 32, 64, 16],
    "o_c_re": [1, 32, 16, 64], "o_c_im": [1, 32, 16, 64], "o_d": [1, 512], "o_w_out": [1, 512, 2048],
    "ca_norm": [2, 1024], "ca_mem_norm": [2, 1024], "ca_wq": [2, 1024, 1024], "ca_wk": [2, 1024, 1024],
    "ca_wv": [2, 1024, 1024], "ca_wo": [2, 1024, 1024], "ffn_norm": [2, 1024], "ffn_w_gate": [2, 1024, 2816],
    "ffn_w_up": [2, 1024, 2816], "ffn_w_down": [2, 2816, 1024], "final_norm": [1024],
}


class KB:
    def __init__(self, nc, phases, ntiles, final_norm=True):
        self.nc = nc
        self.S = Sched(nc)
        self.phases = phases
        self.ntiles = ntiles
        self.final_norm = final_norm
        self.ps_rr = 0
        self.ring_rr = 0
        self.scr_rr = 0
        self.sq_rr = 0
        self.dbg_bufs = []
        self.deng = "dve"
        self.dbg = False

    def mm(self, out, lhsT, rhs, start, stop, R, W):
        self.S.op("pe", lambda h: h.matmul(out, lhsT, rhs, start=start, stop=stop), reads=R, writes=W)

    def tr(self, out, in_, R, W, ident=None):
        ident = self.ident if ident is None else ident
        self.S.op("pe", lambda h: h.transpose(out, in_, ident[:]), reads=list(R) + [self.b_const], writes=W)

    def act(self, out, in_, func, R, W, bias=None, scale=None, accum=None):
        kw = {}
        if bias is not None:
            kw["bias"] = bias
        if scale is not None:
            kw["scale"] = scale
        if accum is not None:
            kw["accum_out"] = accum
        self.S.op("act", lambda h: h.activation(out, in_, func, **kw), reads=R, writes=W)

    def tt(self, out, a, b, op, R, W, eng=None):
        eng = eng or self.deng
        self.S.op(eng, lambda h: h.tensor_tensor(out, a, b, op), reads=R, writes=W)

    def ts(self, out, a, s1, s2, op0, op1, R, W, eng=None):
        eng = eng or self.deng
        if op1 is None:
            self.S.op(eng, lambda h: h.tensor_scalar(out, a, s1, s2, op0), reads=R, writes=W)
        else:
            self.S.op(eng, lambda h: h.tensor_scalar(out, a, s1, s2, op0, op1), reads=R, writes=W)

    def stt(self, out, in0, scalar, in1, op0, op1, R, W, eng=None):
        eng = eng or self.deng
        self.S.op(eng, lambda h: h.scalar_tensor_tensor(out, in0, scalar, in1, op0, op1), reads=R, writes=W)

    def cp(self, out, in_, R, W, eng=None):
        eng = eng or self.deng
        self.S.op(eng, lambda h: h.tensor_copy(out, in_), reads=R, writes=W)

    def recip(self, out, in_, R, W):
        self.S.op("dve", lambda h: h.reciprocal(out, in_), reads=R, writes=W)

    def memset(self, ap, val, W, eng=None):
        eng = eng or self.deng
        self.S.op(eng, lambda h: h.memset(ap, val), writes=W)

    def dma(self, q, out, in_, R, W, slow=False):
        if slow:
            self.S.dma(q, lambda h: h.dma_start(out=out, in_=in_, allow_slow_non_contiguous=True), reads=R, writes=W)
        else:
            self.S.dma(q, lambda h: h.dma_start(out=out, in_=in_), reads=R, writes=W)

    def dump(self, name, ap, bufs, dt=F32):
        d = self.nc.dram_tensor("dbg_" + name, list(ap.shape), dt, kind="ExternalOutput").ap()
        b = Buf()
        self.dma("sp", d, ap, list(bufs), [b])
        self.dbg_bufs.append(b)

    def bank(self):
        i = self.ps_rr
        self.ps_rr = (i + 1) % 8
        return self.psum[:, i, :], self.b_ps[i]

    def scr_next(self):
        i = self.scr_rr
        self.scr_rr = (i + 1) % 4
        return self.scr[:, i, :], self.b_scr[i]

    def load_slab(self, src, a, b, q="pool", R=()):
        i = self.ring_rr
        self.ring_rr = (i + 1) % NSLOT
        dst = self.ring[i][:, 0:a * b].rearrange("p (a b) -> p a b", b=b)
        self.dma(q, dst, src, list(R), [self.b_ring[i]])
        return dst, self.b_ring[i]

    def load_w(self, W2d, c0, n):
        kc = W2d.shape[0] // 128
        src = W2d[:, c0:c0 + n].rearrange("(c p) f -> p c f", p=128)
        return self.load_slab(src, kc, n)

    def build(self):
        nc = self.nc
        dr = lambda name, shape, kind="ExternalInput", dt=F32: nc.dram_tensor(name, shape, dt, kind=kind).ap()
        self.x_d = dr("x", [SEQ, D])
        self.mem_d = dr("mem", [MEM, D])
        self.p = {k: dr(k, s) for k, s in PARAM_SHAPES.items()}
        self.ident_d = dr("c_ident", [128, 128])
        self.out_d = dr("out", [SEQ, D], kind="ExternalOutput")
        self.diag_d = dr("scr_diag", [2, 128, 62 * 128], kind="Internal", dt=BF16)
        self.s5_d = dr("scr_s5", [128, 544 * 128], kind="Internal", dt=BF16)
        with contextlib.ExitStack() as st:
            sb = lambda n, s, d: st.enter_context(nc.sbuf_tensor(n, s, d))
            self.xT = sb("xT", [128, 8, 512], F32); self.b_xT = [Buf() for _ in range(8)]
            self.xn = sb("xn", [128, 8, 512], BF16); self.b_xn = [Buf() for _ in range(8)]
            self.sq = sb("sq", [128, 4, 512], BF16); self.b_sq = [Buf() for _ in range(4)]
            self.rstd = sb("rstd", [128, 512], F32); self.b_rstd = Buf()
            self.ring = [sb("ring%d" % i, [128, SLOT], BF16) for i in range(NSLOT)]
            self.b_ring = [Buf() for _ in range(NSLOT)]
            self.bigbf = sb("bigbf", [128, 22, 512], BF16); self.b_bb = [Buf() for _ in range(22)]
            self.bigf = sb("bigf", [128, 8, 512], F32); self.b_bf = [Buf() for _ in range(8)]
            self.hbuf = sb("hbuf", [128, 4, 542], BF16); self.b_hb = [Buf() for _ in range(4)]
            self.scr = sb("scr", [128, 4, 512], F32); self.b_scr = [Buf() for _ in range(4)]
            self.aux = sb("aux", [128, 2, 512], F32); self.b_aux = [Buf() for _ in range(2)]
            self.KT = sb("KT", [128, 2, 8, 256], BF16); self.b_KT = Buf()
            self.V = sb("V", [128, 2, 2, 1024], BF16); self.b_V = Buf()
            self.gw = sb("gw", [128, 4, 128], BF16)
            self.bsbc = sb("bsbc", [128, 4, 128], F32)
            self.ident = sb("ident", [128, 128], F32)
            self.ident_bf = sb("ident_bf", [128, 128], BF16)
            self.ones_bf = sb("ones_bf", [128, 128], BF16)
            self.ones_f = sb("ones_f", [128, 128], F32)
            self.epsb = sb("epsb", [128, 1], F32)
            self.gv = sb("gv", [128, G_N], F32)
            self.fg = sb("fg", [128, 1024], F32)
            self.small = sb("small", [128, 64], F32); self.b_small = [Buf() for _ in range(8)]
            self.b_const = Buf()
            self.cosT = sb("cosT", [128, 16, 16], F32)
            self.sinT = sb("sinT", [128, 16, 16], F32)
            self.c8T = sb("c8T", [128, 16, 64], F32)
            self.s8T = sb("s8T", [128, 16, 64], F32)
            self.r8T = sb("r8T", [128, 16, 64], F32)
            self.apw = sb("apw", [128, 3, 9, 16], F32); self.b_apw = Buf()
            self.ktap = sb("ktap", [128, 32, 128], BF16); self.b_ktap = Buf()
            self.s5c = sb("s5c", [128, 8, 16], F32)
            self.b_s5c = [Buf() for _ in range(8)]
            self.b_tab = Buf()
            self.s5t = sb("s5t", [128, 6, 1024], F32); self.b_s5t = [Buf() for _ in range(6)]
            self.psum = st.enter_context(nc.psum_tensor("ps", [128, 8, 512], F32))
            self.b_ps = [Buf(excl=True) for _ in range(8)]
            self.b_out = Buf()
            self.b_diag = Buf()
            self.b_s5d = Buf()

            self.prologue()
            self.S.barrier()
            two = lambda n: [[Buf() for _ in range(n)] for _ in range(2)]
            self.hb_xT = two(8); self.hb_xn = two(8); self.hb_bb = two(22); self.hb_bf = two(8)
            self.hb_aux = two(2); self.hb_rstd = [Buf(), Buf()]
            self.hb_h = [[Buf(), Buf(), Buf()] for _ in range(4)]
            self.hb_scr = [Buf() for _ in range(8)]
            self.hb_sq = [Buf() for _ in range(8)]
            self.hb_small = two(8)
            self.hb_s5t = [Buf() for _ in range(6)]
            self.scr_rr = 0
            self.sq_rr = 0
            self.ps_rr = 0
            for ti in range(self.ntiles):
                self.tile(ti)
            self.S.final_wait("sp", [self.b_out] + self.dbg_bufs)
            self.S.emit()
        return nc

    def prologue(self):
        p = self.p
        cW = [self.b_const]
        self.dma("sp", self.ident[:], self.ident_d, [], cW)
        if "l0mix" in self.phases:
            self.pro_gc_loads()
        self.memset(self.ones_bf[:], 1.0, cW)
        self.memset(self.ones_f[:], 1.0, cW)
        self.memset(self.epsb[:], EPS, cW)
        for fc in range(4):
            self.memset(self.hbuf[:, fc, 0:30], 0.0, [self.b_hb[fc]])
        gv = self.gv
        G = self.s5t[:, 5, 0:128]
        bG = [self.b_s5t[5]]
        self.memset(G, 0.0, bG)

        def ld_gain(row, src2d):
            n = src2d.shape[0]
            self.dma("sp", G[row:row + n, :], src2d, [], bG)

        r8 = lambda v: v.rearrange("(c p) -> c p", p=128)
        ld_gain(G_E, r8(p["e_norm"][0]))
        ld_gain(G_O, r8(p["o_norm"][0]))
        ld_gain(G_CA, p["ca_norm"].rearrange("l (c p) -> (l c) p", p=128))
        ld_gain(G_FFN, p["ffn_norm"].rearrange("l (c p) -> (l c) p", p=128))
        ld_gain(G_MEM, p["ca_mem_norm"].rearrange("l (c p) -> (l c) p", p=128))
        ld_gain(G_CB, r8(p["e_conv_b"][0]))
        ld_gain(G_LG, r8(p["e_conv_ln_g"][0]))
        ld_gain(G_LB, r8(p["e_conv_ln_b"][0]))
        ld_gain(G_OD, r8(p["o_d"][0]))
        pb, bpb = self.bank()
        self.tr(pb[:, 0:128], G, bG, [bpb])
        self.cp(gv[:, 0:G_N], pb[:, 0:G_N], [bpb], cW)
        need_mem = "ca0" in self.phases or "ca1" in self.phases
        if need_mem:
            self.pro_mem_loads()
        if "l1mix" in self.phases:
            self.pro_s5_loads()
        self.dma("sp", self.fg[:], p["final_norm"].partition_broadcast(128), [], cW)
        self.dma("sp", self.bsbc[:].rearrange("p g i -> p (g i)"),
                 p["e_gmlp_b"][0].rearrange("g i -> (g i)").partition_broadcast(128), [], cW)
        if "l0mix" in self.phases:
            self.pro_gmlp_conv()
        g5 = self.pro_s5() if "l1mix" in self.phases else iter(())
        next(g5, None)
        if need_mem:
            self.pro_mem()
        next(g5, None)

    def pro_gc_loads(self):
        p = self.p
        cw = self.s5t[:, 1, 0:512]
        self.memset(cw, 0.0, [self.b_s5t[1]])
        self.dma("sp", cw[0:31, :], p["e_conv_w"][0], [], [self.b_s5t[1]])
        wtmp = self.s5t[:, 0, 0:512].rearrange("p (g j) -> p g j", g=4)
        self.dma("sp", wtmp, p["e_gmlp_w"][0].rearrange("g i j -> i g j"), [], [self.b_s5t[0]])

    def pro_gmlp_conv(self):
        p = self.p
        cW = [self.b_const]
        wtmp = self.s5t[:, 0, 0:512].rearrange("p (g j) -> p g j", g=4)
        for g in range(4):
            pb, bpb = self.bank()
            self.tr(pb[:, 0:128], wtmp[:, g, :], [self.b_s5t[0]], [bpb])
            self.cp(self.gw[:, g, :], pb[:, 0:128], [bpb], cW)
            self.memset(self.gw[64:128, g, 0:64], 0.0, cW)
        cw = self.s5t[:, 1, 0:512]
        cwT = self.aux[:, 0, 0:128].rearrange("p (f k) -> p f k", k=32)
        for fc in range(4):
            pb, bpb = self.bank()
            self.tr(pb[:, 0:128], cw[:, fc * 128:(fc + 1) * 128], [self.b_s5t[1]], [bpb])
            self.cp(cwT[:, fc, 0:31], pb[:, 0:31], [bpb], [self.b_aux[0]])
        stage = self.bigbf[:].rearrange("p a b -> p (a b)")
        bdgb = [Buf() for _ in range(62)]
        for half in range(2):
            for i in range(62):
                fc = half * 2 + i // 31
                k = i % 31
                self.ts(stage[:, i * 128:(i + 1) * 128], self.ident[:], cwT[:, fc, k:k + 1], None, ALU.mult, None,
                        [self.b_aux[0], self.b_const], [bdgb[i]])
            self.dma("sp", self.diag_d[half], stage[:, 0:62 * 128], bdgb + self.b_bb[0:16], [self.b_diag])

    def rms_generic(self, src_fn, bsrc, n, gcol, dst_fn, bdst, rstd=None, brstd=None, sqbufs=None):
        if rstd is None:
            rstd = self.rstd[:, 0:n]; brstd = self.b_rstd
        if sqbufs is None:
            sqbufs = self.b_sq
        sqf = self.sq[:].rearrange("p a b -> p (a b)")
        nsq = len(sqbufs)
        w = 2048 // nsq
        pb, bpb = self.bank()
        for dc in range(8):
            k = self.sq_rr % nsq
            self.sq_rr = k + 1
            sq = sqf[:, k * w:k * w + n]
            self.act(sq, src_fn(dc), AF.Square, [bsrc[dc]], [sqbufs[k]])
            self.mm(pb[:, 0:n], self.ones_bf[:], sq, dc == 0, dc == 7, [sqbufs[k], self.b_const], [bpb])
        self.act(rstd, pb[:, 0:n], AF.Sqrt, [bpb, self.b_const], [brstd], bias=self.epsb[:], scale=1.0 / D)
        self.recip(rstd, rstd, [brstd], [brstd])
        for dc in range(8):
            self.stt(dst_fn(dc), src_fn(dc), self.gv[:, gcol + dc:gcol + dc + 1], rstd,
                     ALU.mult, ALU.mult, [bsrc[dc], brstd, self.b_const], [bdst[dc]])

    def pro_mem_loads(self):
        p = self.p
        memtok = self.bigf[:, 0:4, :].rearrange("p a b -> p (a b)").rearrange("p (m d) -> p m d", m=2)
        self.dma("sp", memtok, self.mem_d.rearrange("(m p) d -> p m d", p=128), [], self.b_bf[0:4])
        self.mem_w = [self.load_w(p["ca_wk"][0], 0, 1024), self.load_w(p["ca_wv"][0], 0, 1024),
                      self.load_w(p["ca_wk"][1], 0, 1024)]

    def pro_mem(self):
        p = self.p
        memtok = self.bigf[:, 0:4, :].rearrange("p a b -> p (a b)").rearrange("p (m d) -> p m d", m=2)
        memT = self.bigf[:, 4:8, :].rearrange("p a b -> p (a b)").rearrange("p (c m) -> p c m", c=8)
        bmT = [self.b_bf[4 + dc // 2] for dc in range(8)]
        for dc in range(8):
            pb, bpb = self.bank()
            for mb in range(2):
                self.tr(pb[:, mb * 128:(mb + 1) * 128], memtok[:, mb, dc * 128:(dc + 1) * 128], self.b_bf[0:4], [bpb])
            self.cp(memT[:, dc, :], pb[:, 0:256], [bpb], [bmT[dc]])
        for li in range(2):
            self.rms_generic(lambda dc: memT[:, dc, :], bmT, 256, G_MEM + 8 * li,
                             lambda dc: self.xn[:, dc, 0:256], self.b_xn)
            wk, bwk = self.mem_w[2 * li]
            for oc in range(8):
                pb, bpb = self.bank()
                for dc in range(8):
                    self.mm(pb[:, 0:256], wk[:, dc, oc * 128:(oc + 1) * 128], self.xn[:, dc, 0:256], dc == 0, dc == 7,
                            [bwk, self.b_xn[dc]], [bpb])
                self.act(self.KT[:, li, oc, :], pb[:, 0:256], AF.Copy, [bpb], [self.b_KT])
            wv, bwv = self.mem_w[1] if li == 0 else self.load_w(p["ca_wv"][li], 0, 1024)
            for mc in range(2):
                for half in range(2):
                    pb, bpb = self.bank()
                    for dc in range(8):
                        self.mm(pb, self.xn[:, dc, mc * 128:(mc + 1) * 128], wv[:, dc, half * 512:(half + 1) * 512],
                                dc == 0, dc == 7, [bwv, self.b_xn[dc]], [bpb])
                    self.act(self.V[:, li, mc, half * 512:(half + 1) * 512], pb, AF.Copy, [bpb], [self.b_V])

    def rot_tables(self, cosT, sinT, n, g_re, g_im, tW):
        bg = self.b_aux[1]
        g2a = self.aux[:, 1, 32:48]; g2b = self.aux[:, 1, 48:64]; g2c = self.aux[:, 1, 64:80]
        ta = self.s5t[:, 4, :].rearrange("p (q l) -> p q l", q=16)
        tb = self.s5t[:, 5, :].rearrange("p (q l) -> p q l", q=16)
        W0 = [self.b_s5t[4]]; W1 = [self.b_s5t[5]]
        s = 1
        while s < n:
            if s >= 2:
                gb_re = g_re.unsqueeze(2).to_broadcast([128, 16, s])
                gb_im = g_im.unsqueeze(2).to_broadcast([128, 16, s])
                Rg = [self.b_tab, bg]
                self.tt(ta[:, :, 0:s], cosT[:, :, 0:s], gb_re, ALU.mult, Rg, W0)
                self.tt(tb[:, :, 0:s], sinT[:, :, 0:s], gb_im, ALU.mult, Rg, W1)
                self.tt(cosT[:, :, s:2 * s], ta[:, :, 0:s], tb[:, :, 0:s], ALU.subtract, W0 + W1, tW)
                self.tt(ta[:, :, 0:s], cosT[:, :, 0:s], gb_im, ALU.mult, Rg, W0)
                self.tt(tb[:, :, 0:s], sinT[:, :, 0:s], gb_re, ALU.mult, Rg, W1)
                self.tt(sinT[:, :, s:2 * s], ta[:, :, 0:s], tb[:, :, 0:s], ALU.add, W0 + W1, tW)
            if 2 * s < n:
                self.tt(g2a, g_re, g_re, ALU.mult, [bg], [bg])
                self.tt(g2b, g_im, g_im, ALU.mult, [bg], [bg])
                self.tt(g2c, g_re, g_im, ALU.mult, [bg], [bg])
                self.tt(g_re, g2a, g2b, ALU.subtract, [bg], [bg])
                self.ts(g_im, g2c, 2.0, None, ALU.mult, None, [bg], [bg])
            s *= 2

    def pro_s5_loads(self):
        p = self.p
        sm = self.small
        bsm = self.b_small
        L = self.s5t[:, 5, 128:256]
        bL = [self.b_s5t[5]]
        self.memset(L, 0.0, bL)
        for gi in range(2):
            self.dma("sp", L[0:16, 64 * gi:64 * gi + 64], p["o_lam_re"][0].rearrange("(q gi) p -> gi q p", gi=2)[gi], [], bL)
            self.dma("sp", L[16:32, 64 * gi:64 * gi + 64], p["o_lam_im"][0].rearrange("(q gi) p -> gi q p", gi=2)[gi], [], bL)
        self.dma("sp", sm[32:48, 48:50], p["o_log_dt"][0].rearrange("(q gi) -> q gi", gi=2), [], [bsm[3]])
        for gi in range(2):
            self.cp(L[32:48, 64 * gi:64 * gi + 64], sm[32:48, 48 + gi:49 + gi].to_broadcast([16, 64]), [bsm[3]], bL)
        pb, bpb = self.bank()
        self.tr(pb[:, 0:128], L, bL, [bpb])
        self.cp(sm[:, 0:48], pb[:, 0:48], [bpb], [bsm[0], bsm[1], bsm[2]])
        Zc = self.xT[:].rearrange("p a b -> p (a b)").rearrange("p (z c) -> p z c", c=128)
        self.memset(self.xT[:], 0.0, self.b_xT)
        for part in range(2):
            Cd = p["o_c_re"][0] if part == 0 else p["o_c_im"][0]
            Cv = Cd.rearrange("(cc r) c p -> r c cc p", r=8)
            for j in range(4):
                for gi in range(2):
                    self.dma("act", Zc[32 * j + 16 * gi:32 * j + 16 * gi + 16, part * 16 + j:part * 16 + 16:4,
                                       64 * gi:64 * gi + 64],
                             Cv[2 * j + gi], [], self.b_xT[part * 4:part * 4 + 4])

    def pro_s5(self):
        self.deng = "pool"
        p = self.p
        c = self.s5c
        bc = self.b_s5c
        sm = self.small
        bsm = self.b_small
        lr = sm[:, 0:16]; li_ = sm[:, 16:32]; ldt = sm[:, 32:48]; tmp = sm[:, 48:64]
        R3 = [bsm[0], bsm[1], bsm[2]]
        dt = c[:, 4, :]; ang = c[:, 5, :]; r = c[:, 6, :]; t7 = c[:, 7, :]
        self.act(dt, ldt, AF.Exp, R3, [bc[4]])
        self.tt(ang, li_, dt, ALU.mult, R3 + [bc[4]], [bc[5]])
        self.tt(t7, lr, dt, ALU.mult, R3 + [bc[4]], [bc[7]])
        self.act(r, t7, AF.Exp, [bc[7]], [bc[6]])
        s1 = c[:, 3, :]; c1 = c[:, 2, :]
        for which in range(2):
            m = t7 if which == 0 else tmp
            bm = bc[7] if which == 0 else bsm[3]
            shift = 0.0 if which == 0 else PI / 2
            base = self.aux[:, 0, 0:16]
            self.ts(base, ang, shift, None, ALU.add, None, [bc[5]], [self.b_aux[0]])
            self.cp(m, base, [self.b_aux[0]], [bm])
            for thr in (1, 3, 5, 7):
                cm = self.aux[:, 0, 16:32]
                self.ts(cm, base, thr * PI, -2 * PI, ALU.is_gt, ALU.mult, [self.b_aux[0]], [self.b_aux[0]])
                self.tt(m, m, cm, ALU.add, [bm, self.b_aux[0]], [bm])
            self.act(s1 if which == 0 else c1, m, AF.Sin, [bm], [bc[3] if which == 0 else bc[2]])
        ar = c[:, 0, :]; ai = c[:, 1, :]
        cosT, sinT = self.cosT, self.sinT
        tW = [self.b_tab]
        self.memset(cosT[:, :, 0:1], 1.0, tW)
        self.memset(sinT[:, :, 0:1], 0.0, tW)
        self.cp(cosT[:, :, 1:2], c1.unsqueeze(2), [bc[2]], tW)
        self.cp(sinT[:, :, 1:2], s1.unsqueeze(2), [bc[3]], tW)
        self.tt(ar, r, c1, ALU.mult, [bc[6], bc[2]], [bc[0]])
        self.tt(ai, r, s1, ALU.mult, [bc[6], bc[3]], [bc[1]])
        gre = self.aux[:, 1, 0:16]; gim = self.aux[:, 1, 16:32]
        bg = self.b_aux[1]
        self.cp(gre, c1, [bc[2]], [bg])
        self.cp(gim, s1, [bc[3]], [bg])
        self.rot_tables(cosT, sinT, 16, gre, gim, tW)
        den = self.aux[:, 0, 32:48]; am1 = self.aux[:, 0, 48:64]
        qr = self.aux[:, 0, 64:80]; qi = self.aux[:, 0, 80:96]; u1 = self.aux[:, 0, 96:112]; u2 = self.aux[:, 0, 112:128]
        A0 = [self.b_aux[0]]
        Rall = R3 + [bc[0], bc[1]] + A0
        self.tt(den, lr, lr, ALU.mult, Rall, A0)
        self.tt(u1, li_, li_, ALU.mult, Rall, A0)
        self.tt(den, den, u1, ALU.add, A0, A0)
        self.recip(den, den, A0, A0)
        self.ts(am1, ar, -1.0, None, ALU.add, None, Rall, A0)
        self.tt(u1, am1, lr, ALU.mult, Rall, A0)
        self.tt(u2, ai, li_, ALU.mult, Rall, A0)
        self.tt(u1, u1, u2, ALU.add, A0, A0)
        self.tt(qr, u1, den, ALU.mult, A0, A0)
        self.tt(u1, ai, lr, ALU.mult, Rall, A0)
        self.tt(u2, am1, li_, ALU.mult, Rall, A0)
        self.tt(u1, u1, u2, ALU.subtract, A0, A0)
        self.tt(qi, u1, den, ALU.mult, A0, A0)
        apw = self.apw
        bap = self.b_apw
        self.memset(apw[:, 0, 0, :], 1.0, [bap])
        self.memset(apw[:, 1, 0, :], 0.0, [bap])
        self.memset(apw[:, 2, 0, :], 0.0, [bap])
        rp = t7
        self.cp(rp, r, [bc[6]], [bc[7]])
        for tau in range(1, 9):
            self.tt(apw[:, 0, tau, :], rp, cosT[:, :, tau], ALU.mult, [bc[7], self.b_tab], [bap])
            self.tt(apw[:, 1, tau, :], rp, sinT[:, :, tau], ALU.mult, [bc[7], self.b_tab], [bap])
            self.ts(apw[:, 2, tau, :], apw[:, 1, tau, :], -1.0, None, ALU.mult, None, [bap], [bap])
            if tau < 8:
                self.tt(rp, rp, r, ALU.mult, [bc[7], bc[6]], [bc[7]])
        c8T, s8T, r8T = self.c8T, self.s8T, self.r8T
        self.memset(c8T[:, :, 0:1], 1.0, tW)
        self.memset(s8T[:, :, 0:1], 0.0, tW)
        self.cp(c8T[:, :, 1:2], cosT[:, :, 8:9], tW, tW)
        self.cp(s8T[:, :, 1:2], sinT[:, :, 8:9], tW, tW)
        self.cp(gre, cosT[:, :, 8], tW, [bg])
        self.cp(gim, sinT[:, :, 8], tW, [bg])
        self.rot_tables(c8T, s8T, 64, gre, gim, tW)
        self.cp(r8T[:], rp.unsqueeze(2).to_broadcast([128, 16, 64]), [bc[7]], tW)
        self.memset(r8T[:, :, 0:1], 0.0, tW)
        def v3(ap):
            return ap.rearrange("p (q c) -> p q c", c=16)
        Bre = v3(self.s5t[:, 4, 0:256]); Bim = v3(self.s5t[:, 4, 256:512])
        Bbr = v3(self.s5t[:, 4, 512:768]); Bbi = v3(self.s5t[:, 4, 768:1024])
        t1 = v3(self.s5t[:, 5, 0:256]); t2 = v3(self.s5t[:, 5, 256:512])
        ABr = v3(self.s5t[:, 5, 512:768]); ABi = v3(self.s5t[:, 5, 768:1024])
        B4 = [self.b_s5t[4]]; B5 = [self.b_s5t[5]]
        self.dma("sp", Bre, p["o_b_re"][0].rearrange("(q gi) p c -> (gi p) q c", gi=2), [], B4)
        self.dma("sp", Bim, p["o_b_im"][0].rearrange("(q gi) p c -> (gi p) q c", gi=2), [], B4)
        qrb = qr.unsqueeze(2).to_broadcast([128, 16, 16]); qib = qi.unsqueeze(2).to_broadcast([128, 16, 16])
        self.tt(t1, Bre, qrb, ALU.mult, B4 + A0, B5)
        self.tt(t2, Bim, qib, ALU.mult, B4 + A0, B5)
        self.tt(Bbr, t1, t2, ALU.subtract, B5, B4)
        self.tt(t1, Bim, qrb, ALU.mult, B4 + A0, B5)
        self.tt(t2, Bre, qib, ALU.mult, B4 + A0, B5)
        self.tt(Bbi, t1, t2, ALU.add, B5, B4)
        self.cp(ar, apw[:, 0, 8, :], [bap], [bc[0]])
        self.cp(ai, apw[:, 1, 8, :], [bap], [bc[1]])
        self.memset(c[:, 2, :], 0.0, [bc[2]])
        self.memset(c[:, 3, :], 0.0, [bc[3]])
        self.deng = "dve"
        yield
        CTf = self.bigf[:].rearrange("p a b -> p (a b)").rearrange("p (t q c) -> p t q c", t=2, q=16)
        Zc = self.xT[:].rearrange("p a b -> p (a b)").rearrange("p (z c) -> p z c", c=128)
        for part in range(2):
            for q4 in range(4):
                pb, bpb = self.bank()
                for k in range(4):
                    q = q4 * 4 + k
                    z = part * 16 + q
                    self.tr(pb[:, k * 128:(k + 1) * 128], Zc[:, z, :], [self.b_xT[z // 4]], [bpb])
                self.act(CTf[:, part, q4 * 4:q4 * 4 + 4, :], pb.rearrange("p (k c) -> p k c", c=128), AF.Copy, [bpb],
                         [self.b_bf[part * 4 + q4]], scale=(1.0 if part == 0 else -1.0))
        CTb = self.s5t[:, 2:4, :].rearrange("p a b -> p (a b)").bitcast(BF16).rearrange("p (t q c) -> p t q c", t=2, q=16)
        bCTb = self.b_s5t[2:4]
        self.cp(CTb, CTf, self.b_bf, bCTb)
        self.cp(self.ident_bf[:], self.ident[:], [self.b_const], [self.b_const])
        stg_flat = self.bigbf[:].rearrange("p a b -> p (a b)")
        stg = [stg_flat[:, 0:4096], stg_flat[:, 4096:8192]]
        bstg = [self.b_bb[0:8], self.b_bb[8:16]]
        Zt = self.s5t[:, 0:2, :].rearrange("p a b -> p (a b)").bitcast(BF16).rearrange("p (t q c) -> p t q c", t=2, q=16)
        bZ = self.b_s5t[0:2]
        W1v = self.s5_d[:, 0:256 * 128].rearrange("p (cc s x) -> p cc s x", cc=4, s=8)
        si = 0
        ktf = self.ktap[:].rearrange("p a b -> p (a b)")
        for tau in range(8):
            s = 7 - tau
            apr_b = apw[:, 0, tau, :].unsqueeze(2).to_broadcast([128, 16, 16])
            api_b = apw[:, 1, tau, :].unsqueeze(2).to_broadcast([128, 16, 16])
            self.tt(t1, Bbr, apr_b, ALU.mult, B4 + [bap], B5)
            self.tt(t2, Bbi, api_b, ALU.mult, B4 + [bap], B5)
            self.tt(ABr, t1, t2, ALU.subtract, B5, B5)
            self.tt(t1, Bbr, api_b, ALU.mult, B4 + [bap], B5)
            self.tt(t2, Bbi, apr_b, ALU.mult, B4 + [bap], B5)
            self.tt(ABi, t1, t2, ALU.add, B5, B5)
            if tau == 0:
                bZj = [[Buf() for _ in range(4)] for _ in range(2)]
                self.memset(Zt, 0.0, bZ + bZj[0] + bZj[1])
            n = 0
            for part in range(2):
                AB = ABr if part == 0 else ABi
                for gi in range(2):
                    for j in range(4):
                        dst = Zt[64 * gi:64 * gi + 64, part, j::4, 32 * j + 16 * gi:32 * j + 16 * gi + 16]
                        src = AB[64 * gi:64 * gi + 64, j::4, :]
                        if n % 2 == 0:
                            self.cp(dst, src, B5, [bZj[part][j]])
                        else:
                            self.act(dst, src, AF.Copy, B5, [bZj[part][j]])
                        n += 1
            sl = stg[si % 2]; bsl = bstg[si % 2]; si += 1
            for cc in range(4):
                for hb in range(2):
                    pb, bpb = self.bank()
                    pbb = pb.bitcast(BF16)
                    for k in range(4):
                        jj = hb * 2 + k // 2
                        part = k % 2
                        q = cc * 4 + jj
                        self.tr(pbb[:, k * 128:(k + 1) * 128], Zt[:, part, q, :], [bZj[part][q % 4]], [bpb], ident=self.ident_bf)
                    o0 = (cc * 8 + hb * 4) * 128
                    if hb == 0:
                        self.act(sl[:, o0:o0 + 512], pbb[:, 0:512], AF.Copy, [bpb], [bsl[cc * 2 + hb]])
                    else:
                        self.cp(sl[:, o0:o0 + 512], pbb[:, 0:512], [bpb], [bsl[cc * 2 + hb]])
            self.dma("sp", W1v[:, :, s, :], sl.rearrange("p (cc x) -> p cc x", cc=4), bsl, [self.b_s5d])
            for cc in range(4):
                pb, bpb = self.bank()
                n = 0
                for jj in range(4):
                    q = cc * 4 + jj
                    for part in range(2):
                        self.mm(pb[:, 0:128], Zt[:, part, q, :], CTb[:, part, q, :], n == 0, n == 7,
                                [bZj[part][q % 4]] + bCTb, [bpb])
                        n += 1
                blk = cc * 8 + tau
                self.cp(ktf[:, blk * 128:(blk + 1) * 128], pb[:, 0:128], [bpb], [self.b_ktap])
        W2v = self.s5_d[:, 288 * 128:544 * 128].rearrange("p (cc l t x) -> p l t cc x", cc=4, l=8, t=2)
        Cc = self.s5t[:, 4, 0:512].rearrange("p (t q c) -> p t q c", t=2, q=16)
        for part in range(2):
            for gi in range(2):
                for j in range(4):
                    self.cp(Cc[64 * gi:64 * gi + 64, part, j::4, :],
                            CTf[64 * gi:64 * gi + 64, part, j::4, 32 * j + 16 * gi:32 * j + 16 * gi + 16],
                            self.b_bf, B4)
        w1_ = v3(self.s5t[:, 5, 0:256]); w2_ = v3(self.s5t[:, 5, 256:512])
        oc = self.s5t[:, 5, 512:1024].rearrange("p (t q c) -> p t q c", t=2, q=16)
        bW = [[[Buf() for _ in range(4)] for _ in range(2)] for _ in range(2)]
        for k in range(2):
            self.memset(stg[k], 0.0, bstg[k] + bW[k][0] + bW[k][1])
        for l in range(8):
            apr_b = apw[:, 0, l + 1, :].unsqueeze(2).to_broadcast([128, 16, 16])
            api_b = apw[:, 1, l + 1, :].unsqueeze(2).to_broadcast([128, 16, 16])
            sl = stg[si % 2]; bsl = bstg[si % 2]; bWs = bW[si % 2]; si += 1
            sl4 = sl.rearrange("p (t q c) -> p t q c", t=2, q=16)
            self.tt(w1_, Cc[:, 0], apr_b, ALU.mult, B4 + [bap], B5)
            self.tt(w2_, Cc[:, 1], api_b, ALU.mult, B4 + [bap], B5)
            self.tt(oc[:, 0], w1_, w2_, ALU.add, B5, B5)
            self.tt(w1_, Cc[:, 1], apr_b, ALU.mult, B4 + [bap], B5)
            self.tt(w2_, Cc[:, 0], api_b, ALU.mult, B4 + [bap], B5)
            self.tt(oc[:, 1], w1_, w2_, ALU.subtract, B5, B5)
            n = 0
            for part in range(2):
                for gi in range(2):
                    for j in range(4):
                        dst = sl4[64 * gi:64 * gi + 64, part, j::4, 32 * j + 16 * gi:32 * j + 16 * gi + 16]
                        src = oc[64 * gi:64 * gi + 64, part, j::4, :]
                        if n % 2 == 0:
                            self.cp(dst, src, B5, [bWs[part][j]])
                        else:
                            self.act(dst, src, AF.Copy, B5, [bWs[part][j]])
                        n += 1
            slv = sl.rearrange("p (t cc x) -> p t cc x", t=2, cc=4)
            for part in range(2):
                self.dma("sp", W2v[:, l, part], slv[:, part], bWs[part], [self.b_s5d])
        if self.dbg:
            self.dump("s5d", self.s5_d, [self.b_s5d], BF16)
            self.dump("apw", self.apw[:], [self.b_apw])
            self.dump("c8T", self.c8T[:], [self.b_tab])
            self.dump("s8T", self.s8T[:], [self.b_tab])
            self.dump("r8T", self.r8T[:], [self.b_tab])
            self.dump("ctf", self.bigf[:], self.b_bf)

    def scr_next(self):
        i = self.scr_rr % 8
        self.scr_rr = i + 1
        return self.scr[:].rearrange("p a b -> p (a b)")[:, i * 256:(i + 1) * 256], self.hb_scr[i]

    def scr_full(self):
        i0 = ((self.scr_rr + 1) // 2 * 2) % 8
        self.scr_rr = (i0 + 2) % 8
        return self.scr[:, i0 // 2, :], [self.hb_scr[i0], self.hb_scr[i0 + 1]]

    def load_x(self, ti):
        t0 = ti * T
        self.dma("sp", self.s5t[:, 0:4, :], self.x_d[t0:t0 + T, :].rearrange("(t p) d -> p t d", p=128), [],
                 self.hb_s5t[0:4])

    def tile(self, ti):
        t0 = ti * T
        HS = (0, 1)
        xtok = self.bigf[:].rearrange("p a b -> p (a b)").rearrange("p (t d) -> p t d", t=4)
        allbf = [b for h in HS for b in self.hb_bf[h]]
        xin = self.s5t[:, 0:4, :]
        if ti == 0:
            self.load_x(0)
        for h in HS:
            for dc in range(8):
                pb, bpb = self.bank()
                for k in range(2):
                    tb = 2 * h + k
                    self.tr(pb[:, k * 128:(k + 1) * 128], xin[:, tb, dc * 128:(dc + 1) * 128], [self.hb_s5t[tb]], [bpb])
                dst = self.xT[:, dc, h * 256:(h + 1) * 256]
                if dc % 2 == 0:
                    self.act(dst, pb[:, 0:256], AF.Copy, [bpb], [self.hb_xT[h][dc]])
                else:
                    self.cp(dst, pb[:, 0:256], [bpb], [self.hb_xT[h][dc]])
        prefetched = False
        for ph in self.phases:
            if ph == "l0mix":
                self.l0_mixer(ti)
            elif ph == "l1mix":
                self.l1_mixer(ti)
                if ti + 1 < self.ntiles:
                    self.load_x(ti + 1)
                    prefetched = True
            elif ph in ("ca0", "ca1"):
                self.cross_attn(int(ph[2]))
            elif ph in ("ffn0", "ffn1"):
                self.ffn(int(ph[3]))
        if ti + 1 < self.ntiles and not prefetched:
            self.load_x(ti + 1)
        otok = xtok
        for tb in range(4):
            h = tb // 2
            cs = slice(tb * 128, (tb + 1) * 128)
            banks = []
            sm = self.small
            ssq = sm[:, 32 + 2 * (tb % 2):34 + 2 * (tb % 2)]
            bss = self.hb_small[h][4 + tb % 2]
            for half in range(2):
                pb, bpb = self.bank()
                for j in range(4):
                    dc = half * 4 + j
                    self.tr(pb[:, j * 128:(j + 1) * 128], self.xT[:, dc, cs], [self.hb_xT[h][dc]], [bpb])
                banks.append((pb, bpb))
                if self.final_norm:
                    s_ = self.scr[:, (self.scr_rr // 2) % 4, :]
                    k0 = ((self.scr_rr // 2) % 4) * 2
                    self.scr_rr = (k0 + 2) % 8
                    bs_ = [self.hb_scr[k0], self.hb_scr[k0 + 1]]
                    self.act(s_, pb, AF.Square, [bpb], bs_ + [bss], accum=ssq[:, half:half + 1])
            if self.final_norm:
                rs = sm[:, 36 + (tb % 2):37 + (tb % 2)]
                brs = self.hb_small[h][6 + tb % 2]
                self.tt(rs, ssq[:, 0:1], ssq[:, 1:2], ALU.add, [bss], [brs])
                self.act(rs, rs, AF.Sqrt, [brs, self.b_const], [brs], bias=self.epsb[:], scale=1.0 / D)
                self.recip(rs, rs, [brs], [brs])
                for half in range(2):
                    pb, bpb = banks[half]
                    self.stt(otok[:, tb, half * 512:(half + 1) * 512], pb, rs, self.fg[:, half * 512:(half + 1) * 512],
                             ALU.mult, ALU.mult, [bpb, brs, self.b_const],
                             [self.hb_bf[0][2 * tb + half], self.hb_bf[1][2 * tb + half]])
            else:
                for half in range(2):
                    pb, bpb = banks[half]
                    self.cp(otok[:, tb, half * 512:(half + 1) * 512], pb, [bpb],
                            [self.hb_bf[0][2 * tb + half], self.hb_bf[1][2 * tb + half]])
        self.dma("sp", self.out_d[t0:t0 + T, :].rearrange("(t p) d -> p t d", p=128), otok, allbf, [self.b_out])

    def rmsnorm(self, h, gcol):
        cs = slice(h * 256, (h + 1) * 256)
        self.rms_generic(lambda dc: self.xT[:, dc, cs], self.hb_xT[h], 256, gcol,
                         lambda dc: self.xn[:, dc, cs], self.hb_xn[h],
                         rstd=self.rstd[:, cs], brstd=self.hb_rstd[h], sqbufs=self.hb_sq)

    def resid_add(self, h, dc, pb, bpb):
        x = self.xT[:, dc, h * 256:(h + 1) * 256]
        self.tt(x, x, pb[:, 0:256], ALU.add, [self.hb_xT[h][dc], bpb], [self.hb_xT[h][dc]])

    def l0_mixer(self, ti):
        p = self.p
        bb, bf = self.bigbf, self.bigf
        HS = (0, 1)
        CS = [slice(0, 256), slice(256, 512)]
        for h in HS:
            self.rmsnorm(h, G_E)
        w0, bw0 = self.load_w(p["e_w_in"][0], 0, 1024)
        for h in HS:
            bbb, bbf, bxn = self.hb_bb[h], self.hb_bf[h], self.hb_xn[h]
            for fc in range(4):
                pb, bpb = self.bank()
                for dc in range(8):
                    self.mm(pb[:, 0:256], w0[:, dc, fc * 128:(fc + 1) * 128], self.xn[:, dc, CS[h]], dc == 0, dc == 7,
                            [bw0, bxn[dc]], [bpb])
                self.act(bf[:, fc, CS[h]], pb[:, 0:256], AF.Gelu, [bpb], [bbf[fc]])
            for k in range(2):
                tb = 2 * h + k
                pb, bpb = self.bank()
                for dc in range(8):
                    self.mm(pb, self.xn[:, dc, tb * 128:(tb + 1) * 128], w0[:, dc, 512:1024], dc == 0, dc == 7,
                            [bw0, bxn[dc]], [bpb])
                i0 = ((self.scr_rr + 1) // 2 * 2) % 8
                self.scr_rr = (i0 + 2) % 8
                s_ = self.scr[:, i0 // 2, :]
                bs_ = [self.hb_scr[i0], self.hb_scr[i0 + 1]]
                self.act(s_, pb, AF.Gelu, [bpb], bs_)
                sm = self.small
                st6 = sm[:, 8 * k:8 * k + 6]; mv = sm[:, 16 + 4 * k:16 + 4 * k + 2]
                rv = sm[:, 16 + 4 * k + 2:16 + 4 * k + 3]
                bsm = self.hb_small[0][k]
                self.S.op("dve", lambda hh, st6=st6, s_=s_: hh.bn_stats(st6, s_), reads=bs_, writes=[bsm])
                self.S.op("dve", lambda hh, st6=st6, mv=mv: hh.bn_aggr(mv, st6), reads=[bsm], writes=[bsm])
                self.act(rv, mv[:, 1:2], AF.Sqrt, [bsm, self.b_const], [bsm], bias=self.epsb[:], scale=1.0)
                self.recip(rv, rv, [bsm], [bsm])
                self.ts(bb[:, 8 + tb, :], s_, mv[:, 0:1], rv, ALU.subtract, ALU.mult, bs_ + [bsm],
                        [self.hb_bb[0][8 + tb], self.hb_bb[1][8 + tb]])
        w1, bw1 = self.load_w(p["e_w_in"][0], 1024, 1024)
        for h in HS:
            bxn = self.hb_xn[h]
            for fc in range(4):
                pa, bpa = self.bank()
                for dc in range(8):
                    self.mm(pa[:, 0:256], w1[:, dc, fc * 128:(fc + 1) * 128], self.xn[:, dc, CS[h]], dc == 0, dc == 7,
                            [bw1, bxn[dc]], [bpa])
                pg, bpg = self.bank()
                for dc in range(8):
                    self.mm(pg[:, 0:256], w1[:, dc, 512 + fc * 128:512 + (fc + 1) * 128], self.xn[:, dc, CS[h]],
                            dc == 0, dc == 7, [bw1, bxn[dc]], [bpg])
                s_, bs_ = self.scr_next()
                self.act(s_, pg[:, 0:256], AF.Sigmoid, [bpg], [bs_])
                self.tt(self.hbuf[:, fc, 30 + h * 256:30 + (h + 1) * 256], pa[:, 0:256], s_, ALU.mult, [bpa, bs_],
                        [self.hb_h[fc][1 + h]])
        for h in HS:
            bbb, bbf = self.hb_bb[h], self.hb_bf[h]
            for g in range(4):
                pb, bpb = self.bank()
                for k in range(2):
                    tb = 2 * h + k
                    o = pb[:, k * 128:(k + 1) * 128]
                    self.mm(o, bb[:, 8 + tb, g * 128:(g + 1) * 128], self.gw[:, g, :], True, True,
                            [self.hb_bb[0][8 + tb], self.hb_bb[1][8 + tb], self.b_const], [bpb])
                s_, bs_ = self.scr_next()
                self.tt(s_.rearrange("p (k i) -> p k i", k=2), pb[:, 0:256].rearrange("p (k i) -> p k i", k=2),
                        self.bsbc[:, g, :].unsqueeze(1).to_broadcast([128, 2, 128]), ALU.add, [bpb, self.b_const], [bs_])
                self.tt(bb[:, g, CS[h]], bf[:, g, CS[h]], s_, ALU.mult, [bbf[g], bs_], [bbb[g]])
        for half in range(2):
            dg, bdg = self.load_slab(self.diag_d[half].rearrange("p (a b) -> p a b", b=128), 62, 128, q="sp",
                                     R=[self.b_diag])
            if half == 0:
                for f2 in range(2):
                    fc = half * 2 + f2
                    base = f2 * 31
                    pb, bpb = self.bank()
                    for k in range(31):
                        self.mm(pb, dg[:, base + k, :], self.hbuf[:, fc, k:k + 512], k == 0, k == 30,
                                [bdg] + self.hb_h[fc], [bpb])
                    self.act(bf[:, 4 + fc, :], pb, AF.Identity, [bpb, self.b_const],
                             [self.hb_bf[0][4 + fc], self.hb_bf[1][4 + fc]],
                             bias=self.gv[:, G_CB + fc:G_CB + fc + 1], scale=1.0)
                    self.cp(self.hbuf[:, fc, 0:30], self.hbuf[:, fc, 512:542], [self.hb_h[fc][2]], [self.hb_h[fc][0]])
            else:
                for h in HS:
                    for f2 in range(2):
                        fc = half * 2 + f2
                        base = f2 * 31
                        pb, bpb = self.bank()
                        Rh = [bdg, self.hb_h[fc][0], self.hb_h[fc][1]] if h == 0 else [bdg, self.hb_h[fc][1], self.hb_h[fc][2]]
                        for k in range(31):
                            self.mm(pb[:, 0:256], dg[:, base + k, :], self.hbuf[:, fc, h * 256 + k:h * 256 + k + 256],
                                    k == 0, k == 30, Rh, [bpb])
                        self.act(bf[:, 4 + fc, CS[h]], pb[:, 0:256], AF.Identity, [bpb, self.b_const], [self.hb_bf[h][4 + fc]],
                                 bias=self.gv[:, G_CB + fc:G_CB + fc + 1], scale=1.0)
                        if h == 1:
                            self.cp(self.hbuf[:, fc, 0:30], self.hbuf[:, fc, 512:542], [self.hb_h[fc][2]], [self.hb_h[fc][0]])
                    self.l0_ln(h)
        wo, bwo = self.load_w(p["e_w_out"][0], 0, 1024)
        for h in HS:
            bbb = self.hb_bb[h]
            for dc in range(8):
                pb, bpb = self.bank()
                for kc in range(8):
                    self.mm(pb[:, 0:256], wo[:, kc, dc * 128:(dc + 1) * 128], bb[:, kc, CS[h]], kc == 0, kc == 7,
                            [bwo, bbb[kc]], [bpb])
                self.resid_add(h, dc, pb, bpb)

    def l0_ln(self, h):
        bb, bf = self.bigbf, self.bigf
        CS = [slice(0, 256), slice(256, 512)]
        bbb, bbf = self.hb_bb[h], self.hb_bf[h]
        pm, bpm = self.bank()
        for fc in range(4):
            s_, bs_ = self.scr_next()
            sb_ = s_.bitcast(BF16)[:, 0:256]
            self.act(sb_, bf[:, 4 + fc, CS[h]], AF.Copy, [bbf[4 + fc]], [bs_])
            self.mm(pm[:, 0:256], self.ones_bf[:], sb_, fc == 0, fc == 3, [bs_, self.b_const], [bpm])
        pq, bpq = self.bank()
        for fc in range(4):
            s_, bs_ = self.scr_next()
            sb_ = s_.bitcast(BF16)[:, 0:256]
            self.act(sb_, bf[:, 4 + fc, CS[h]], AF.Square, [bbf[4 + fc]], [bs_])
            self.mm(pq[:, 0:256], self.ones_bf[:], sb_, fc == 0, fc == 3, [bs_, self.b_const], [bpq])
        mean = self.aux[:, 0, CS[h]]; rs = self.aux[:, 1, CS[h]]
        bmean, brs = self.hb_aux[h]
        self.ts(mean, pm[:, 0:256], 1.0 / 512, None, ALU.mult, None, [bpm], [bmean])
        self.tt(rs, mean, mean, ALU.mult, [bmean], [brs])
        self.stt(rs, pq[:, 0:256], 1.0 / 512, rs, ALU.mult, ALU.subtract, [bpq, brs], [brs])
        self.act(rs, rs, AF.Sqrt, [brs, self.b_const], [brs], bias=self.epsb[:], scale=1.0)
        self.recip(rs, rs, [brs], [brs])
        for fc in range(4):
            s_, bs_ = self.scr_next()
            self.tt(s_, bf[:, 4 + fc, CS[h]], mean, ALU.subtract, [bbf[4 + fc], bmean], [bs_])
            self.tt(s_, s_, rs, ALU.mult, [bs_, brs], [bs_])
            self.act(bb[:, 4 + fc, CS[h]], s_, AF.Silu, [bs_, self.b_const], [bbb[4 + fc]],
                     bias=self.gv[:, G_LB + fc:G_LB + fc + 1], scale=self.gv[:, G_LG + fc:G_LG + fc + 1])

    def cross_attn(self, li):
        p = self.p
        bb = self.bigbf
        HS = (0, 1)
        CS = [slice(0, 256), slice(256, 512)]
        for h in HS:
            self.rmsnorm(h, G_CA + 8 * li)
        wq, bwq = self.load_w(p["ca_wq"][li], 0, 1024)
        for h in HS:
            bbb, bxn = self.hb_bb[h], self.hb_xn[h]
            for oc in range(8):
                pb, bpb = self.bank()
                for dc in range(8):
                    self.mm(pb[:, 0:256], wq[:, dc, oc * 128:(oc + 1) * 128], self.xn[:, dc, CS[h]], dc == 0, dc == 7,
                            [bwq, bxn[dc]], [bpb])
                self.act(bb[:, oc, CS[h]], pb[:, 0:256], AF.Copy, [bpb], [bbb[oc]], scale=1.0 / 16)
        for h in HS:
            bbb = self.hb_bb[h]
            for hd in range(4):
                pc = 16 + 2 * (hd % 2)
                for mc in range(2):
                    pb, bpb = self.bank()
                    for j in range(2):
                        self.mm(pb[:, 0:256], self.KT[:, li, 2 * hd + j, mc * 128:(mc + 1) * 128], bb[:, 2 * hd + j, CS[h]],
                                j == 0, j == 1, [self.b_KT, bbb[2 * hd + j]], [bpb])
                    self.act(bb[:, pc + mc, CS[h]], pb[:, 0:256], AF.Exp, [bpb], [bbb[pc + mc]])
                pd, bpd = self.bank()
                for mc in range(2):
                    self.mm(pd[:, 0:256], self.ones_bf[:], bb[:, pc + mc, CS[h]], mc == 0, mc == 1,
                            [bbb[pc + mc], self.b_const], [bpd])
                rden = self.aux[:, hd % 2, CS[h]]; brd = self.hb_aux[h][hd % 2]
                self.recip(rden, pd[:, 0:256], [bpd], [brd])
                for j in range(2):
                    po, bpo = self.bank()
                    for mc in range(2):
                        self.mm(po[:, 0:256], self.V[:, li, mc, (2 * hd + j) * 128:(2 * hd + j + 1) * 128], bb[:, pc + mc, CS[h]],
                                mc == 0, mc == 1, [self.b_V, bbb[pc + mc]], [bpo])
                    self.tt(bb[:, 8 + 2 * hd + j, CS[h]], po[:, 0:256], rden, ALU.mult, [bpo, brd], [bbb[8 + 2 * hd + j]])
        wo, bwo = self.load_w(p["ca_wo"][li], 0, 1024)
        for h in HS:
            bbb = self.hb_bb[h]
            for dc in range(8):
                pb, bpb = self.bank()
                for kc in range(8):
                    self.mm(pb[:, 0:256], wo[:, kc, dc * 128:(dc + 1) * 128], bb[:, 8 + kc, CS[h]], kc == 0, kc == 7,
                            [bwo, bbb[8 + kc]], [bpb])
                self.resid_add(h, dc, pb, bpb)

    def ffn(self, li):
        p = self.p
        bb = self.bigbf
        HS = (0, 1)
        CS = [slice(0, 256), slice(256, 512)]
        for h in HS:
            self.rmsnorm(h, G_FFN + 8 * li)
        for s in range(6):
            nh = 4 if s < 5 else 2
            i = self.ring_rr
            self.ring_rr = (i + 1) % NSLOT
            slot = self.ring[i][:, 0:8192].rearrange("p (a b) -> p a b", b=1024)
            bsl = self.b_ring[i]
            c0 = s * 512
            for k, wname in enumerate(("ffn_w_gate", "ffn_w_up")):
                src = p[wname][li][:, c0:c0 + nh * 128].rearrange("(c p) f -> p c f", p=128)
                self.dma("pool", slot[:, :, k * 512:k * 512 + nh * 128], src, [], [bsl])
            if True:
                for h in HS:
                    bbb, bxn = self.hb_bb[h], self.hb_xn[h]
                    for j in range(nh):
                        hc = s * 4 + j
                        pg, bpg = self.bank()
                        for dc in range(8):
                            self.mm(pg[:, 0:256], slot[:, dc, j * 128:(j + 1) * 128], self.xn[:, dc, CS[h]], dc == 0, dc == 7,
                                    [bsl, bxn[dc]], [bpg])
                        pu, bpu = self.bank()
                        for dc in range(8):
                            self.mm(pu[:, 0:256], slot[:, dc, 512 + j * 128:512 + (j + 1) * 128], self.xn[:, dc, CS[h]],
                                    dc == 0, dc == 7, [bsl, bxn[dc]], [bpu])
                        s_, bs_ = self.scr_next()
                        self.act(s_, pg[:, 0:256], AF.Silu, [bpg], [bs_])
                        self.tt(bb[:, hc, CS[h]], s_, pu[:, 0:256], ALU.mult, [bs_, bpu], [bbb[hc]])
            else:
                for j in range(nh):
                    hc = s * 4 + j
                    pg, bpg = self.bank()
                    for dc in range(8):
                        self.mm(pg, slot[:, dc, j * 128:(j + 1) * 128], self.xn[:, dc, :], dc == 0, dc == 7,
                                [bsl, self.hb_xn[0][dc], self.hb_xn[1][dc]], [bpg])
                    pu, bpu = self.bank()
                    for dc in range(8):
                        self.mm(pu, slot[:, dc, 512 + j * 128:512 + (j + 1) * 128], self.xn[:, dc, :], dc == 0, dc == 7,
                                [bsl, self.hb_xn[0][dc], self.hb_xn[1][dc]], [bpu])
                    s_, bs_ = self.scr_full()
                    self.act(s_, pg, AF.Silu, [bpg], bs_)
                    self.tt(bb[:, hc, :], s_, pu, ALU.mult, bs_ + [bpu], [self.hb_bb[0][hc], self.hb_bb[1][hc]])
        for s in range(3):
            ndc = 3 if s < 2 else 2
            d0 = s * 3
            wd, bwd = self.load_w(p["ffn_w_down"][li], d0 * 128, ndc * 128)
            if True:
                for h in HS:
                    bbb = self.hb_bb[h]
                    for j in range(ndc):
                        dc = d0 + j
                        pb, bpb = self.bank()
                        for hc in range(HC):
                            self.mm(pb[:, 0:256], wd[:, hc, j * 128:(j + 1) * 128], bb[:, hc, CS[h]], hc == 0, hc == HC - 1,
                                    [bwd, bbb[hc]], [bpb])
                        self.resid_add(h, dc, pb, bpb)
            else:
                for j in range(ndc):
                    dc = d0 + j
                    pb, bpb = self.bank()
                    for hc in range(HC):
                        self.mm(pb, wd[:, hc, j * 128:(j + 1) * 128], bb[:, hc, :], hc == 0, hc == HC - 1,
                                [bwd, self.hb_bb[0][hc], self.hb_bb[1][hc]], [bpb])
                    bx = [self.hb_xT[0][dc], self.hb_xT[1][dc]]
                    self.tt(self.xT[:, dc, :], self.xT[:, dc, :], pb, ALU.add, bx + [bpb], bx)

    def l1_mixer(self, ti):
        p = self.p
        bb, bf = self.bigbf, self.bigf
        HS = (0, 1)
        CS = [slice(0, 256), slice(256, 512)]
        c, bc = self.s5c, self.b_s5c
        both = lambda lst, k: [lst[0][k], lst[1][k]]
        for h in HS:
            self.rmsnorm(h, G_O)
        w_, bw = self.load_w(p["o_w_in"][0], 0, 512)
        for h in HS:
            bxn = self.hb_xn[h]
            for cc in range(4):
                pb, bpb = self.bank()
                for dc in range(8):
                    self.mm(pb[:, 0:256], w_[:, dc, cc * 128:(cc + 1) * 128], self.xn[:, dc, CS[h]], dc == 0, dc == 7,
                            [bw, bxn[dc]], [bpb])
                self.act(bf[:, cc, CS[h]], pb[:, 0:256], AF.Copy, [bpb], [self.hb_bf[h][cc]])
                self.cp(bb[:, cc, CS[h]], pb[:, 0:256], [bpb], [self.hb_bb[h][cc]])
        self.ps_rr = 0
        py = [(self.psum[:, i, :], self.b_ps[i]) for i in range(4)]
        pre = self.psum[:, 4:6, :].rearrange("p a b -> p (a b)"); bpre = [self.b_ps[4], self.b_ps[5]]
        pim = self.psum[:, 6:8, :].rearrange("p a b -> p (a b)"); bpim = [self.b_ps[6], self.b_ps[7]]
        u3 = [bb[:, cc, :].rearrange("p (n s) -> p n s", s=8) for cc in range(4)]
        for cc in range(4):
            sw, bsw = self.load_slab(self.s5_d[:, cc * 8192:(cc + 1) * 8192].rearrange("p (a b) -> p a b", b=128),
                                     64, 128, q="sp", R=[self.b_s5d])
            for jj in range(4):
                q = cc * 4 + jj
                for part in range(2):
                    o = (pre if part == 0 else pim)[:, q * 64:(q + 1) * 64]
                    bo = (bpre if part == 0 else bpim)[q // 8]
                    for s in range(8):
                        self.mm(o, sw[:, (s * 4 + jj) * 2 + part, :], u3[cc][:, :, s], s == 0, s == 7,
                                [bsw] + both(self.hb_bb, cc), [bo])
        cosf = self.c8T[:].rearrange("p q l -> p (q l)")
        sinf = self.s8T[:].rearrange("p q l -> p (q l)")
        rTf = self.r8T[:].rearrange("p q l -> p (q l)")
        tabR = [self.b_tab]
        t = [self.s5t[:, i, :] for i in range(6)]
        bt = self.hb_s5t
        xs_re = bb[:, 8:10, :].rearrange("p a b -> p (a b)"); bxr = both(self.hb_bb, 8) + both(self.hb_bb, 9)
        xs_im = bb[:, 10:12, :].rearrange("p a b -> p (a b)"); bxi = both(self.hb_bb, 10) + both(self.hb_bb, 11)
        ar = c[:, 0, :]; ai = c[:, 1, :]; xpr = c[:, 2, :]; xpi = c[:, 3, :]
        u1 = c[:, 4, :]; u2 = c[:, 5, :]
        self.tt(t[0], pre, cosf, ALU.mult, bpre + tabR, [bt[0]])
        self.tt(t[1], pim, sinf, ALU.mult, bpim + tabR, [bt[1]])
        self.tt(t[0], t[0], t[1], ALU.add, [bt[0], bt[1]], [bt[0]])
        self.tt(t[2], pim, cosf, ALU.mult, bpim + tabR, [bt[2]])
        self.tt(t[3], pre, sinf, ALU.mult, bpre + tabR, [bt[3]])
        self.tt(t[2], t[2], t[3], ALU.subtract, [bt[2], bt[3]], [bt[2]])
        if ti > 0:
            w_re0 = t[0].rearrange("p (q l) -> p q l", l=64)[:, :, 0:1]
            w_im0 = t[2].rearrange("p (q l) -> p q l", l=64)[:, :, 0:1]
            self.tt(u1, ar, xpr, ALU.mult, [bc[0], bc[2]], [bc[4]])
            self.tt(u2, ai, xpi, ALU.mult, [bc[1], bc[3]], [bc[5]])
            self.tt(u1, u1, u2, ALU.subtract, [bc[4], bc[5]], [bc[4]])
            self.tt(w_re0, w_re0, u1.unsqueeze(2), ALU.add, [bt[0], bc[4]], [bt[0]])
            self.tt(u1, ar, xpi, ALU.mult, [bc[0], bc[3]], [bc[4]])
            self.tt(u2, ai, xpr, ALU.mult, [bc[1], bc[2]], [bc[5]])
            self.tt(u1, u1, u2, ALU.add, [bc[4], bc[5]], [bc[4]])
            self.tt(w_im0, w_im0, u1.unsqueeze(2), ALU.add, [bt[2], bc[4]], [bt[2]])
        self.S.op("dve", lambda hh, o=t[1], a=rTf, b=t[0]: hh.tensor_tensor_scan(o, a, b, 0.0, ALU.mult, ALU.add),
                  reads=[bt[0]] + tabR, writes=[bt[1]])
        self.S.op("dve", lambda hh, o=t[3], a=rTf, b=t[2]: hh.tensor_tensor_scan(o, a, b, 0.0, ALU.mult, ALU.add),
                  reads=[bt[2]] + tabR, writes=[bt[3]])
        self.tt(t[0], t[1], cosf, ALU.mult, [bt[1]] + tabR, [bt[0]])
        self.tt(t[2], t[3], sinf, ALU.mult, [bt[3]] + tabR, [bt[2]])
        self.tt(t[4], t[0], t[2], ALU.subtract, [bt[0], bt[2]], [bt[4]])
        self.tt(t[0], t[1], sinf, ALU.mult, [bt[1]] + tabR, [bt[0]])
        self.tt(t[2], t[3], cosf, ALU.mult, [bt[3]] + tabR, [bt[2]])
        self.tt(t[5], t[0], t[2], ALU.add, [bt[0], bt[2]], [bt[5]])
        x_re3 = t[4].rearrange("p (q l) -> p q l", l=64)
        x_im3 = t[5].rearrange("p (q l) -> p q l", l=64)
        xs_re3 = xs_re.rearrange("p (q l) -> p q l", l=64)
        xs_im3 = xs_im.rearrange("p (q l) -> p q l", l=64)
        self.act(xs_re3[:, :, 0:1], xpr.unsqueeze(2), AF.Copy, [bc[2]], bxr)
        self.act(xs_im3[:, :, 0:1], xpi.unsqueeze(2), AF.Copy, [bc[3]], bxi)
        self.act(xs_re3[:, :, 1:64], x_re3[:, :, 0:63], AF.Copy, [bt[4]], bxr)
        self.act(xs_im3[:, :, 1:64], x_im3[:, :, 0:63], AF.Copy, [bt[5]], bxi)
        self.cp(xpr.unsqueeze(2), x_re3[:, :, 63:64], [bt[4]], [bc[2]])
        self.cp(xpi.unsqueeze(2), x_im3[:, :, 63:64], [bt[5]], [bc[3]])
        kt, bkt = self.ktap, self.b_ktap
        for cc in range(4):
            o3 = py[cc][0].rearrange("p (n l) -> p n l", l=8)
            bo = py[cc][1]
            Ru = [bkt] + both(self.hb_bb, cc)
            self.mm(py[cc][0], kt[:, cc * 8, :], bb[:, cc, :], True, False, Ru, [bo])
            for l in range(1, 8):
                for tau in range(1, l + 1):
                    self.mm(o3[:, :, l], kt[:, cc * 8 + tau, :], u3[cc][:, :, l - tau], False, False, Ru, [bo])
        for cc in range(4):
            sw, bsw = self.load_slab(self.s5_d[:, (288 + cc * 64) * 128:(288 + (cc + 1) * 64) * 128]
                                     .rearrange("p (a b) -> p a b", b=128), 64, 128, q="sp", R=[self.b_s5d])
            o3 = py[cc][0].rearrange("p (n l) -> p n l", l=8)
            bo = py[cc][1]
            for l in range(8):
                for jj in range(4):
                    q = cc * 4 + jj
                    for part in range(2):
                        xs = xs_re if part == 0 else xs_im
                        bx = bxr if part == 0 else bxi
                        self.mm(o3[:, :, l], sw[:, (l * 2 + part) * 4 + jj, :], xs[:, q * 64:(q + 1) * 64], False,
                                (l == 7 and jj == 3 and part == 1), [bsw] + bx, [bo])
        for h in HS:
            for cc in range(4):
                s_, bs_ = self.scr_next()
                self.stt(s_, bf[:, cc, CS[h]], self.gv[:, G_OD + cc:G_OD + cc + 1], py[cc][0][:, CS[h]], ALU.mult, ALU.add,
                         [self.hb_bf[h][cc], py[cc][1], self.b_const], [bs_])
                self.act(bb[:, 4 + cc, CS[h]], s_, AF.Gelu, [bs_], [self.hb_bb[h][4 + cc]])
        wo, bwo = self.load_slab(p["o_w_out"][0].rearrange("(c p) f -> p c f", p=128), 4, 2048)
        for h in HS:
            bbb = self.hb_bb[h]
            for dc in range(8):
                pa, bpa = self.bank()
                for kc in range(4):
                    self.mm(pa[:, 0:256], wo[:, kc, dc * 128:(dc + 1) * 128], bb[:, 4 + kc, CS[h]], kc == 0, kc == 3,
                            [bwo, bbb[4 + kc]], [bpa])
                pg, bpg = self.bank()
                for kc in range(4):
                    self.mm(pg[:, 0:256], wo[:, kc, 1024 + dc * 128:1024 + (dc + 1) * 128], bb[:, 4 + kc, CS[h]],
                            kc == 0, kc == 3, [bwo, bbb[4 + kc]], [bpg])
                s_, bs_ = self.scr_next()
                self.act(s_, pg[:, 0:256], AF.Sigmoid, [bpg], [bs_])
                self.tt(s_, pa[:, 0:256], s_, ALU.mult, [bpa, bs_], [bs_])
                x = self.xT[:, dc, CS[h]]
                self.tt(x, x, s_, ALU.add, [self.hb_xT[h][dc], bs_], [self.hb_xT[h][dc]])


ALL_PHASES = ("l0mix", "ca0", "ffn0", "l1mix", "ca1", "ffn1")


def build_nc(phases=ALL_PHASES, ntiles=SEQ // T, final_norm=True, dbg=False):
    nc = bass.Bass("TRN2", target_bir_lowering=False)
    kb = KB(nc, phases, ntiles, final_norm)
    kb.dbg = dbg
    kb.build()
    return nc


def make_in_maps(inputs):
    ident = np.eye(128, dtype=np.float32)
    maps = []
    for b in range(8):
        m = {"x": np.ascontiguousarray(inputs["x"][b]), "mem": np.ascontiguousarray(inputs["mem"][b]),
             "c_ident": ident}
        for k in PARAM_SHAPES:
            m[k] = np.ascontiguousarray(np.asarray(inputs[k], dtype=np.float32))
        maps.append(m)
    return maps


def kernel(**inputs):
    inputs = {k: np.asarray(v) for k, v in inputs.items()}
    nc = build_nc()
    res = run_bass_kernel_spmd(nc, make_in_maps(inputs), core_ids=list(range(8)))
    return np.stack([res.results[b]["out"] for b in range(8)], axis=0).astype(np.float32)
```

```python
import contextlib
import numpy as np
import concourse.bass as bass
import concourse.mybir as mybir
from concourse.bass_utils import run_bass_kernel_spmd

F32 = mybir.dt.float32
BF16 = mybir.dt.bfloat16
AF = mybir.ActivationFunctionType
ALU = mybir.AluOpType

D = 1024
SEQ = 4096
T = 512
DC = 8
H = 2816
HC = 22
MEM = 256
LS = 64
SLOT = 8448
NSLOT = 3
EPS = 1e-6
PI = float(np.pi)

ENGS = ("pe", "act", "dve", "pool", "sp")
NDMA_SEM = 12

G_E, G_O, G_CA, G_FFN, G_MEM, G_CB, G_LG, G_LB, G_OD, G_N = 0, 8, 16, 32, 48, 64, 68, 72, 76, 80


class Buf:
    __slots__ = ("name", "w", "r", "excl")

    def __init__(self, name="", excl=False):
        self.name = name
        self.w = None
        self.r = {}
        self.excl = excl


class Sched:
    def __init__(self, nc):
        self.nc = nc
        self.streams = {e: [] for e in ENGS}
        self.cnt = {e: 0 for e in ENGS}
        self.seen = {e: {} for e in ENGS}
        self.dma_cnt = {}
        self.dma_rr = {e: 0 for e in ENGS}

    def _need(self, eng, tok, out):
        if tok is None:
            return
        key, val, src = tok
        if self.seen[eng].get(key, 0) >= val:
            return
        out[key] = max(out.get(key, 0), val)

    def _deps(self, eng, reads, writes, same_eng_raw=True):
        need = {}
        for b in reads:
            if b.w is not None:
                if b.w[2] == eng and not same_eng_raw:
                    continue
                self._need(eng, b.w, need)
            if b.excl:
                for key, (val, src) in b.r.items():
                    if src != eng:
                        self._need(eng, (key, val, src), need)
        for b in writes:
            if b.w is not None and (b.w[2] != eng or eng != "pe"):
                self._need(eng, b.w, need)
            for key, (val, src) in b.r.items():
                if src != eng or eng != "pe":
                    self._need(eng, (key, val, src), need)
        for key, val in need.items():
            self.streams[eng].append(("wait", key, val))
            self.seen[eng][key] = val

    def _commit(self, tok, reads, writes):
        key, val, src = tok
        for b in reads:
            b.r[key] = (val, src)
        for b in writes:
            b.w = tok
            b.r = {}

    def op(self, eng, fn, reads=(), writes=()):
        self._deps(eng, reads, writes, same_eng_raw=(eng != "pe"))
        self.cnt[eng] += 1
        tok = (("e", eng), self.cnt[eng], eng)
        self.streams[eng].append(("op", fn))
        self._commit(tok, reads, writes)

    def dma(self, q, fn, reads=(), writes=()):
        i = self.dma_rr[q]
        self.dma_rr[q] = (i + 1) % NDMA_SEM
        key = ("d", q, i)
        prev = self.dma_cnt.get(key, 0)
        self._deps(q, reads, writes)
        if prev > 0 and self.seen[q].get(key, 0) < prev:
            self.streams[q].append(("wait", key, prev))
            self.seen[q][key] = prev
        val = prev + 16
        self.dma_cnt[key] = val
        self.streams[q].append(("dma", fn, key))
        self._commit((key, val, "dma"), reads, writes)

    def barrier(self):
        keys = {("e", e): self.cnt[e] for e in ENGS if self.cnt[e] > 0}
        keys.update(self.dma_cnt)
        for eng in ENGS:
            for key, val in keys.items():
                if key == ("e", "pe") and eng == "pe":
                    continue
                if self.seen[eng].get(key, 0) < val:
                    self.streams[eng].append(("wait", key, val))
                    self.seen[eng][key] = val

    def final_wait(self, eng, bufs):
        need = {}
        for b in bufs:
            self._need(eng, b.w, need)
        for key, val in need.items():
            self.streams[eng].append(("wait", key, val))
            self.seen[eng][key] = val

    def emit(self):
        nc = self.nc
        with contextlib.ExitStack() as st:
            sems = {}
            for e in ENGS:
                sems[("e", e)] = st.enter_context(nc.semaphore("s_" + e))
            for key in self.dma_cnt:
                sems[key] = st.enter_context(nc.semaphore("d_%s%d" % (key[1], key[2])))
            block = st.enter_context(nc.Block())

            def run(stream, ename):
                def body(h):
                    for item in stream:
                        if item[0] == "wait":
                            h.wait_ge(sems[item[1]], item[2])
                        elif item[0] == "op":
                            item[1](h).then_inc(sems[("e", ename)], 1)
                        else:
                            item[1](h).then_inc(sems[item[2]], 16)
                return body

            block.tensor(run(self.streams["pe"], "pe"))
            block.scalar(run(self.streams["act"], "act"))
            block.vector(run(self.streams["dve"], "dve"))
            block.gpsimd(run(self.streams["pool"], "pool"))
            block.sync(run(self.streams["sp"], "sp"))


PARAM_SHAPES = {
    "e_norm": [1, 1024], "e_w_in": [1, 1024, 2048], "e_gmlp_w": [1, 4, 128, 128], "e_gmlp_b": [1, 4, 128],
    "e_conv_w": [1, 31, 512], "e_conv_b": [1, 512], "e_conv_ln_g": [1, 512], "e_conv_ln_b": [1, 512],
    "e_w_out": [1, 1024, 1024], "o_norm": [1, 1024], "o_w_in": [1, 1024, 512], "o_lam_re": [1, 32, 64],
    "o_lam_im": [1, 32, 64], "o_log_dt": [1, 32], "o_b_re": [1, 32, 64, 16], "o_b_im": [1, 32, 64, 16],
    "o_c_re": [1, 32, 16, 64], "o_c_im": [1, 32, 16, 64], "o_d": [1, 512], "o_w_out": [1, 512, 2048],
    "ca_norm": [2, 1024], "ca_mem_norm": [2, 1024], "ca_wq": [2, 1024, 1024], "ca_wk": [2, 1024, 1024],
    "ca_wv": [2, 1024, 1024], "ca_wo": [2, 1024, 1024], "ffn_norm": [2, 1024], "ffn_w_gate": [2, 1024, 2816],
    "ffn_w_up": [2, 1024, 2816], "ffn_w_down": [2, 2816, 1024], "final_norm": [1024],
}


class KB:
    def __init__(self, nc, phases, ntiles, final_norm=True):
        self.nc = nc
        self.S = Sched(nc)
        self.phases = phases
        self.ntiles = ntiles
        self.final_norm = final_norm
        self.ps_rr = 0
        self.ring_rr = 0
        self.scr_rr = 0
        self.sq_rr = 0
        self.dbg_bufs = []
        self.deng = "dve"
        self.dbg = False

    def mm(self, out, lhsT, rhs, start, stop, R, W):
        self.S.op("pe", lambda h: h.matmul(out, lhsT, rhs, start=start, stop=stop), reads=R, writes=W)

    def tr(self, out, in_, R, W, ident=None):
        ident = self.ident if ident is None else ident
        self.S.op("pe", lambda h: h.transpose(out, in_, ident[:]), reads=list(R) + [self.b_const], writes=W)

    def act(self, out, in_, func, R, W, bias=None, scale=None, accum=None):
        kw = {}
        if bias is not None:
            kw["bias"] = bias
        if scale is not None:
            kw["scale"] = scale
        if accum is not None:
            kw["accum_out"] = accum
        self.S.op("act", lambda h: h.activation(out, in_, func, **kw), reads=R, writes=W)

    def tt(self, out, a, b, op, R, W, eng=None):
        eng = eng or self.deng
        self.S.op(eng, lambda h: h.tensor_tensor(out, a, b, op), reads=R, writes=W)

    def ts(self, out, a, s1, s2, op0, op1, R, W, eng=None):
        eng = eng or self.deng
        if op1 is None:
            self.S.op(eng, lambda h: h.tensor_scalar(out, a, s1, s2, op0), reads=R, writes=W)
        else:
            self.S.op(eng, lambda h: h.tensor_scalar(out, a, s1, s2, op0, op1), reads=R, writes=W)

    def stt(self, out, in0, scalar, in1, op0, op1, R, W, eng=None):
        eng = eng or self.deng
        self.S.op(eng, lambda h: h.scalar_tensor_tensor(out, in0, scalar, in1, op0, op1), reads=R, writes=W)

    def cp(self, out, in_, R, W, eng=None):
        eng = eng or self.deng
        self.S.op(eng, lambda h: h.tensor_copy(out, in_), reads=R, writes=W)

    def recip(self, out, in_, R, W):
        self.S.op("dve", lambda h: h.reciprocal(out, in_), reads=R, writes=W)

    def memset(self, ap, val, W, eng=None):
        eng = eng or self.deng
        self.S.op(eng, lambda h: h.memset(ap, val), writes=W)

    def dma(self, q, out, in_, R, W, slow=False):
        if slow:
            self.S.dma(q, lambda h: h.dma_start(out=out, in_=in_, allow_slow_non_contiguous=True), reads=R, writes=W)
        else:
            self.S.dma(q, lambda h: h.dma_start(out=out, in_=in_), reads=R, writes=W)

    def dump(self, name, ap, bufs, dt=F32):
        d = self.nc.dram_tensor("dbg_" + name, list(ap.shape), dt, kind="ExternalOutput").ap()
        b = Buf()
        self.dma("sp", d, ap, list(bufs), [b])
        self.dbg_bufs.append(b)

    def bank(self):
        i = self.ps_rr
        self.ps_rr = (i + 1) % 8
        return self.psum[:, i, :], self.b_ps[i]

    def scr_next(self):
        i = self.scr_rr
        self.scr_rr = (i + 1) % 4
        return self.scr[:, i, :], self.b_scr[i]

    def load_slab(self, src, a, b, q="pool", R=()):
        i = self.ring_rr
        self.ring_rr = (i + 1) % NSLOT
        dst = self.ring[i][:, 0:a * b].rearrange("p (a b) -> p a b", b=b)
        self.dma(q, dst, src, list(R), [self.b_ring[i]])
        return dst, self.b_ring[i]

    def load_w(self, W2d, c0, n):
        kc = W2d.shape[0] // 128
        src = W2d[:, c0:c0 + n].rearrange("(c p) f -> p c f", p=128)
        return self.load_slab(src, kc, n)

    def build(self):
        nc = self.nc
        dr = lambda name, shape, kind="ExternalInput", dt=F32: nc.dram_tensor(name, shape, dt, kind=kind).ap()
        self.x_d = dr("x", [SEQ, D])
        self.mem_d = dr("mem", [MEM, D])
        self.p = {k: dr(k, s) for k, s in PARAM_SHAPES.items()}
        self.ident_d = dr("c_ident", [128, 128])
        self.out_d = dr("out", [SEQ, D], kind="ExternalOutput")
        self.diag_d = dr("scr_diag", [2, 128, 62 * 128], kind="Internal", dt=BF16)
        self.s5_d = dr("scr_s5", [128, 544 * 128], kind="Internal", dt=BF16)
        with contextlib.ExitStack() as st:
            sb = lambda n, s, d: st.enter_context(nc.sbuf_tensor(n, s, d))
            self.xT = sb("xT", [128, 8, 512], F32); self.b_xT = [Buf() for _ in range(8)]
            self.xn = sb("xn", [128, 8, 512], BF16); self.b_xn = [Buf() for _ in range(8)]
            self.sq = sb("sq", [128, 4, 512], BF16); self.b_sq = [Buf() for _ in range(4)]
            self.rstd = sb("rstd", [128, 512], F32); self.b_rstd = Buf()
            self.ring = [sb("ring%d" % i, [128, SLOT], BF16) for i in range(NSLOT)]
            self.b_ring = [Buf() for _ in range(NSLOT)]
            self.bigbf = sb("bigbf", [128, 22, 512], BF16); self.b_bb = [Buf() for _ in range(22)]
            self.bigf = sb("bigf", [128, 8, 512], F32); self.b_bf = [Buf() for _ in range(8)]
            self.hbuf = sb("hbuf", [128, 4, 542], BF16); self.b_hb = [Buf() for _ in range(4)]
            self.scr = sb("scr", [128, 4, 512], F32); self.b_scr = [Buf() for _ in range(4)]
            self.aux = sb("aux", [128, 2, 512], F32); self.b_aux = [Buf() for _ in range(2)]
            self.KT = sb("KT", [128, 2, 8, 256], BF16); self.b_KT = Buf()
            self.V = sb("V", [128, 2, 2, 1024], BF16); self.b_V = Buf()
            self.gw = sb("gw", [128, 4, 128], BF16)
            self.bsbc = sb("bsbc", [128, 4, 128], F32)
            self.ident = sb("ident", [128, 128], F32)
            self.ident_bf = sb("ident_bf", [128, 128], BF16)
            self.ones_bf = sb("ones_bf", [128, 128], BF16)
            self.ones_f = sb("ones_f", [128, 128], F32)
            self.epsb = sb("epsb", [128, 1], F32)
            self.gv = sb("gv", [128, G_N], F32)
            self.fg = sb("fg", [128, 1024], F32)
            self.small = sb("small", [128, 64], F32); self.b_small = [Buf() for _ in range(8)]
            self.b_const = Buf()
            self.cosT = sb("cosT", [128, 16, 16], F32)
            self.sinT = sb("sinT", [128, 16, 16], F32)
            self.c8T = sb("c8T", [128, 16, 64], F32)
            self.s8T = sb("s8T", [128, 16, 64], F32)
            self.r8T = sb("r8T", [128, 16, 64], F32)
            self.apw = sb("apw", [128, 3, 9, 16], F32); self.b_apw = Buf()
            self.ktap = sb("ktap", [128, 32, 128], BF16); self.b_ktap = Buf()
            self.s5c = sb("s5c", [128, 8, 16], F32)
            self.b_s5c = [Buf() for _ in range(8)]
            self.b_tab = Buf()
            self.s5t = sb("s5t", [128, 6, 1024], F32); self.b_s5t = [Buf() for _ in range(6)]
            self.psum = st.enter_context(nc.psum_tensor("ps", [128, 8, 512], F32))
            self.b_ps = [Buf(excl=True) for _ in range(8)]
            self.b_out = Buf()
            self.b_diag = Buf()
            self.b_s5d = Buf()

            self.prologue()
            self.S.barrier()
            two = lambda n: [[Buf() for _ in range(n)] for _ in range(2)]
            self.hb_xT = two(8); self.hb_xn = two(8); self.hb_bb = two(22); self.hb_bf = two(8)
            self.hb_aux = two(2); self.hb_rstd = [Buf(), Buf()]
            self.hb_h = [[Buf(), Buf(), Buf()] for _ in range(4)]
            self.hb_scr = [Buf() for _ in range(8)]
            self.hb_sq = [Buf() for _ in range(8)]
            self.hb_small = two(8)
            self.hb_s5t = [Buf() for _ in range(6)]
            self.scr_rr = 0
            self.sq_rr = 0
            self.ps_rr = 0
            self.stats_ready = [False, False]
            self.pre_next = False
            for ti in range(self.ntiles):
                self.tile(ti)
            self.S.final_wait("sp", [self.b_out] + self.dbg_bufs)
            self.S.emit()
        return nc

    def prologue(self):
        p = self.p
        cW = [self.b_const]
        self.dma("sp", self.ident[:], self.ident_d, [], cW)
        if "l0mix" in self.phases:
            self.pro_gc_loads()
        self.memset(self.ones_bf[:], 1.0, cW)
        self.memset(self.ones_f[:], 1.0, cW)
        self.memset(self.epsb[:], EPS, cW)
        for fc in range(4):
            self.memset(self.hbuf[:, fc, 0:30], 0.0, [self.b_hb[fc]])
        gv = self.gv
        G = self.s5t[:, 5, 0:128]
        bG = [self.b_s5t[5]]
        self.memset(G, 0.0, bG)

        def ld_gain(row, src2d):
            n = src2d.shape[0]
            self.dma("sp", G[row:row + n, :], src2d, [], bG)

        r8 = lambda v: v.rearrange("(c p) -> c p", p=128)
        ld_gain(G_E, r8(p["e_norm"][0]))
        ld_gain(G_O, r8(p["o_norm"][0]))
        ld_gain(G_CA, p["ca_norm"].rearrange("l (c p) -> (l c) p", p=128))
        ld_gain(G_FFN, p["ffn_norm"].rearrange("l (c p) -> (l c) p", p=128))
        ld_gain(G_MEM, p["ca_mem_norm"].rearrange("l (c p) -> (l c) p", p=128))
        ld_gain(G_CB, r8(p["e_conv_b"][0]))
        ld_gain(G_LG, r8(p["e_conv_ln_g"][0]))
        ld_gain(G_LB, r8(p["e_conv_ln_b"][0]))
        ld_gain(G_OD, r8(p["o_d"][0]))
        pb, bpb = self.bank()
        self.tr(pb[:, 0:128], G, bG, [bpb])
        self.cp(gv[:, 0:G_N], pb[:, 0:G_N], [bpb], cW)
        need_mem = "ca0" in self.phases or "ca1" in self.phases
        if need_mem:
            self.pro_mem_loads()
        if "l1mix" in self.phases:
            self.pro_s5_loads()
        self.dma("sp", self.fg[:], p["final_norm"].partition_broadcast(128), [], cW)
        self.dma("sp", self.bsbc[:].rearrange("p g i -> p (g i)"),
                 p["e_gmlp_b"][0].rearrange("g i -> (g i)").partition_broadcast(128), [], cW)
        if "l0mix" in self.phases:
            self.pro_gmlp_conv()
        g5 = self.pro_s5() if "l1mix" in self.phases else iter(())
        next(g5, None)
        if need_mem:
            self.pro_mem()
        next(g5, None)

    def pro_gc_loads(self):
        p = self.p
        cw = self.s5t[:, 1, 0:512]
        self.memset(cw, 0.0, [self.b_s5t[1]])
        self.dma("sp", cw[0:31, :], p["e_conv_w"][0], [], [self.b_s5t[1]])
        wtmp = self.s5t[:, 0, 0:512].rearrange("p (g j) -> p g j", g=4)
        self.dma("sp", wtmp, p["e_gmlp_w"][0].rearrange("g i j -> i g j"), [], [self.b_s5t[0]])

    def pro_gmlp_conv(self):
        p = self.p
        cW = [self.b_const]
        wtmp = self.s5t[:, 0, 0:512].rearrange("p (g j) -> p g j", g=4)
        for g in range(4):
            pb, bpb = self.bank()
            self.tr(pb[:, 0:128], wtmp[:, g, :], [self.b_s5t[0]], [bpb])
            self.cp(self.gw[:, g, :], pb[:, 0:128], [bpb], cW)
            self.memset(self.gw[64:128, g, 0:64], 0.0, cW)
        cw = self.s5t[:, 1, 0:512]
        cwT = self.aux[:, 0, 0:128].rearrange("p (f k) -> p f k", k=32)
        for fc in range(4):
            pb, bpb = self.bank()
            self.tr(pb[:, 0:128], cw[:, fc * 128:(fc + 1) * 128], [self.b_s5t[1]], [bpb])
            self.cp(cwT[:, fc, 0:31], pb[:, 0:31], [bpb], [self.b_aux[0]])
        stage = self.bigbf[:].rearrange("p a b -> p (a b)")
        bdgb = [Buf() for _ in range(62)]
        for half in range(2):
            for i in range(62):
                fc = half * 2 + i // 31
                k = i % 31
                self.ts(stage[:, i * 128:(i + 1) * 128], self.ident[:], cwT[:, fc, k:k + 1], None, ALU.mult, None,
                        [self.b_aux[0], self.b_const], [bdgb[i]])
            self.dma("sp", self.diag_d[half], stage[:, 0:62 * 128], bdgb + self.b_bb[0:16], [self.b_diag])

    def rms_generic(self, src_fn, bsrc, n, gcol, dst_fn, bdst, rstd=None, brstd=None, sqbufs=None):
        if rstd is None:
            rstd = self.rstd[:, 0:n]; brstd = self.b_rstd
        if sqbufs is None:
            sqbufs = self.b_sq
        sqf = self.sq[:].rearrange("p a b -> p (a b)")
        nsq = len(sqbufs)
        w = 2048 // nsq
        pb, bpb = self.bank()
        for dc in range(8):
            k = self.sq_rr % nsq
            self.sq_rr = k + 1
            sq = sqf[:, k * w:k * w + n]
            self.act(sq, src_fn(dc), AF.Square, [bsrc[dc]], [sqbufs[k]])
            self.mm(pb[:, 0:n], self.ones_bf[:], sq, dc == 0, dc == 7, [sqbufs[k], self.b_const], [bpb])
        self.act(rstd, pb[:, 0:n], AF.Sqrt, [bpb, self.b_const], [brstd], bias=self.epsb[:], scale=1.0 / D)
        self.recip(rstd, rstd, [brstd], [brstd])
        for dc in range(8):
            self.stt(dst_fn(dc), src_fn(dc), self.gv[:, gcol + dc:gcol + dc + 1], rstd,
                     ALU.mult, ALU.mult, [bsrc[dc], brstd, self.b_const], [bdst[dc]])

    def pro_mem_loads(self):
        p = self.p
        memtok = self.bigf[:, 0:4, :].rearrange("p a b -> p (a b)").rearrange("p (m d) -> p m d", m=2)
        self.dma("sp", memtok, self.mem_d.rearrange("(m p) d -> p m d", p=128), [], self.b_bf[0:4])
        self.mem_w = [self.load_w(p["ca_wk"][0], 0, 1024), self.load_w(p["ca_wv"][0], 0, 1024),
                      self.load_w(p["ca_wk"][1], 0, 1024)]

    def pro_mem(self):
        p = self.p
        memtok = self.bigf[:, 0:4, :].rearrange("p a b -> p (a b)").rearrange("p (m d) -> p m d", m=2)
        memT = self.bigf[:, 4:8, :].rearrange("p a b -> p (a b)").rearrange("p (c m) -> p c m", c=8)
        bmT = [self.b_bf[4 + dc // 2] for dc in range(8)]
        for dc in range(8):
            pb, bpb = self.bank()
            for mb in range(2):
                self.tr(pb[:, mb * 128:(mb + 1) * 128], memtok[:, mb, dc * 128:(dc + 1) * 128], self.b_bf[0:4], [bpb])
            self.cp(memT[:, dc, :], pb[:, 0:256], [bpb], [bmT[dc]])
        for li in range(2):
            self.rms_generic(lambda dc: memT[:, dc, :], bmT, 256, G_MEM + 8 * li,
                             lambda dc: self.xn[:, dc, 0:256], self.b_xn)
            wk, bwk = self.mem_w[2 * li]
            for oc in range(8):
                pb, bpb = self.bank()
                for dc in range(8):
                    self.mm(pb[:, 0:256], wk[:, dc, oc * 128:(oc + 1) * 128], self.xn[:, dc, 0:256], dc == 0, dc == 7,
                            [bwk, self.b_xn[dc]], [bpb])
                self.act(self.KT[:, li, oc, :], pb[:, 0:256], AF.Copy, [bpb], [self.b_KT])
            wv, bwv = self.mem_w[1] if li == 0 else self.load_w(p["ca_wv"][li], 0, 1024)
            for mc in range(2):
                for half in range(2):
                    pb, bpb = self.bank()
                    for dc in range(8):
                        self.mm(pb, self.xn[:, dc, mc * 128:(mc + 1) * 128], wv[:, dc, half * 512:(half + 1) * 512],
                                dc == 0, dc == 7, [bwv, self.b_xn[dc]], [bpb])
                    self.act(self.V[:, li, mc, half * 512:(half + 1) * 512], pb, AF.Copy, [bpb], [self.b_V])

    def rot_tables(self, cosT, sinT, n, g_re, g_im, tW):
        bg = self.b_aux[1]
        g2a = self.aux[:, 1, 32:48]; g2b = self.aux[:, 1, 48:64]; g2c = self.aux[:, 1, 64:80]
        ta = self.s5t[:, 4, :].rearrange("p (q l) -> p q l", q=16)
        tb = self.s5t[:, 5, :].rearrange("p (q l) -> p q l", q=16)
        W0 = [self.b_s5t[4]]; W1 = [self.b_s5t[5]]
        s = 1
        while s < n:
            if s >= 2:
                gb_re = g_re.unsqueeze(2).to_broadcast([128, 16, s])
                gb_im = g_im.unsqueeze(2).to_broadcast([128, 16, s])
                Rg = [self.b_tab, bg]
                self.tt(ta[:, :, 0:s], cosT[:, :, 0:s], gb_re, ALU.mult, Rg, W0)
                self.tt(tb[:, :, 0:s], sinT[:, :, 0:s], gb_im, ALU.mult, Rg, W1)
                self.tt(cosT[:, :, s:2 * s], ta[:, :, 0:s], tb[:, :, 0:s], ALU.subtract, W0 + W1, tW)
                self.tt(ta[:, :, 0:s], cosT[:, :, 0:s], gb_im, ALU.mult, Rg, W0)
                self.tt(tb[:, :, 0:s], sinT[:, :, 0:s], gb_re, ALU.mult, Rg, W1)
                self.tt(sinT[:, :, s:2 * s], ta[:, :, 0:s], tb[:, :, 0:s], ALU.add, W0 + W1, tW)
            if 2 * s < n:
                self.tt(g2a, g_re, g_re, ALU.mult, [bg], [bg])
                self.tt(g2b, g_im, g_im, ALU.mult, [bg], [bg])
                self.tt(g2c, g_re, g_im, ALU.mult, [bg], [bg])
                self.tt(g_re, g2a, g2b, ALU.subtract, [bg], [bg])
                self.ts(g_im, g2c, 2.0, None, ALU.mult, None, [bg], [bg])
            s *= 2

    def pro_s5_loads(self):
        p = self.p
        sm = self.small
        bsm = self.b_small
        L = self.s5t[:, 5, 128:256]
        bL = [self.b_s5t[5]]
        self.memset(L, 0.0, bL)
        for gi in range(2):
            self.dma("sp", L[0:16, 64 * gi:64 * gi + 64], p["o_lam_re"][0].rearrange("(q gi) p -> gi q p", gi=2)[gi], [], bL)
            self.dma("sp", L[16:32, 64 * gi:64 * gi + 64], p["o_lam_im"][0].rearrange("(q gi) p -> gi q p", gi=2)[gi], [], bL)
        self.dma("sp", sm[32:48, 48:50], p["o_log_dt"][0].rearrange("(q gi) -> q gi", gi=2), [], [bsm[3]])
        for gi in range(2):
            self.cp(L[32:48, 64 * gi:64 * gi + 64], sm[32:48, 48 + gi:49 + gi].to_broadcast([16, 64]), [bsm[3]], bL)
        pb, bpb = self.bank()
        self.tr(pb[:, 0:128], L, bL, [bpb])
        self.cp(sm[:, 0:48], pb[:, 0:48], [bpb], [bsm[0], bsm[1], bsm[2]])
        Zc = self.xT[:].rearrange("p a b -> p (a b)").rearrange("p (z c) -> p z c", c=128)
        self.memset(self.xT[:], 0.0, self.b_xT)
        for part in range(2):
            Cd = p["o_c_re"][0] if part == 0 else p["o_c_im"][0]
            Cv = Cd.rearrange("(cc r) c p -> r c cc p", r=8)
            for j in range(4):
                for gi in range(2):
                    self.dma("act", Zc[32 * j + 16 * gi:32 * j + 16 * gi + 16, part * 16 + j:part * 16 + 16:4,
                                       64 * gi:64 * gi + 64],
                             Cv[2 * j + gi], [], self.b_xT[part * 4:part * 4 + 4])

    def pro_s5(self):
        self.deng = "pool"
        p = self.p
        c = self.s5c
        bc = self.b_s5c
        sm = self.small
        bsm = self.b_small
        lr = sm[:, 0:16]; li_ = sm[:, 16:32]; ldt = sm[:, 32:48]; tmp = sm[:, 48:64]
        R3 = [bsm[0], bsm[1], bsm[2]]
        dt = c[:, 4, :]; ang = c[:, 5, :]; r = c[:, 6, :]; t7 = c[:, 7, :]
        self.act(dt, ldt, AF.Exp, R3, [bc[4]])
        self.tt(ang, li_, dt, ALU.mult, R3 + [bc[4]], [bc[5]])
        self.tt(t7, lr, dt, ALU.mult, R3 + [bc[4]], [bc[7]])
        self.act(r, t7, AF.Exp, [bc[7]], [bc[6]])
        s1 = c[:, 3, :]; c1 = c[:, 2, :]
        for which in range(2):
            m = t7 if which == 0 else tmp
            bm = bc[7] if which == 0 else bsm[3]
            shift = 0.0 if which == 0 else PI / 2
            base = self.aux[:, 0, 0:16]
            self.ts(base, ang, shift, None, ALU.add, None, [bc[5]], [self.b_aux[0]])
            self.cp(m, base, [self.b_aux[0]], [bm])
            for thr in (1, 3, 5, 7):
                cm = self.aux[:, 0, 16:32]
                self.ts(cm, base, thr * PI, -2 * PI, ALU.is_gt, ALU.mult, [self.b_aux[0]], [self.b_aux[0]])
                self.tt(m, m, cm, ALU.add, [bm, self.b_aux[0]], [bm])
            self.act(s1 if which == 0 else c1, m, AF.Sin, [bm], [bc[3] if which == 0 else bc[2]])
        ar = c[:, 0, :]; ai = c[:, 1, :]
        cosT, sinT = self.cosT, self.sinT
        tW = [self.b_tab]
        self.memset(cosT[:, :, 0:1], 1.0, tW)
        self.memset(sinT[:, :, 0:1], 0.0, tW)
        self.cp(cosT[:, :, 1:2], c1.unsqueeze(2), [bc[2]], tW)
        self.cp(sinT[:, :, 1:2], s1.unsqueeze(2), [bc[3]], tW)
        self.tt(ar, r, c1, ALU.mult, [bc[6], bc[2]], [bc[0]])
        self.tt(ai, r, s1, ALU.mult, [bc[6], bc[3]], [bc[1]])
        gre = self.aux[:, 1, 0:16]; gim = self.aux[:, 1, 16:32]
        bg = self.b_aux[1]
        self.cp(gre, c1, [bc[2]], [bg])
        self.cp(gim, s1, [bc[3]], [bg])
        self.rot_tables(cosT, sinT, 16, gre, gim, tW)
        den = self.aux[:, 0, 32:48]; am1 = self.aux[:, 0, 48:64]
        qr = self.aux[:, 0, 64:80]; qi = self.aux[:, 0, 80:96]; u1 = self.aux[:, 0, 96:112]; u2 = self.aux[:, 0, 112:128]
        A0 = [self.b_aux[0]]
        Rall = R3 + [bc[0], bc[1]] + A0
        self.tt(den, lr, lr, ALU.mult, Rall, A0)
        self.tt(u1, li_, li_, ALU.mult, Rall, A0)
        self.tt(den, den, u1, ALU.add, A0, A0)
        self.recip(den, den, A0, A0)
        self.ts(am1, ar, -1.0, None, ALU.add, None, Rall, A0)
        self.tt(u1, am1, lr, ALU.mult, Rall, A0)
        self.tt(u2, ai, li_, ALU.mult, Rall, A0)
        self.tt(u1, u1, u2, ALU.add, A0, A0)
        self.tt(qr, u1, den, ALU.mult, A0, A0)
        self.tt(u1, ai, lr, ALU.mult, Rall, A0)
        self.tt(u2, am1, li_, ALU.mult, Rall, A0)
        self.tt(u1, u1, u2, ALU.subtract, A0, A0)
        self.tt(qi, u1, den, ALU.mult, A0, A0)
        apw = self.apw
        bap = self.b_apw
        self.memset(apw[:, 0, 0, :], 1.0, [bap])
        self.memset(apw[:, 1, 0, :], 0.0, [bap])
        self.memset(apw[:, 2, 0, :], 0.0, [bap])
        rp = t7
        self.cp(rp, r, [bc[6]], [bc[7]])
        for tau in range(1, 9):
            self.tt(apw[:, 0, tau, :], rp, cosT[:, :, tau], ALU.mult, [bc[7], self.b_tab], [bap])
            self.tt(apw[:, 1, tau, :], rp, sinT[:, :, tau], ALU.mult, [bc[7], self.b_tab], [bap])
            self.ts(apw[:, 2, tau, :], apw[:, 1, tau, :], -1.0, None, ALU.mult, None, [bap], [bap])
            if tau < 8:
                self.tt(rp, rp, r, ALU.mult, [bc[7], bc[6]], [bc[7]])
        c8T, s8T, r8T = self.c8T, self.s8T, self.r8T
        self.memset(c8T[:, :, 0:1], 1.0, tW)
        self.memset(s8T[:, :, 0:1], 0.0, tW)
        self.cp(c8T[:, :, 1:2], cosT[:, :, 8:9], tW, tW)
        self.cp(s8T[:, :, 1:2], sinT[:, :, 8:9], tW, tW)
        self.cp(gre, cosT[:, :, 8], tW, [bg])
        self.cp(gim, sinT[:, :, 8], tW, [bg])
        self.rot_tables(c8T, s8T, 64, gre, gim, tW)
        self.cp(r8T[:], rp.unsqueeze(2).to_broadcast([128, 16, 64]), [bc[7]], tW)
        self.memset(r8T[:, :, 0:1], 0.0, tW)
        def v3(ap):
            return ap.rearrange("p (q c) -> p q c", c=16)
        Bre = v3(self.s5t[:, 4, 0:256]); Bim = v3(self.s5t[:, 4, 256:512])
        Bbr = v3(self.s5t[:, 4, 512:768]); Bbi = v3(self.s5t[:, 4, 768:1024])
        t1 = v3(self.s5t[:, 5, 0:256]); t2 = v3(self.s5t[:, 5, 256:512])
        ABr = v3(self.s5t[:, 5, 512:768]); ABi = v3(self.s5t[:, 5, 768:1024])
        B4 = [self.b_s5t[4]]; B5 = [self.b_s5t[5]]
        self.dma("sp", Bre, p["o_b_re"][0].rearrange("(q gi) p c -> (gi p) q c", gi=2), [], B4)
        self.dma("sp", Bim, p["o_b_im"][0].rearrange("(q gi) p c -> (gi p) q c", gi=2), [], B4)
        qrb = qr.unsqueeze(2).to_broadcast([128, 16, 16]); qib = qi.unsqueeze(2).to_broadcast([128, 16, 16])
        self.tt(t1, Bre, qrb, ALU.mult, B4 + A0, B5)
        self.tt(t2, Bim, qib, ALU.mult, B4 + A0, B5)
        self.tt(Bbr, t1, t2, ALU.subtract, B5, B4)
        self.tt(t1, Bim, qrb, ALU.mult, B4 + A0, B5)
        self.tt(t2, Bre, qib, ALU.mult, B4 + A0, B5)
        self.tt(Bbi, t1, t2, ALU.add, B5, B4)
        self.cp(ar, apw[:, 0, 8, :], [bap], [bc[0]])
        self.cp(ai, apw[:, 1, 8, :], [bap], [bc[1]])
        self.memset(c[:, 2, :], 0.0, [bc[2]])
        self.memset(c[:, 3, :], 0.0, [bc[3]])
        self.deng = "dve"
        yield
        CTf = self.bigf[:].rearrange("p a b -> p (a b)").rearrange("p (t q c) -> p t q c", t=2, q=16)
        Zc = self.xT[:].rearrange("p a b -> p (a b)").rearrange("p (z c) -> p z c", c=128)
        for part in range(2):
            for q4 in range(4):
                pb, bpb = self.bank()
                for k in range(4):
                    q = q4 * 4 + k
                    z = part * 16 + q
                    self.tr(pb[:, k * 128:(k + 1) * 128], Zc[:, z, :], [self.b_xT[z // 4]], [bpb])
                self.act(CTf[:, part, q4 * 4:q4 * 4 + 4, :], pb.rearrange("p (k c) -> p k c", c=128), AF.Copy, [bpb],
                         [self.b_bf[part * 4 + q4]], scale=(1.0 if part == 0 else -1.0))
        CTb = self.s5t[:, 2:4, :].rearrange("p a b -> p (a b)").bitcast(BF16).rearrange("p (t q c) -> p t q c", t=2, q=16)
        bCTb = self.b_s5t[2:4]
        self.cp(CTb, CTf, self.b_bf, bCTb)
        self.cp(self.ident_bf[:], self.ident[:], [self.b_const], [self.b_const])
        stg_flat = self.bigbf[:].rearrange("p a b -> p (a b)")
        stg = [stg_flat[:, 0:4096], stg_flat[:, 4096:8192]]
        bstg = [self.b_bb[0:8], self.b_bb[8:16]]
        Zt = self.s5t[:, 0:2, :].rearrange("p a b -> p (a b)").bitcast(BF16).rearrange("p (t q c) -> p t q c", t=2, q=16)
        bZ = self.b_s5t[0:2]
        W1v = self.s5_d[:, 0:256 * 128].rearrange("p (cc s x) -> p cc s x", cc=4, s=8)
        si = 0
        ktf = self.ktap[:].rearrange("p a b -> p (a b)")
        for tau in range(8):
            s = 7 - tau
            apr_b = apw[:, 0, tau, :].unsqueeze(2).to_broadcast([128, 16, 16])
            api_b = apw[:, 1, tau, :].unsqueeze(2).to_broadcast([128, 16, 16])
            self.tt(t1, Bbr, apr_b, ALU.mult, B4 + [bap], B5)
            self.tt(t2, Bbi, api_b, ALU.mult, B4 + [bap], B5)
            self.tt(ABr, t1, t2, ALU.subtract, B5, B5)
            self.tt(t1, Bbr, api_b, ALU.mult, B4 + [bap], B5)
            self.tt(t2, Bbi, apr_b, ALU.mult, B4 + [bap], B5)
            self.tt(ABi, t1, t2, ALU.add, B5, B5)
            if tau == 0:
                bZj = [[Buf() for _ in range(4)] for _ in range(2)]
                self.memset(Zt, 0.0, bZ + bZj[0] + bZj[1])
            n = 0
            for part in range(2):
                AB = ABr if part == 0 else ABi
                for gi in range(2):
                    for j in range(4):
                        dst = Zt[64 * gi:64 * gi + 64, part, j::4, 32 * j + 16 * gi:32 * j + 16 * gi + 16]
                        src = AB[64 * gi:64 * gi + 64, j::4, :]
                        if n % 2 == 0:
                            self.cp(dst, src, B5, [bZj[part][j]])
                        else:
                            self.act(dst, src, AF.Copy, B5, [bZj[part][j]])
                        n += 1
            sl = stg[si % 2]; bsl = bstg[si % 2]; si += 1
            for cc in range(4):
                for hb in range(2):
                    pb, bpb = self.bank()
                    pbb = pb.bitcast(BF16)
                    for k in range(4):
                        jj = hb * 2 + k // 2
                        part = k % 2
                        q = cc * 4 + jj
                        self.tr(pbb[:, k * 128:(k + 1) * 128], Zt[:, part, q, :], [bZj[part][q % 4]], [bpb], ident=self.ident_bf)
                    o0 = (cc * 8 + hb * 4) * 128
                    if hb == 0:
                        self.act(sl[:, o0:o0 + 512], pbb[:, 0:512], AF.Copy, [bpb], [bsl[cc * 2 + hb]])
                    else:
                        self.cp(sl[:, o0:o0 + 512], pbb[:, 0:512], [bpb], [bsl[cc * 2 + hb]])
            self.dma("sp", W1v[:, :, s, :], sl.rearrange("p (cc x) -> p cc x", cc=4), bsl, [self.b_s5d])
            for cc in range(4):
                pb, bpb = self.bank()
                n = 0
                for jj in range(4):
                    q = cc * 4 + jj
                    for part in range(2):
                        self.mm(pb[:, 0:128], Zt[:, part, q, :], CTb[:, part, q, :], n == 0, n == 7,
                                [bZj[part][q % 4]] + bCTb, [bpb])
                        n += 1
                blk = cc * 8 + tau
                self.cp(ktf[:, blk * 128:(blk + 1) * 128], pb[:, 0:128], [bpb], [self.b_ktap])
        W2v = self.s5_d[:, 288 * 128:544 * 128].rearrange("p (cc l t x) -> p l t cc x", cc=4, l=8, t=2)
        Cc = self.s5t[:, 4, 0:512].rearrange("p (t q c) -> p t q c", t=2, q=16)
        for part in range(2):
            for gi in range(2):
                for j in range(4):
                    self.cp(Cc[64 * gi:64 * gi + 64, part, j::4, :],
                            CTf[64 * gi:64 * gi + 64, part, j::4, 32 * j + 16 * gi:32 * j + 16 * gi + 16],
                            self.b_bf, B4)
        w1_ = v3(self.s5t[:, 5, 0:256]); w2_ = v3(self.s5t[:, 5, 256:512])
        oc = self.s5t[:, 5, 512:1024].rearrange("p (t q c) -> p t q c", t=2, q=16)
        bW = [[[Buf() for _ in range(4)] for _ in range(2)] for _ in range(2)]
        for k in range(2):
            self.memset(stg[k], 0.0, bstg[k] + bW[k][0] + bW[k][1])
        for l in range(8):
            apr_b = apw[:, 0, l + 1, :].unsqueeze(2).to_broadcast([128, 16, 16])
            api_b = apw[:, 1, l + 1, :].unsqueeze(2).to_broadcast([128, 16, 16])
            sl = stg[si % 2]; bsl = bstg[si % 2]; bWs = bW[si % 2]; si += 1
            sl4 = sl.rearrange("p (t q c) -> p t q c", t=2, q=16)
            self.tt(w1_, Cc[:, 0], apr_b, ALU.mult, B4 + [bap], B5)
            self.tt(w2_, Cc[:, 1], api_b, ALU.mult, B4 + [bap], B5)
            self.tt(oc[:, 0], w1_, w2_, ALU.add, B5, B5)
            self.tt(w1_, Cc[:, 1], apr_b, ALU.mult, B4 + [bap], B5)
            self.tt(w2_, Cc[:, 0], api_b, ALU.mult, B4 + [bap], B5)
            self.tt(oc[:, 1], w1_, w2_, ALU.subtract, B5, B5)
            n = 0
            for part in range(2):
                for gi in range(2):
                    for j in range(4):
                        dst = sl4[64 * gi:64 * gi + 64, part, j::4, 32 * j + 16 * gi:32 * j + 16 * gi + 16]
                        src = oc[64 * gi:64 * gi + 64, part, j::4, :]
                        if n % 2 == 0:
                            self.cp(dst, src, B5, [bWs[part][j]])
                        else:
                            self.act(dst, src, AF.Copy, B5, [bWs[part][j]])
                        n += 1
            slv = sl.rearrange("p (t cc x) -> p t cc x", t=2, cc=4)
            for part in range(2):
                self.dma("sp", W2v[:, l, part], slv[:, part], bWs[part], [self.b_s5d])
        if self.dbg:
            self.dump("s5d", self.s5_d, [self.b_s5d], BF16)
            self.dump("apw", self.apw[:], [self.b_apw])
            self.dump("c8T", self.c8T[:], [self.b_tab])
            self.dump("s8T", self.s8T[:], [self.b_tab])
            self.dump("r8T", self.r8T[:], [self.b_tab])
            self.dump("ctf", self.bigf[:], self.b_bf)

    def scr_next(self):
        i = self.scr_rr % 8
        self.scr_rr = i + 1
        return self.scr[:].rearrange("p a b -> p (a b)")[:, i * 256:(i + 1) * 256], self.hb_scr[i]

    def scr_full(self):
        i0 = ((self.scr_rr + 1) // 2 * 2) % 8
        self.scr_rr = (i0 + 2) % 8
        return self.scr[:, i0 // 2, :], [self.hb_scr[i0], self.hb_scr[i0 + 1]]

    def load_x(self, ti):
        t0 = ti * T
        self.dma("sp", self.s5t[:, 0:4, :], self.x_d[t0:t0 + T, :].rearrange("(t p) d -> p t d", p=128), [],
                 self.hb_s5t[0:4])

    def tile(self, ti):
        t0 = ti * T
        HS = (0, 1)
        xtok = self.bigf[:].rearrange("p a b -> p (a b)").rearrange("p (t d) -> p t d", t=4)
        allbf = [b for h in HS for b in self.hb_bf[h]]
        xin = self.s5t[:, 0:4, :]
        if ti == 0:
            self.load_x(0)
        for h in HS:
            for dc in range(8):
                pb, bpb = self.bank()
                for k in range(2):
                    tb = 2 * h + k
                    self.tr(pb[:, k * 128:(k + 1) * 128], xin[:, tb, dc * 128:(dc + 1) * 128], [self.hb_s5t[tb]], [bpb])
                dst = self.xT[:, dc, h * 256:(h + 1) * 256]
                if dc % 2 == 0:
                    self.act(dst, pb[:, 0:256], AF.Copy, [bpb], [self.hb_xT[h][dc]])
                else:
                    self.cp(dst, pb[:, 0:256], [bpb], [self.hb_xT[h][dc]])
        prefetched = False
        for pi, ph in enumerate(self.phases):
            self.pre_next = pi + 1 < len(self.phases)
            if ph == "l0mix":
                self.l0_mixer(ti)
            elif ph == "l1mix":
                self.l1_mixer(ti)
                if ti + 1 < self.ntiles:
                    self.load_x(ti + 1)
                    prefetched = True
            elif ph in ("ca0", "ca1"):
                self.cross_attn(int(ph[2]))
            elif ph in ("ffn0", "ffn1"):
                self.ffn(int(ph[3]))
        if ti + 1 < self.ntiles and not prefetched:
            self.load_x(ti + 1)
        otok = xtok
        for tb in range(4):
            h = tb // 2
            cs = slice(tb * 128, (tb + 1) * 128)
            banks = []
            sm = self.small
            ssq = sm[:, 32 + 2 * (tb % 2):34 + 2 * (tb % 2)]
            bss = self.hb_small[h][4 + tb % 2]
            for half in range(2):
                pb, bpb = self.bank()
                for j in range(4):
                    dc = half * 4 + j
                    self.tr(pb[:, j * 128:(j + 1) * 128], self.xT[:, dc, cs], [self.hb_xT[h][dc]], [bpb])
                banks.append((pb, bpb))
                if self.final_norm:
                    s_ = self.scr[:, (self.scr_rr // 2) % 4, :]
                    k0 = ((self.scr_rr // 2) % 4) * 2
                    self.scr_rr = (k0 + 2) % 8
                    bs_ = [self.hb_scr[k0], self.hb_scr[k0 + 1]]
                    self.act(s_, pb, AF.Square, [bpb], bs_ + [bss], accum=ssq[:, half:half + 1])
            if self.final_norm:
                rs = sm[:, 36 + (tb % 2):37 + (tb % 2)]
                brs = self.hb_small[h][6 + tb % 2]
                self.tt(rs, ssq[:, 0:1], ssq[:, 1:2], ALU.add, [bss], [brs])
                self.act(rs, rs, AF.Sqrt, [brs, self.b_const], [brs], bias=self.epsb[:], scale=1.0 / D)
                self.recip(rs, rs, [brs], [brs])
                for half in range(2):
                    pb, bpb = banks[half]
                    self.stt(otok[:, tb, half * 512:(half + 1) * 512], pb, rs, self.fg[:, half * 512:(half + 1) * 512],
                             ALU.mult, ALU.mult, [bpb, brs, self.b_const],
                             [self.hb_bf[0][2 * tb + half], self.hb_bf[1][2 * tb + half]])
            else:
                for half in range(2):
                    pb, bpb = banks[half]
                    self.cp(otok[:, tb, half * 512:(half + 1) * 512], pb, [bpb],
                            [self.hb_bf[0][2 * tb + half], self.hb_bf[1][2 * tb + half]])
        self.dma("sp", self.out_d[t0:t0 + T, :].rearrange("(t p) d -> p t d", p=128), otok, allbf, [self.b_out])

    def norm_stats(self, h):
        cs = slice(h * 256, (h + 1) * 256)
        sqf = self.sq[:].rearrange("p a b -> p (a b)")
        rstd = self.rstd[:, cs]
        brstd = self.hb_rstd[h]
        pb, bpb = self.bank()
        for dc in range(8):
            k = self.sq_rr % 8
            self.sq_rr = k + 1
            sq = sqf[:, k * 256:(k + 1) * 256]
            self.act(sq, self.xT[:, dc, cs], AF.Square, [self.hb_xT[h][dc]], [self.hb_sq[k]])
            self.mm(pb[:, 0:256], self.ones_bf[:], sq, dc == 0, dc == 7, [self.hb_sq[k], self.b_const], [bpb])
        self.act(rstd, pb[:, 0:256], AF.Sqrt, [bpb, self.b_const], [brstd], bias=self.epsb[:], scale=1.0 / D)
        self.recip(rstd, rstd, [brstd], [brstd])
        self.stats_ready[h] = True

    def rmsnorm(self, h, gcol):
        cs = slice(h * 256, (h + 1) * 256)
        if not self.stats_ready[h]:
            self.norm_stats(h)
        for dc in range(8):
            self.stt(self.xn[:, dc, cs], self.xT[:, dc, cs], self.gv[:, gcol + dc:gcol + dc + 1], self.rstd[:, cs],
                     ALU.mult, ALU.mult, [self.hb_xT[h][dc], self.hb_rstd[h], self.b_const], [self.hb_xn[h][dc]])
        self.stats_ready[h] = False

    def resid_add(self, h, dc, pb, bpb):
        x = self.xT[:, dc, h * 256:(h + 1) * 256]
        self.tt(x, x, pb[:, 0:256], ALU.add, [self.hb_xT[h][dc], bpb], [self.hb_xT[h][dc]])

    def l0_mixer(self, ti):
        p = self.p
        bb, bf = self.bigbf, self.bigf
        HS = (0, 1)
        CS = [slice(0, 256), slice(256, 512)]
        for h in HS:
            self.rmsnorm(h, G_E)
        w0, bw0 = self.load_w(p["e_w_in"][0], 0, 1024)
        for h in HS:
            bbb, bbf, bxn = self.hb_bb[h], self.hb_bf[h], self.hb_xn[h]
            for fc in range(4):
                pb, bpb = self.bank()
                for dc in range(8):
                    self.mm(pb[:, 0:256], w0[:, dc, fc * 128:(fc + 1) * 128], self.xn[:, dc, CS[h]], dc == 0, dc == 7,
                            [bw0, bxn[dc]], [bpb])
                self.act(bf[:, fc, CS[h]], pb[:, 0:256], AF.Gelu, [bpb], [bbf[fc]])
            for k in range(2):
                tb = 2 * h + k
                pb, bpb = self.bank()
                for dc in range(8):
                    self.mm(pb, self.xn[:, dc, tb * 128:(tb + 1) * 128], w0[:, dc, 512:1024], dc == 0, dc == 7,
                            [bw0, bxn[dc]], [bpb])
                i0 = ((self.scr_rr + 1) // 2 * 2) % 8
                self.scr_rr = (i0 + 2) % 8
                s_ = self.scr[:, i0 // 2, :]
                bs_ = [self.hb_scr[i0], self.hb_scr[i0 + 1]]
                self.act(s_, pb, AF.Gelu, [bpb], bs_)
                sm = self.small
                st6 = sm[:, 8 * k:8 * k + 6]; mv = sm[:, 16 + 4 * k:16 + 4 * k + 2]
                rv = sm[:, 16 + 4 * k + 2:16 + 4 * k + 3]
                bsm = self.hb_small[0][k]
                self.S.op("dve", lambda hh, st6=st6, s_=s_: hh.bn_stats(st6, s_), reads=bs_, writes=[bsm])
                self.S.op("dve", lambda hh, st6=st6, mv=mv: hh.bn_aggr(mv, st6), reads=[bsm], writes=[bsm])
                self.act(rv, mv[:, 1:2], AF.Sqrt, [bsm, self.b_const], [bsm], bias=self.epsb[:], scale=1.0)
                self.recip(rv, rv, [bsm], [bsm])
                self.ts(bb[:, 8 + tb, :], s_, mv[:, 0:1], rv, ALU.subtract, ALU.mult, bs_ + [bsm],
                        [self.hb_bb[0][8 + tb], self.hb_bb[1][8 + tb]])
        w1, bw1 = self.load_w(p["e_w_in"][0], 1024, 1024)
        for h in HS:
            bxn = self.hb_xn[h]
            for fc in range(4):
                pa, bpa = self.bank()
                for dc in range(8):
                    self.mm(pa[:, 0:256], w1[:, dc, fc * 128:(fc + 1) * 128], self.xn[:, dc, CS[h]], dc == 0, dc == 7,
                            [bw1, bxn[dc]], [bpa])
                pg, bpg = self.bank()
                for dc in range(8):
                    self.mm(pg[:, 0:256], w1[:, dc, 512 + fc * 128:512 + (fc + 1) * 128], self.xn[:, dc, CS[h]],
                            dc == 0, dc == 7, [bw1, bxn[dc]], [bpg])
                s_, bs_ = self.scr_next()
                self.act(s_, pg[:, 0:256], AF.Sigmoid, [bpg], [bs_])
                self.tt(self.hbuf[:, fc, 30 + h * 256:30 + (h + 1) * 256], pa[:, 0:256], s_, ALU.mult, [bpa, bs_],
                        [self.hb_h[fc][1 + h]])
        for h in HS:
            bbb, bbf = self.hb_bb[h], self.hb_bf[h]
            for g in range(4):
                pb, bpb = self.bank()
                for k in range(2):
                    tb = 2 * h + k
                    o = pb[:, k * 128:(k + 1) * 128]
                    self.mm(o, bb[:, 8 + tb, g * 128:(g + 1) * 128], self.gw[:, g, :], True, True,
                            [self.hb_bb[0][8 + tb], self.hb_bb[1][8 + tb], self.b_const], [bpb])
                s_, bs_ = self.scr_next()
                self.tt(s_.rearrange("p (k i) -> p k i", k=2), pb[:, 0:256].rearrange("p (k i) -> p k i", k=2),
                        self.bsbc[:, g, :].unsqueeze(1).to_broadcast([128, 2, 128]), ALU.add, [bpb, self.b_const], [bs_])
                self.tt(bb[:, g, CS[h]], bf[:, g, CS[h]], s_, ALU.mult, [bbf[g], bs_], [bbb[g]])
        for half in range(2):
            dg, bdg = self.load_slab(self.diag_d[half].rearrange("p (a b) -> p a b", b=128), 62, 128, q="sp",
                                     R=[self.b_diag])
            if half == 0:
                for f2 in range(2):
                    fc = half * 2 + f2
                    base = f2 * 31
                    pb, bpb = self.bank()
                    for k in range(31):
                        self.mm(pb, dg[:, base + k, :], self.hbuf[:, fc, k:k + 512], k == 0, k == 30,
                                [bdg] + self.hb_h[fc], [bpb])
                    self.act(bf[:, 4 + fc, :], pb, AF.Identity, [bpb, self.b_const],
                             [self.hb_bf[0][4 + fc], self.hb_bf[1][4 + fc]],
                             bias=self.gv[:, G_CB + fc:G_CB + fc + 1], scale=1.0)
                    self.cp(self.hbuf[:, fc, 0:30], self.hbuf[:, fc, 512:542], [self.hb_h[fc][2]], [self.hb_h[fc][0]])
            else:
                for h in HS:
                    for f2 in range(2):
                        fc = half * 2 + f2
                        base = f2 * 31
                        pb, bpb = self.bank()
                        Rh = [bdg, self.hb_h[fc][0], self.hb_h[fc][1]] if h == 0 else [bdg, self.hb_h[fc][1], self.hb_h[fc][2]]
                        for k in range(31):
                            self.mm(pb[:, 0:256], dg[:, base + k, :], self.hbuf[:, fc, h * 256 + k:h * 256 + k + 256],
                                    k == 0, k == 30, Rh, [bpb])
                        self.act(bf[:, 4 + fc, CS[h]], pb[:, 0:256], AF.Identity, [bpb, self.b_const], [self.hb_bf[h][4 + fc]],
                                 bias=self.gv[:, G_CB + fc:G_CB + fc + 1], scale=1.0)
                        if h == 1:
                            self.cp(self.hbuf[:, fc, 0:30], self.hbuf[:, fc, 512:542], [self.hb_h[fc][2]], [self.hb_h[fc][0]])
                    self.l0_ln(h)
        wo, bwo = self.load_w(p["e_w_out"][0], 0, 1024)
        for h in HS:
            bbb = self.hb_bb[h]
            for dc in range(8):
                pb, bpb = self.bank()
                for kc in range(8):
                    self.mm(pb[:, 0:256], wo[:, kc, dc * 128:(dc + 1) * 128], bb[:, kc, CS[h]], kc == 0, kc == 7,
                            [bwo, bbb[kc]], [bpb])
                self.resid_add(h, dc, pb, bpb)
            if self.pre_next:
                self.norm_stats(h)

    def l0_ln(self, h):
        bb, bf = self.bigbf, self.bigf
        CS = [slice(0, 256), slice(256, 512)]
        bbb, bbf = self.hb_bb[h], self.hb_bf[h]
        pm, bpm = self.bank()
        for fc in range(4):
            s_, bs_ = self.scr_next()
            sb_ = s_.bitcast(BF16)[:, 0:256]
            self.act(sb_, bf[:, 4 + fc, CS[h]], AF.Copy, [bbf[4 + fc]], [bs_])
            self.mm(pm[:, 0:256], self.ones_bf[:], sb_, fc == 0, fc == 3, [bs_, self.b_const], [bpm])
        pq, bpq = self.bank()
        for fc in range(4):
            s_, bs_ = self.scr_next()
            sb_ = s_.bitcast(BF16)[:, 0:256]
            self.act(sb_, bf[:, 4 + fc, CS[h]], AF.Square, [bbf[4 + fc]], [bs_])
            self.mm(pq[:, 0:256], self.ones_bf[:], sb_, fc == 0, fc == 3, [bs_, self.b_const], [bpq])
        mean = self.aux[:, 0, CS[h]]; rs = self.aux[:, 1, CS[h]]
        bmean, brs = self.hb_aux[h]
        self.ts(mean, pm[:, 0:256], 1.0 / 512, None, ALU.mult, None, [bpm], [bmean])
        self.tt(rs, mean, mean, ALU.mult, [bmean], [brs])
        self.stt(rs, pq[:, 0:256], 1.0 / 512, rs, ALU.mult, ALU.subtract, [bpq, brs], [brs])
        self.act(rs, rs, AF.Sqrt, [brs, self.b_const], [brs], bias=self.epsb[:], scale=1.0)
        self.recip(rs, rs, [brs], [brs])
        for fc in range(4):
            s_, bs_ = self.scr_next()
            self.tt(s_, bf[:, 4 + fc, CS[h]], mean, ALU.subtract, [bbf[4 + fc], bmean], [bs_])
            self.tt(s_, s_, rs, ALU.mult, [bs_, brs], [bs_])
            self.act(bb[:, 4 + fc, CS[h]], s_, AF.Silu, [bs_, self.b_const], [bbb[4 + fc]],
                     bias=self.gv[:, G_LB + fc:G_LB + fc + 1], scale=self.gv[:, G_LG + fc:G_LG + fc + 1])

    def cross_attn(self, li):
        p = self.p
        bb = self.bigbf
        HS = (0, 1)
        CS = [slice(0, 256), slice(256, 512)]
        for h in HS:
            self.rmsnorm(h, G_CA + 8 * li)
        wq, bwq = self.load_w(p["ca_wq"][li], 0, 1024)
        for h in HS:
            bbb, bxn = self.hb_bb[h], self.hb_xn[h]
            for oc in range(8):
                pb, bpb = self.bank()
                for dc in range(8):
                    self.mm(pb[:, 0:256], wq[:, dc, oc * 128:(oc + 1) * 128], self.xn[:, dc, CS[h]], dc == 0, dc == 7,
                            [bwq, bxn[dc]], [bpb])
                self.act(bb[:, oc, CS[h]], pb[:, 0:256], AF.Copy, [bpb], [bbb[oc]], scale=1.0 / 16)
        for h in HS:
            bbb = self.hb_bb[h]
            for hd in range(4):
                pc = 16 + 2 * (hd % 2)
                for mc in range(2):
                    pb, bpb = self.bank()
                    for j in range(2):
                        self.mm(pb[:, 0:256], self.KT[:, li, 2 * hd + j, mc * 128:(mc + 1) * 128], bb[:, 2 * hd + j, CS[h]],
                                j == 0, j == 1, [self.b_KT, bbb[2 * hd + j]], [bpb])
                    self.act(bb[:, pc + mc, CS[h]], pb[:, 0:256], AF.Exp, [bpb], [bbb[pc + mc]])
                pd, bpd = self.bank()
                for mc in range(2):
                    self.mm(pd[:, 0:256], self.ones_bf[:], bb[:, pc + mc, CS[h]], mc == 0, mc == 1,
                            [bbb[pc + mc], self.b_const], [bpd])
                rden = self.aux[:, hd % 2, CS[h]]; brd = self.hb_aux[h][hd % 2]
                self.recip(rden, pd[:, 0:256], [bpd], [brd])
                for j in range(2):
                    po, bpo = self.bank()
                    for mc in range(2):
                        self.mm(po[:, 0:256], self.V[:, li, mc, (2 * hd + j) * 128:(2 * hd + j + 1) * 128], bb[:, pc + mc, CS[h]],
                                mc == 0, mc == 1, [self.b_V, bbb[pc + mc]], [bpo])
                    self.tt(bb[:, 8 + 2 * hd + j, CS[h]], po[:, 0:256], rden, ALU.mult, [bpo, brd], [bbb[8 + 2 * hd + j]])
        wo, bwo = self.load_w(p["ca_wo"][li], 0, 1024)
        for h in HS:
            bbb = self.hb_bb[h]
            for dc in range(8):
                pb, bpb = self.bank()
                for kc in range(8):
                    self.mm(pb[:, 0:256], wo[:, kc, dc * 128:(dc + 1) * 128], bb[:, 8 + kc, CS[h]], kc == 0, kc == 7,
                            [bwo, bbb[8 + kc]], [bpb])
                self.resid_add(h, dc, pb, bpb)
            if self.pre_next:
                self.norm_stats(h)

    def ffn(self, li):
        p = self.p
        bb = self.bigbf
        HS = (0, 1)
        CS = [slice(0, 256), slice(256, 512)]
        for h in HS:
            self.rmsnorm(h, G_FFN + 8 * li)
        for s in range(6):
            nh = 4 if s < 5 else 2
            i = self.ring_rr
            self.ring_rr = (i + 1) % NSLOT
            slot = self.ring[i][:, 0:8192].rearrange("p (a b) -> p a b", b=1024)
            bsl = self.b_ring[i]
            c0 = s * 512
            for k, wname in enumerate(("ffn_w_gate", "ffn_w_up")):
                src = p[wname][li][:, c0:c0 + nh * 128].rearrange("(c p) f -> p c f", p=128)
                self.dma("pool", slot[:, :, k * 512:k * 512 + nh * 128], src, [], [bsl])
            if True:
                for h in HS:
                    bbb, bxn = self.hb_bb[h], self.hb_xn[h]
                    for j in range(nh):
                        hc = s * 4 + j
                        pg, bpg = self.bank()
                        for dc in range(8):
                            self.mm(pg[:, 0:256], slot[:, dc, j * 128:(j + 1) * 128], self.xn[:, dc, CS[h]], dc == 0, dc == 7,
                                    [bsl, bxn[dc]], [bpg])
                        pu, bpu = self.bank()
                        for dc in range(8):
                            self.mm(pu[:, 0:256], slot[:, dc, 512 + j * 128:512 + (j + 1) * 128], self.xn[:, dc, CS[h]],
                                    dc == 0, dc == 7, [bsl, bxn[dc]], [bpu])
                        s_, bs_ = self.scr_next()
                        self.act(s_, pg[:, 0:256], AF.Silu, [bpg], [bs_])
                        self.tt(bb[:, hc, CS[h]], s_, pu[:, 0:256], ALU.mult, [bs_, bpu], [bbb[hc]])
            else:
                for j in range(nh):
                    hc = s * 4 + j
                    pg, bpg = self.bank()
                    for dc in range(8):
                        self.mm(pg, slot[:, dc, j * 128:(j + 1) * 128], self.xn[:, dc, :], dc == 0, dc == 7,
                                [bsl, self.hb_xn[0][dc], self.hb_xn[1][dc]], [bpg])
                    pu, bpu = self.bank()
                    for dc in range(8):
                        self.mm(pu, slot[:, dc, 512 + j * 128:512 + (j + 1) * 128], self.xn[:, dc, :], dc == 0, dc == 7,
                                [bsl, self.hb_xn[0][dc], self.hb_xn[1][dc]], [bpu])
                    s_, bs_ = self.scr_full()
                    self.act(s_, pg, AF.Silu, [bpg], bs_)
                    self.tt(bb[:, hc, :], s_, pu, ALU.mult, bs_ + [bpu], [self.hb_bb[0][hc], self.hb_bb[1][hc]])
        for s in range(3):
            ndc = 3 if s < 2 else 2
            d0 = s * 3
            wd, bwd = self.load_w(p["ffn_w_down"][li], d0 * 128, ndc * 128)
            if True:
                for h in HS:
                    bbb = self.hb_bb[h]
                    for j in range(ndc):
                        dc = d0 + j
                        pb, bpb = self.bank()
                        for hc in range(HC):
                            self.mm(pb[:, 0:256], wd[:, hc, j * 128:(j + 1) * 128], bb[:, hc, CS[h]], hc == 0, hc == HC - 1,
                                    [bwd, bbb[hc]], [bpb])
                        self.resid_add(h, dc, pb, bpb)
                    if s == 2 and self.pre_next:
                        self.norm_stats(h)
            else:
                for j in range(ndc):
                    dc = d0 + j
                    pb, bpb = self.bank()
                    for hc in range(HC):
                        self.mm(pb, wd[:, hc, j * 128:(j + 1) * 128], bb[:, hc, :], hc == 0, hc == HC - 1,
                                [bwd, self.hb_bb[0][hc], self.hb_bb[1][hc]], [bpb])
                    bx = [self.hb_xT[0][dc], self.hb_xT[1][dc]]
                    self.tt(self.xT[:, dc, :], self.xT[:, dc, :], pb, ALU.add, bx + [bpb], bx)

    def l1_mixer(self, ti):
        p = self.p
        bb, bf = self.bigbf, self.bigf
        HS = (0, 1)
        CS = [slice(0, 256), slice(256, 512)]
        c, bc = self.s5c, self.b_s5c
        both = lambda lst, k: [lst[0][k], lst[1][k]]
        for h in HS:
            self.rmsnorm(h, G_O)
        w_, bw = self.load_w(p["o_w_in"][0], 0, 512)
        for h in HS:
            bxn = self.hb_xn[h]
            for cc in range(4):
                pb, bpb = self.bank()
                for dc in range(8):
                    self.mm(pb[:, 0:256], w_[:, dc, cc * 128:(cc + 1) * 128], self.xn[:, dc, CS[h]], dc == 0, dc == 7,
                            [bw, bxn[dc]], [bpb])
                self.act(bf[:, cc, CS[h]], pb[:, 0:256], AF.Copy, [bpb], [self.hb_bf[h][cc]])
                self.cp(bb[:, cc, CS[h]], pb[:, 0:256], [bpb], [self.hb_bb[h][cc]])
        self.ps_rr = 0
        py = [(self.psum[:, i, :], self.b_ps[i]) for i in range(4)]
        pre = self.psum[:, 4:6, :].rearrange("p a b -> p (a b)"); bpre = [self.b_ps[4], self.b_ps[5]]
        pim = self.psum[:, 6:8, :].rearrange("p a b -> p (a b)"); bpim = [self.b_ps[6], self.b_ps[7]]
        u3 = [bb[:, cc, :].rearrange("p (n s) -> p n s", s=8) for cc in range(4)]
        for cc in range(4):
            sw, bsw = self.load_slab(self.s5_d[:, cc * 8192:(cc + 1) * 8192].rearrange("p (a b) -> p a b", b=128),
                                     64, 128, q="sp", R=[self.b_s5d])
            for jj in range(4):
                q = cc * 4 + jj
                for part in range(2):
                    o = (pre if part == 0 else pim)[:, q * 64:(q + 1) * 64]
                    bo = (bpre if part == 0 else bpim)[q // 8]
                    for s in range(8):
                        self.mm(o, sw[:, (s * 4 + jj) * 2 + part, :], u3[cc][:, :, s], s == 0, s == 7,
                                [bsw] + both(self.hb_bb, cc), [bo])
        cosf = self.c8T[:].rearrange("p q l -> p (q l)")
        sinf = self.s8T[:].rearrange("p q l -> p (q l)")
        rTf = self.r8T[:].rearrange("p q l -> p (q l)")
        tabR = [self.b_tab]
        t = [self.s5t[:, i, :] for i in range(6)]
        bt = self.hb_s5t
        xs_re = bb[:, 8:10, :].rearrange("p a b -> p (a b)"); bxr = both(self.hb_bb, 8) + both(self.hb_bb, 9)
        xs_im = bb[:, 10:12, :].rearrange("p a b -> p (a b)"); bxi = both(self.hb_bb, 10) + both(self.hb_bb, 11)
        ar = c[:, 0, :]; ai = c[:, 1, :]; xpr = c[:, 2, :]; xpi = c[:, 3, :]
        u1 = c[:, 4, :]; u2 = c[:, 5, :]
        self.tt(t[0], pre, cosf, ALU.mult, bpre + tabR, [bt[0]])
        self.tt(t[1], pim, sinf, ALU.mult, bpim + tabR, [bt[1]])
        self.tt(t[0], t[0], t[1], ALU.add, [bt[0], bt[1]], [bt[0]])
        self.tt(t[2], pim, cosf, ALU.mult, bpim + tabR, [bt[2]])
        self.tt(t[3], pre, sinf, ALU.mult, bpre + tabR, [bt[3]])
        self.tt(t[2], t[2], t[3], ALU.subtract, [bt[2], bt[3]], [bt[2]])
        if ti > 0:
            w_re0 = t[0].rearrange("p (q l) -> p q l", l=64)[:, :, 0:1]
            w_im0 = t[2].rearrange("p (q l) -> p q l", l=64)[:, :, 0:1]
            self.tt(u1, ar, xpr, ALU.mult, [bc[0], bc[2]], [bc[4]])
            self.tt(u2, ai, xpi, ALU.mult, [bc[1], bc[3]], [bc[5]])
            self.tt(u1, u1, u2, ALU.subtract, [bc[4], bc[5]], [bc[4]])
            self.tt(w_re0, w_re0, u1.unsqueeze(2), ALU.add, [bt[0], bc[4]], [bt[0]])
            self.tt(u1, ar, xpi, ALU.mult, [bc[0], bc[3]], [bc[4]])
            self.tt(u2, ai, xpr, ALU.mult, [bc[1], bc[2]], [bc[5]])
            self.tt(u1, u1, u2, ALU.add, [bc[4], bc[5]], [bc[4]])
            self.tt(w_im0, w_im0, u1.unsqueeze(2), ALU.add, [bt[2], bc[4]], [bt[2]])
        self.S.op("dve", lambda hh, o=t[1], a=rTf, b=t[0]: hh.tensor_tensor_scan(o, a, b, 0.0, ALU.mult, ALU.add),
                  reads=[bt[0]] + tabR, writes=[bt[1]])
        self.S.op("dve", lambda hh, o=t[3], a=rTf, b=t[2]: hh.tensor_tensor_scan(o, a, b, 0.0, ALU.mult, ALU.add),
                  reads=[bt[2]] + tabR, writes=[bt[3]])
        self.tt(t[0], t[1], cosf, ALU.mult, [bt[1]] + tabR, [bt[0]])
        self.tt(t[2], t[3], sinf, ALU.mult, [bt[3]] + tabR, [bt[2]])
        self.tt(t[4], t[0], t[2], ALU.subtract, [bt[0], bt[2]], [bt[4]])
        self.tt(t[0], t[1], sinf, ALU.mult, [bt[1]] + tabR, [bt[0]])
        self.tt(t[2], t[3], cosf, ALU.mult, [bt[3]] + tabR, [bt[2]])
        self.tt(t[5], t[0], t[2], ALU.add, [bt[0], bt[2]], [bt[5]])
        x_re3 = t[4].rearrange("p (q l) -> p q l", l=64)
        x_im3 = t[5].rearrange("p (q l) -> p q l", l=64)
        xs_re3 = xs_re.rearrange("p (q l) -> p q l", l=64)
        xs_im3 = xs_im.rearrange("p (q l) -> p q l", l=64)
        self.act(xs_re3[:, :, 0:1], xpr.unsqueeze(2), AF.Copy, [bc[2]], bxr)
        self.act(xs_im3[:, :, 0:1], xpi.unsqueeze(2), AF.Copy, [bc[3]], bxi)
        self.act(xs_re3[:, :, 1:64], x_re3[:, :, 0:63], AF.Copy, [bt[4]], bxr)
        self.act(xs_im3[:, :, 1:64], x_im3[:, :, 0:63], AF.Copy, [bt[5]], bxi)
        self.cp(xpr.unsqueeze(2), x_re3[:, :, 63:64], [bt[4]], [bc[2]])
        self.cp(xpi.unsqueeze(2), x_im3[:, :, 63:64], [bt[5]], [bc[3]])
        kt, bkt = self.ktap, self.b_ktap
        for cc in range(4):
            o3 = py[cc][0].rearrange("p (n l) -> p n l", l=8)
            bo = py[cc][1]
            Ru = [bkt] + both(self.hb_bb, cc)
            self.mm(py[cc][0], kt[:, cc * 8, :], bb[:, cc, :], True, False, Ru, [bo])
            for l in range(1, 8):
                for tau in range(1, l + 1):
                    self.mm(o3[:, :, l], kt[:, cc * 8 + tau, :], u3[cc][:, :, l - tau], False, False, Ru, [bo])
        for cc in range(4):
            sw, bsw = self.load_slab(self.s5_d[:, (288 + cc * 64) * 128:(288 + (cc + 1) * 64) * 128]
                                     .rearrange("p (a b) -> p a b", b=128), 64, 128, q="sp", R=[self.b_s5d])
            o3 = py[cc][0].rearrange("p (n l) -> p n l", l=8)
            bo = py[cc][1]
            for l in range(8):
                for jj in range(4):
                    q = cc * 4 + jj
                    for part in range(2):
                        xs = xs_re if part == 0 else xs_im
                        bx = bxr if part == 0 else bxi
                        self.mm(o3[:, :, l], sw[:, (l * 2 + part) * 4 + jj, :], xs[:, q * 64:(q + 1) * 64], False,
                                (l == 7 and jj == 3 and part == 1), [bsw] + bx, [bo])
        for h in HS:
            for cc in range(4):
                s_, bs_ = self.scr_next()
                self.stt(s_, bf[:, cc, CS[h]], self.gv[:, G_OD + cc:G_OD + cc + 1], py[cc][0][:, CS[h]], ALU.mult, ALU.add,
                         [self.hb_bf[h][cc], py[cc][1], self.b_const], [bs_])
                self.act(bb[:, 4 + cc, CS[h]], s_, AF.Gelu, [bs_], [self.hb_bb[h][4 + cc]])
        wo, bwo = self.load_slab(p["o_w_out"][0].rearrange("(c p) f -> p c f", p=128), 4, 2048)
        for h in HS:
            bbb = self.hb_bb[h]
            for dc in range(8):
                pa, bpa = self.bank()
                for kc in range(4):
                    self.mm(pa[:, 0:256], wo[:, kc, dc * 128:(dc + 1) * 128], bb[:, 4 + kc, CS[h]], kc == 0, kc == 3,
                            [bwo, bbb[4 + kc]], [bpa])
                pg, bpg = self.bank()
                for kc in range(4):
                    self.mm(pg[:, 0:256], wo[:, kc, 1024 + dc * 128:1024 + (dc + 1) * 128], bb[:, 4 + kc, CS[h]],
                            kc == 0, kc == 3, [bwo, bbb[4 + kc]], [bpg])
                s_, bs_ = self.scr_next()
                self.act(s_, pg[:, 0:256], AF.Sigmoid, [bpg], [bs_])
                self.tt(s_, pa[:, 0:256], s_, ALU.mult, [bpa, bs_], [bs_])
                x = self.xT[:, dc, CS[h]]
                self.tt(x, x, s_, ALU.add, [self.hb_xT[h][dc], bs_], [self.hb_xT[h][dc]])
            if self.pre_next:
                self.norm_stats(h)


ALL_PHASES = ("l0mix", "ca0", "ffn0", "l1mix", "ca1", "ffn1")


def build_nc(phases=ALL_PHASES, ntiles=SEQ // T, final_norm=True, dbg=False):
    nc = bass.Bass("TRN2", target_bir_lowering=False)
    kb = KB(nc, phases, ntiles, final_norm)
    kb.dbg = dbg
    kb.build()
    return nc


def make_in_maps(inputs):
    ident = np.eye(128, dtype=np.float32)
    maps = []
    for b in range(8):
        m = {"x": np.ascontiguousarray(inputs["x"][b]), "mem": np.ascontiguousarray(inputs["mem"][b]),
             "c_ident": ident}
        for k in PARAM_SHAPES:
            m[k] = np.ascontiguousarray(np.asarray(inputs[k], dtype=np.float32))
        maps.append(m)
    return maps


def kernel(**inputs):
    inputs = {k: np.asarray(v) for k, v in inputs.items()}
    nc = build_nc()
    res = run_bass_kernel_spmd(nc, make_in_maps(inputs), core_ids=list(range(8)))
    return np.stack([res.results[b]["out"] for b in range(8)], axis=0).astype(np.float32)
```

```python
import contextlib
import numpy as np
import concourse.bass as bass
import concourse.mybir as mybir
from concourse.bass_utils import run_bass_kernel_spmd

F32 = mybir.dt.float32
BF16 = mybir.dt.bfloat16
AF = mybir.ActivationFunctionType
ALU = mybir.AluOpType

D = 1024
SEQ = 4096
T = 512
DC = 8
H = 2816
HC = 22
MEM = 256
LS = 64
SLOT = 8448
NSLOT = 3
EPS = 1e-6
PI = float(np.pi)

ENGS = ("pe", "act", "dve", "pool", "sp")
NDMA_SEM = 12

G_E, G_O, G_CA, G_FFN, G_MEM, G_CB, G_LG, G_LB, G_OD, G_N = 0, 8, 16, 32, 48, 64, 68, 72, 76, 80


class Buf:
    __slots__ = ("name", "w", "r", "excl")

    def __init__(self, name="", excl=False):
        self.name = name
        self.w = None
        self.r = {}
        self.excl = excl


class Sched:
    def __init__(self, nc):
        self.nc = nc
        self.streams = {e: [] for e in ENGS}
        self.cnt = {e: 0 for e in ENGS}
        self.seen = {e: {} for e in ENGS}
        self.dma_cnt = {}
        self.dma_rr = {e: 0 for e in ENGS}

    def _need(self, eng, tok, out):
        if tok is None:
            return
        key, val, src = tok
        if self.seen[eng].get(key, 0) >= val:
            return
        out[key] = max(out.get(key, 0), val)

    def _deps(self, eng, reads, writes, same_eng_raw=True):
        need = {}
        for b in reads:
            if b.w is not None:
                if b.w[2] == eng and not same_eng_raw:
                    continue
                self._need(eng, b.w, need)
            if b.excl:
                for key, (val, src) in b.r.items():
                    if src != eng:
                        self._need(eng, (key, val, src), need)
        for b in writes:
            if b.w is not None and (b.w[2] != eng or eng != "pe"):
                self._need(eng, b.w, need)
            for key, (val, src) in b.r.items():
                if src != eng or eng != "pe":
                    self._need(eng, (key, val, src), need)
        for key, val in need.items():
            self.streams[eng].append(("wait", key, val))
            self.seen[eng][key] = val

    def _commit(self, tok, reads, writes):
        key, val, src = tok
        for b in reads:
            b.r[key] = (val, src)
        for b in writes:
            b.w = tok
            b.r = {}

    def op(self, eng, fn, reads=(), writes=()):
        self._deps(eng, reads, writes, same_eng_raw=(eng != "pe"))
        self.cnt[eng] += 1
        tok = (("e", eng), self.cnt[eng], eng)
        self.streams[eng].append(("op", fn))
        self._commit(tok, reads, writes)

    def dma(self, q, fn, reads=(), writes=()):
        i = self.dma_rr[q]
        self.dma_rr[q] = (i + 1) % NDMA_SEM
        key = ("d", q, i)
        prev = self.dma_cnt.get(key, 0)
        self._deps(q, reads, writes)
        if prev > 0 and self.seen[q].get(key, 0) < prev:
            self.streams[q].append(("wait", key, prev))
            self.seen[q][key] = prev
        val = prev + 16
        self.dma_cnt[key] = val
        self.streams[q].append(("dma", fn, key))
        self._commit((key, val, "dma"), reads, writes)

    def barrier(self):
        keys = {("e", e): self.cnt[e] for e in ENGS if self.cnt[e] > 0}
        keys.update(self.dma_cnt)
        for eng in ENGS:
            for key, val in keys.items():
                if key == ("e", "pe") and eng == "pe":
                    continue
                if self.seen[eng].get(key, 0) < val:
                    self.streams[eng].append(("wait", key, val))
                    self.seen[eng][key] = val

    def final_wait(self, eng, bufs):
        need = {}
        for b in bufs:
            self._need(eng, b.w, need)
        for key, val in need.items():
            self.streams[eng].append(("wait", key, val))
            self.seen[eng][key] = val

    def emit(self):
        nc = self.nc
        with contextlib.ExitStack() as st:
            sems = {}
            for e in ENGS:
                sems[("e", e)] = st.enter_context(nc.semaphore("s_" + e))
            for key in self.dma_cnt:
                sems[key] = st.enter_context(nc.semaphore("d_%s%d" % (key[1], key[2])))
            block = st.enter_context(nc.Block())

            def run(stream, ename):
                def body(h):
                    for item in stream:
                        if item[0] == "wait":
                            h.wait_ge(sems[item[1]], item[2])
                        elif item[0] == "op":
                            item[1](h).then_inc(sems[("e", ename)], 1)
                        else:
                            item[1](h).then_inc(sems[item[2]], 16)
                return body

            block.tensor(run(self.streams["pe"], "pe"))
            block.scalar(run(self.streams["act"], "act"))
            block.vector(run(self.streams["dve"], "dve"))
            block.gpsimd(run(self.streams["pool"], "pool"))
            block.sync(run(self.streams["sp"], "sp"))


PARAM_SHAPES = {
    "e_norm": [1, 1024], "e_w_in": [1, 1024, 2048], "e_gmlp_w": [1, 4, 128, 128], "e_gmlp_b": [1, 4, 128],
    "e_conv_w": [1, 31, 512], "e_conv_b": [1, 512], "e_conv_ln_g": [1, 512], "e_conv_ln_b": [1, 512],
    "e_w_out": [1, 1024, 1024], "o_norm": [1, 1024], "o_w_in": [1, 1024, 512], "o_lam_re": [1, 32, 64],
    "o_lam_im": [1, 32, 64], "o_log_dt": [1, 32], "o_b_re": [1, 32, 64, 16], "o_b_im": [1, 32, 64, 16],
    "o_c_re": [1, 32, 16, 64], "o_c_im": [1, 32, 16, 64], "o_d": [1, 512], "o_w_out": [1, 512, 2048],
    "ca_norm": [2, 1024], "ca_mem_norm": [2, 1024], "ca_wq": [2, 1024, 1024], "ca_wk": [2, 1024, 1024],
    "ca_wv": [2, 1024, 1024], "ca_wo": [2, 1024, 1024], "ffn_norm": [2, 1024], "ffn_w_gate": [2, 1024, 2816],
    "ffn_w_up": [2, 1024, 2816], "ffn_w_down": [2, 2816, 1024], "final_norm": [1024],
}


class KB:
    def __init__(self, nc, phases, ntiles, final_norm=True):
        self.nc = nc
        self.S = Sched(nc)
        self.phases = phases
        self.ntiles = ntiles
        self.final_norm = final_norm
        self.ps_rr = 0
        self.ring_rr = 0
        self.scr_rr = 0
        self.sq_rr = 0
        self.dbg_bufs = []
        self.deng = "dve"
        self.dbg = False

    def mm(self, out, lhsT, rhs, start, stop, R, W):
        self.S.op("pe", lambda h: h.matmul(out, lhsT, rhs, start=start, stop=stop), reads=R, writes=W)

    def tr(self, out, in_, R, W, ident=None):
        ident = self.ident if ident is None else ident
        self.S.op("pe", lambda h: h.transpose(out, in_, ident[:]), reads=list(R) + [self.b_const], writes=W)

    def act(self, out, in_, func, R, W, bias=None, scale=None, accum=None):
        kw = {}
        if bias is not None:
            kw["bias"] = bias
        if scale is not None:
            kw["scale"] = scale
        if accum is not None:
            kw["accum_out"] = accum
        self.S.op("act", lambda h: h.activation(out, in_, func, **kw), reads=R, writes=W)

    def tt(self, out, a, b, op, R, W, eng=None):
        eng = eng or self.deng
        self.S.op(eng, lambda h: h.tensor_tensor(out, a, b, op), reads=R, writes=W)

    def ts(self, out, a, s1, s2, op0, op1, R, W, eng=None):
        eng = eng or self.deng
        if op1 is None:
            self.S.op(eng, lambda h: h.tensor_scalar(out, a, s1, s2, op0), reads=R, writes=W)
        else:
            self.S.op(eng, lambda h: h.tensor_scalar(out, a, s1, s2, op0, op1), reads=R, writes=W)

    def stt(self, out, in0, scalar, in1, op0, op1, R, W, eng=None):
        eng = eng or self.deng
        self.S.op(eng, lambda h: h.scalar_tensor_tensor(out, in0, scalar, in1, op0, op1), reads=R, writes=W)

    def cp(self, out, in_, R, W, eng=None):
        eng = eng or self.deng
        self.S.op(eng, lambda h: h.tensor_copy(out, in_), reads=R, writes=W)

    def recip(self, out, in_, R, W):
        self.S.op("dve", lambda h: h.reciprocal(out, in_), reads=R, writes=W)

    def memset(self, ap, val, W, eng=None):
        eng = eng or self.deng
        self.S.op(eng, lambda h: h.memset(ap, val), writes=W)

    def dma(self, q, out, in_, R, W, slow=False):
        if slow:
            self.S.dma(q, lambda h: h.dma_start(out=out, in_=in_, allow_slow_non_contiguous=True), reads=R, writes=W)
        else:
            self.S.dma(q, lambda h: h.dma_start(out=out, in_=in_), reads=R, writes=W)

    def dump(self, name, ap, bufs, dt=F32):
        d = self.nc.dram_tensor("dbg_" + name, list(ap.shape), dt, kind="ExternalOutput").ap()
        b = Buf()
        self.dma("sp", d, ap, list(bufs), [b])
        self.dbg_bufs.append(b)

    def bank(self):
        i = self.ps_rr
        self.ps_rr = (i + 1) % 8
        return self.psum[:, i, :], self.b_ps[i]

    def scr_next(self):
        i = self.scr_rr
        self.scr_rr = (i + 1) % 4
        return self.scr[:, i, :], self.b_scr[i]

    def load_slab(self, src, a, b, q="pool", R=()):
        i = self.ring_rr
        self.ring_rr = (i + 1) % NSLOT
        dst = self.ring[i][:, 0:a * b].rearrange("p (a b) -> p a b", b=b)
        self.dma(q, dst, src, list(R), [self.b_ring[i]])
        return dst, self.b_ring[i]

    def load_w(self, W2d, c0, n):
        kc = W2d.shape[0] // 128
        src = W2d[:, c0:c0 + n].rearrange("(c p) f -> p c f", p=128)
        return self.load_slab(src, kc, n)

    def build(self):
        nc = self.nc
        dr = lambda name, shape, kind="ExternalInput", dt=F32: nc.dram_tensor(name, shape, dt, kind=kind).ap()
        self.x_d = dr("x", [SEQ, D])
        self.mem_d = dr("mem", [MEM, D])
        self.p = {k: dr(k, s) for k, s in PARAM_SHAPES.items()}
        self.ident_d = dr("c_ident", [128, 128])
        self.out_d = dr("out", [SEQ, D], kind="ExternalOutput")
        self.diag_d = dr("scr_diag", [2, 128, 62 * 128], kind="Internal", dt=BF16)
        self.s5_d = dr("scr_s5", [128, 544 * 128], kind="Internal", dt=BF16)
        with contextlib.ExitStack() as st:
            sb = lambda n, s, d: st.enter_context(nc.sbuf_tensor(n, s, d))
            self.xT = sb("xT", [128, 8, 512], F32); self.b_xT = [Buf() for _ in range(8)]
            self.xn = sb("xn", [128, 8, 512], BF16); self.b_xn = [Buf() for _ in range(8)]
            self.sq = sb("sq", [128, 4, 512], BF16); self.b_sq = [Buf() for _ in range(4)]
            self.rstd = sb("rstd", [128, 512], F32); self.b_rstd = Buf()
            self.ring = [sb("ring%d" % i, [128, SLOT], BF16) for i in range(NSLOT)]
            self.b_ring = [Buf() for _ in range(NSLOT)]
            self.bigbf = sb("bigbf", [128, 22, 512], BF16); self.b_bb = [Buf() for _ in range(22)]
            self.bigf = sb("bigf", [128, 8, 512], F32); self.b_bf = [Buf() for _ in range(8)]
            self.hbuf = sb("hbuf", [128, 4, 542], BF16); self.b_hb = [Buf() for _ in range(4)]
            self.scr = sb("scr", [128, 4, 512], F32); self.b_scr = [Buf() for _ in range(4)]
            self.aux = sb("aux", [128, 2, 512], F32); self.b_aux = [Buf() for _ in range(2)]
            self.KT = sb("KT", [128, 2, 8, 256], BF16); self.b_KT = Buf()
            self.V = sb("V", [128, 2, 2, 1024], BF16); self.b_V = Buf()
            self.gw = sb("gw", [128, 4, 128], BF16)
            self.bsbc = sb("bsbc", [128, 4, 128], F32)
            self.ident = sb("ident", [128, 128], F32)
            self.ident_bf = sb("ident_bf", [128, 128], BF16)
            self.ones_bf = sb("ones_bf", [128, 128], BF16)
            self.ones_f = sb("ones_f", [128, 128], F32)
            self.epsb = sb("epsb", [128, 1], F32)
            self.gv = sb("gv", [128, G_N], F32)
            self.fg = sb("fg", [128, 1024], F32)
            self.small = sb("small", [128, 64], F32); self.b_small = [Buf() for _ in range(8)]
            self.b_const = Buf()
            self.cosT = sb("cosT", [128, 16, 16], F32)
            self.sinT = sb("sinT", [128, 16, 16], F32)
            self.c8T = sb("c8T", [128, 16, 64], F32)
            self.s8T = sb("s8T", [128, 16, 64], F32)
            self.r8T = sb("r8T", [128, 16, 64], F32)
            self.apw = sb("apw", [128, 3, 9, 16], F32); self.b_apw = Buf()
            self.ktap = sb("ktap", [128, 32, 128], BF16); self.b_ktap = Buf()
            self.s5c = sb("s5c", [128, 8, 16], F32)
            self.b_s5c = [Buf() for _ in range(8)]
            self.b_tab = Buf()
            self.s5t = sb("s5t", [128, 6, 1024], F32); self.b_s5t = [Buf() for _ in range(6)]
            self.psum = st.enter_context(nc.psum_tensor("ps", [128, 8, 512], F32))
            self.b_ps = [Buf(excl=True) for _ in range(8)]
            self.b_out = Buf()
            self.b_diag = Buf()
            self.b_s5d = Buf()

            self.prologue()
            self.S.barrier()
            two = lambda n: [[Buf() for _ in range(n)] for _ in range(2)]
            self.hb_xT = two(8); self.hb_xn = two(8); self.hb_bb = two(22); self.hb_bf = two(8)
            self.hb_aux = two(2); self.hb_rstd = [Buf(), Buf()]
            self.hb_h = [[Buf(), Buf(), Buf()] for _ in range(4)]
            self.hb_scr = [Buf() for _ in range(8)]
            self.hb_sq = [Buf() for _ in range(8)]
            self.hb_small = two(8)
            self.hb_s5t = [Buf() for _ in range(6)]
            self.scr_rr = 0
            self.sq_rr = 0
            self.ps_rr = 0
            for ti in range(self.ntiles):
                self.tile(ti)
            self.S.final_wait("sp", [self.b_out] + self.dbg_bufs)
            self.S.emit()
        return nc

    def prologue(self):
        p = self.p
        cW = [self.b_const]
        self.dma("sp", self.ident[:], self.ident_d, [], cW)
        if "l0mix" in self.phases:
            self.pro_gc_loads()
        self.memset(self.ones_bf[:], 1.0, cW)
        self.memset(self.ones_f[:], 1.0, cW)
        self.memset(self.epsb[:], EPS, cW)
        for fc in range(4):
            self.memset(self.hbuf[:, fc, 0:30], 0.0, [self.b_hb[fc]])
        gv = self.gv
        G = self.s5t[:, 5, 0:128]
        bG = [self.b_s5t[5]]
        self.memset(G, 0.0, bG)

        def ld_gain(row, src2d):
            n = src2d.shape[0]
            self.dma("sp", G[row:row + n, :], src2d, [], bG)

        r8 = lambda v: v.rearrange("(c p) -> c p", p=128)
        ld_gain(G_E, r8(p["e_norm"][0]))
        ld_gain(G_O, r8(p["o_norm"][0]))
        ld_gain(G_CA, p["ca_norm"].rearrange("l (c p) -> (l c) p", p=128))
        ld_gain(G_FFN, p["ffn_norm"].rearrange("l (c p) -> (l c) p", p=128))
        ld_gain(G_MEM, p["ca_mem_norm"].rearrange("l (c p) -> (l c) p", p=128))
        ld_gain(G_CB, r8(p["e_conv_b"][0]))
        ld_gain(G_LG, r8(p["e_conv_ln_g"][0]))
        ld_gain(G_LB, r8(p["e_conv_ln_b"][0]))
        ld_gain(G_OD, r8(p["o_d"][0]))
        pb, bpb = self.bank()
        self.tr(pb[:, 0:128], G, bG, [bpb])
        self.cp(gv[:, 0:G_N], pb[:, 0:G_N], [bpb], cW)
        need_mem = "ca0" in self.phases or "ca1" in self.phases
        if need_mem:
            self.pro_mem_loads()
        if "l1mix" in self.phases:
            self.pro_s5_loads()
        self.dma("sp", self.fg[:], p["final_norm"].partition_broadcast(128), [], cW)
        self.dma("sp", self.bsbc[:].rearrange("p g i -> p (g i)"),
                 p["e_gmlp_b"][0].rearrange("g i -> (g i)").partition_broadcast(128), [], cW)
        if "l0mix" in self.phases:
            self.pro_gmlp_conv()
        g5 = self.pro_s5() if "l1mix" in self.phases else iter(())
        next(g5, None)
        if need_mem:
            self.pro_mem()
        next(g5, None)

    def pro_gc_loads(self):
        p = self.p
        cw = self.s5t[:, 1, 0:512]
        self.memset(cw, 0.0, [self.b_s5t[1]])
        self.dma("sp", cw[0:31, :], p["e_conv_w"][0], [], [self.b_s5t[1]])
        wtmp = self.s5t[:, 0, 0:512].rearrange("p (g j) -> p g j", g=4)
        self.dma("sp", wtmp, p["e_gmlp_w"][0].rearrange("g i j -> i g j"), [], [self.b_s5t[0]])

    def pro_gmlp_conv(self):
        p = self.p
        cW = [self.b_const]
        wtmp = self.s5t[:, 0, 0:512].rearrange("p (g j) -> p g j", g=4)
        for g in range(4):
            pb, bpb = self.bank()
            self.tr(pb[:, 0:128], wtmp[:, g, :], [self.b_s5t[0]], [bpb])
            self.cp(self.gw[:, g, :], pb[:, 0:128], [bpb], cW)
            self.memset(self.gw[64:128, g, 0:64], 0.0, cW)
        cw = self.s5t[:, 1, 0:512]
        cwT = self.aux[:, 0, 0:128].rearrange("p (f k) -> p f k", k=32)
        for fc in range(4):
            pb, bpb = self.bank()
            self.tr(pb[:, 0:128], cw[:, fc * 128:(fc + 1) * 128], [self.b_s5t[1]], [bpb])
            self.cp(cwT[:, fc, 0:31], pb[:, 0:31], [bpb], [self.b_aux[0]])
        stage = self.bigbf[:].rearrange("p a b -> p (a b)")
        bdgb = [Buf() for _ in range(62)]
        for half in range(2):
            for i in range(62):
                fc = half * 2 + i // 31
                k = i % 31
                if i % 2 == 0:
                    self.ts(stage[:, i * 128:(i + 1) * 128], self.ident[:], cwT[:, fc, k:k + 1], None, ALU.mult, None,
                            [self.b_aux[0], self.b_const], [bdgb[i]])
                else:
                    self.act(stage[:, i * 128:(i + 1) * 128], self.ident[:], AF.Identity, [self.b_aux[0], self.b_const],
                             [bdgb[i]], scale=cwT[:, fc, k:k + 1])
            self.dma("sp", self.diag_d[half], stage[:, 0:62 * 128], bdgb + self.b_bb[0:16], [self.b_diag])

    def rms_generic(self, src_fn, bsrc, n, gcol, dst_fn, bdst, rstd=None, brstd=None, sqbufs=None):
        if rstd is None:
            rstd = self.rstd[:, 0:n]; brstd = self.b_rstd
        if sqbufs is None:
            sqbufs = self.b_sq
        sqf = self.sq[:].rearrange("p a b -> p (a b)")
        nsq = len(sqbufs)
        w = 2048 // nsq
        pb, bpb = self.bank()
        for dc in range(8):
            k = self.sq_rr % nsq
            self.sq_rr = k + 1
            sq = sqf[:, k * w:k * w + n]
            self.act(sq, src_fn(dc), AF.Square, [bsrc[dc]], [sqbufs[k]])
            self.mm(pb[:, 0:n], self.ones_bf[:], sq, dc == 0, dc == 7, [sqbufs[k], self.b_const], [bpb])
        self.act(rstd, pb[:, 0:n], AF.Sqrt, [bpb, self.b_const], [brstd], bias=self.epsb[:], scale=1.0 / D)
        self.recip(rstd, rstd, [brstd], [brstd])
        for dc in range(8):
            self.stt(dst_fn(dc), src_fn(dc), self.gv[:, gcol + dc:gcol + dc + 1], rstd,
                     ALU.mult, ALU.mult, [bsrc[dc], brstd, self.b_const], [bdst[dc]])

    def pro_mem_loads(self):
        p = self.p
        memtok = self.bigf[:, 0:4, :].rearrange("p a b -> p (a b)").rearrange("p (m d) -> p m d", m=2)
        self.dma("sp", memtok, self.mem_d.rearrange("(m p) d -> p m d", p=128), [], self.b_bf[0:4])
        self.mem_w = [self.load_w(p["ca_wk"][0], 0, 1024), self.load_w(p["ca_wv"][0], 0, 1024),
                      self.load_w(p["ca_wk"][1], 0, 1024)]

    def pro_mem(self):
        p = self.p
        memtok = self.bigf[:, 0:4, :].rearrange("p a b -> p (a b)").rearrange("p (m d) -> p m d", m=2)
        memT = self.bigf[:, 4:8, :].rearrange("p a b -> p (a b)").rearrange("p (c m) -> p c m", c=8)
        bmT = [self.b_bf[4 + dc // 2] for dc in range(8)]
        for dc in range(8):
            pb, bpb = self.bank()
            for mb in range(2):
                self.tr(pb[:, mb * 128:(mb + 1) * 128], memtok[:, mb, dc * 128:(dc + 1) * 128], self.b_bf[0:4], [bpb])
            self.cp(memT[:, dc, :], pb[:, 0:256], [bpb], [bmT[dc]])
        for li in range(2):
            self.rms_generic(lambda dc: memT[:, dc, :], bmT, 256, G_MEM + 8 * li,
                             lambda dc: self.xn[:, dc, 0:256], self.b_xn)
            wk, bwk = self.mem_w[2 * li]
            for oc in range(8):
                pb, bpb = self.bank()
                for dc in range(8):
                    self.mm(pb[:, 0:256], wk[:, dc, oc * 128:(oc + 1) * 128], self.xn[:, dc, 0:256], dc == 0, dc == 7,
                            [bwk, self.b_xn[dc]], [bpb])
                self.act(self.KT[:, li, oc, :], pb[:, 0:256], AF.Copy, [bpb], [self.b_KT])
            wv, bwv = self.mem_w[1] if li == 0 else self.load_w(p["ca_wv"][li], 0, 1024)
            for mc in range(2):
                for half in range(2):
                    pb, bpb = self.bank()
                    for dc in range(8):
                        self.mm(pb, self.xn[:, dc, mc * 128:(mc + 1) * 128], wv[:, dc, half * 512:(half + 1) * 512],
                                dc == 0, dc == 7, [bwv, self.b_xn[dc]], [bpb])
                    self.act(self.V[:, li, mc, half * 512:(half + 1) * 512], pb, AF.Copy, [bpb], [self.b_V])

    def rot_tables(self, cosT, sinT, n, g_re, g_im, tW):
        bg = self.b_aux[1]
        g2a = self.aux[:, 1, 32:48]; g2b = self.aux[:, 1, 48:64]; g2c = self.aux[:, 1, 64:80]
        ta = self.s5t[:, 4, :].rearrange("p (q l) -> p q l", q=16)
        tb = self.s5t[:, 5, :].rearrange("p (q l) -> p q l", q=16)
        W0 = [self.b_s5t[4]]; W1 = [self.b_s5t[5]]
        s = 1
        while s < n:
            if s >= 2:
                gb_re = g_re.unsqueeze(2).to_broadcast([128, 16, s])
                gb_im = g_im.unsqueeze(2).to_broadcast([128, 16, s])
                Rg = [self.b_tab, bg]
                self.tt(ta[:, :, 0:s], cosT[:, :, 0:s], gb_re, ALU.mult, Rg, W0)
                self.tt(tb[:, :, 0:s], sinT[:, :, 0:s], gb_im, ALU.mult, Rg, W1)
                self.tt(cosT[:, :, s:2 * s], ta[:, :, 0:s], tb[:, :, 0:s], ALU.subtract, W0 + W1, tW)
                self.tt(ta[:, :, 0:s], cosT[:, :, 0:s], gb_im, ALU.mult, Rg, W0)
                self.tt(tb[:, :, 0:s], sinT[:, :, 0:s], gb_re, ALU.mult, Rg, W1)
                self.tt(sinT[:, :, s:2 * s], ta[:, :, 0:s], tb[:, :, 0:s], ALU.add, W0 + W1, tW)
            if 2 * s < n:
                self.tt(g2a, g_re, g_re, ALU.mult, [bg], [bg])
                self.tt(g2b, g_im, g_im, ALU.mult, [bg], [bg])
                self.tt(g2c, g_re, g_im, ALU.mult, [bg], [bg])
                self.tt(g_re, g2a, g2b, ALU.subtract, [bg], [bg])
                self.ts(g_im, g2c, 2.0, None, ALU.mult, None, [bg], [bg])
            s *= 2

    def pro_s5_loads(self):
        p = self.p
        sm = self.small
        bsm = self.b_small
        L = self.s5t[:, 5, 128:256]
        bL = [self.b_s5t[5]]
        self.memset(L, 0.0, bL)
        for gi in range(2):
            self.dma("sp", L[0:16, 64 * gi:64 * gi + 64], p["o_lam_re"][0].rearrange("(q gi) p -> gi q p", gi=2)[gi], [], bL)
            self.dma("sp", L[16:32, 64 * gi:64 * gi + 64], p["o_lam_im"][0].rearrange("(q gi) p -> gi q p", gi=2)[gi], [], bL)
        self.dma("sp", sm[32:48, 48:50], p["o_log_dt"][0].rearrange("(q gi) -> q gi", gi=2), [], [bsm[3]])
        for gi in range(2):
            self.cp(L[32:48, 64 * gi:64 * gi + 64], sm[32:48, 48 + gi:49 + gi].to_broadcast([16, 64]), [bsm[3]], bL)
        pb, bpb = self.bank()
        self.tr(pb[:, 0:128], L, bL, [bpb])
        self.cp(sm[:, 0:48], pb[:, 0:48], [bpb], [bsm[0], bsm[1], bsm[2]])
        Zc = self.xT[:].rearrange("p a b -> p (a b)").rearrange("p (z c) -> p z c", c=128)
        self.memset(self.xT[:], 0.0, self.b_xT)
        for part in range(2):
            Cd = p["o_c_re"][0] if part == 0 else p["o_c_im"][0]
            Cv = Cd.rearrange("(cc r) c p -> r c cc p", r=8)
            for j in range(4):
                for gi in range(2):
                    self.dma("act", Zc[32 * j + 16 * gi:32 * j + 16 * gi + 16, part * 16 + j:part * 16 + 16:4,
                                       64 * gi:64 * gi + 64],
                             Cv[2 * j + gi], [], self.b_xT[part * 4:part * 4 + 4])

    def pro_s5(self):
        self.deng = "pool"
        p = self.p
        c = self.s5c
        bc = self.b_s5c
        sm = self.small
        bsm = self.b_small
        lr = sm[:, 0:16]; li_ = sm[:, 16:32]; ldt = sm[:, 32:48]; tmp = sm[:, 48:64]
        R3 = [bsm[0], bsm[1], bsm[2]]
        dt = c[:, 4, :]; ang = c[:, 5, :]; r = c[:, 6, :]; t7 = c[:, 7, :]
        self.act(dt, ldt, AF.Exp, R3, [bc[4]])
        self.tt(ang, li_, dt, ALU.mult, R3 + [bc[4]], [bc[5]])
        self.tt(t7, lr, dt, ALU.mult, R3 + [bc[4]], [bc[7]])
        self.act(r, t7, AF.Exp, [bc[7]], [bc[6]])
        s1 = c[:, 3, :]; c1 = c[:, 2, :]
        for which in range(2):
            m = t7 if which == 0 else tmp
            bm = bc[7] if which == 0 else bsm[3]
            shift = 0.0 if which == 0 else PI / 2
            base = self.aux[:, 0, 0:16]
            self.ts(base, ang, shift, None, ALU.add, None, [bc[5]], [self.b_aux[0]])
            self.cp(m, base, [self.b_aux[0]], [bm])
            for thr in (1, 3, 5, 7):
                cm = self.aux[:, 0, 16:32]
                self.ts(cm, base, thr * PI, -2 * PI, ALU.is_gt, ALU.mult, [self.b_aux[0]], [self.b_aux[0]])
                self.tt(m, m, cm, ALU.add, [bm, self.b_aux[0]], [bm])
            self.act(s1 if which == 0 else c1, m, AF.Sin, [bm], [bc[3] if which == 0 else bc[2]])
        ar = c[:, 0, :]; ai = c[:, 1, :]
        cosT, sinT = self.cosT, self.sinT
        tW = [self.b_tab]
        self.memset(cosT[:, :, 0:1], 1.0, tW)
        self.memset(sinT[:, :, 0:1], 0.0, tW)
        self.cp(cosT[:, :, 1:2], c1.unsqueeze(2), [bc[2]], tW)
        self.cp(sinT[:, :, 1:2], s1.unsqueeze(2), [bc[3]], tW)
        self.tt(ar, r, c1, ALU.mult, [bc[6], bc[2]], [bc[0]])
        self.tt(ai, r, s1, ALU.mult, [bc[6], bc[3]], [bc[1]])
        gre = self.aux[:, 1, 0:16]; gim = self.aux[:, 1, 16:32]
        bg = self.b_aux[1]
        self.cp(gre, c1, [bc[2]], [bg])
        self.cp(gim, s1, [bc[3]], [bg])
        self.rot_tables(cosT, sinT, 16, gre, gim, tW)
        den = self.aux[:, 0, 32:48]; am1 = self.aux[:, 0, 48:64]
        qr = self.aux[:, 0, 64:80]; qi = self.aux[:, 0, 80:96]; u1 = self.aux[:, 0, 96:112]; u2 = self.aux[:, 0, 112:128]
        A0 = [self.b_aux[0]]
        Rall = R3 + [bc[0], bc[1]] + A0
        self.tt(den, lr, lr, ALU.mult, Rall, A0)
        self.tt(u1, li_, li_, ALU.mult, Rall, A0)
        self.tt(den, den, u1, ALU.add, A0, A0)
        self.recip(den, den, A0, A0)
        self.ts(am1, ar, -1.0, None, ALU.add, None, Rall, A0)
        self.tt(u1, am1, lr, ALU.mult, Rall, A0)
        self.tt(u2, ai, li_, ALU.mult, Rall, A0)
        self.tt(u1, u1, u2, ALU.add, A0, A0)
        self.tt(qr, u1, den, ALU.mult, A0, A0)
        self.tt(u1, ai, lr, ALU.mult, Rall, A0)
        self.tt(u2, am1, li_, ALU.mult, Rall, A0)
        self.tt(u1, u1, u2, ALU.subtract, A0, A0)
        self.tt(qi, u1, den, ALU.mult, A0, A0)
        apw = self.apw
        bap = self.b_apw
        self.memset(apw[:, 0, 0, :], 1.0, [bap])
        self.memset(apw[:, 1, 0, :], 0.0, [bap])
        self.memset(apw[:, 2, 0, :], 0.0, [bap])
        rp = t7
        self.cp(rp, r, [bc[6]], [bc[7]])
        for tau in range(1, 9):
            self.tt(apw[:, 0, tau, :], rp, cosT[:, :, tau], ALU.mult, [bc[7], self.b_tab], [bap])
            self.tt(apw[:, 1, tau, :], rp, sinT[:, :, tau], ALU.mult, [bc[7], self.b_tab], [bap])
            self.ts(apw[:, 2, tau, :], apw[:, 1, tau, :], -1.0, None, ALU.mult, None, [bap], [bap])
            if tau < 8:
                self.tt(rp, rp, r, ALU.mult, [bc[7], bc[6]], [bc[7]])
        c8T, s8T, r8T = self.c8T, self.s8T, self.r8T
        self.memset(c8T[:, :, 0:1], 1.0, tW)
        self.memset(s8T[:, :, 0:1], 0.0, tW)
        self.cp(c8T[:, :, 1:2], cosT[:, :, 8:9], tW, tW)
        self.cp(s8T[:, :, 1:2], sinT[:, :, 8:9], tW, tW)
        self.cp(gre, cosT[:, :, 8], tW, [bg])
        self.cp(gim, sinT[:, :, 8], tW, [bg])
        self.rot_tables(c8T, s8T, 64, gre, gim, tW)
        self.cp(r8T[:], rp.unsqueeze(2).to_broadcast([128, 16, 64]), [bc[7]], tW)
        self.memset(r8T[:, :, 0:1], 0.0, tW)
        def v3(ap):
            return ap.rearrange("p (q c) -> p q c", c=16)
        Bre = v3(self.s5t[:, 4, 0:256]); Bim = v3(self.s5t[:, 4, 256:512])
        Bbr = v3(self.s5t[:, 4, 512:768]); Bbi = v3(self.s5t[:, 4, 768:1024])
        t1 = v3(self.s5t[:, 5, 0:256]); t2 = v3(self.s5t[:, 5, 256:512])
        ABr = v3(self.s5t[:, 5, 512:768]); ABi = v3(self.s5t[:, 5, 768:1024])
        B4 = [self.b_s5t[4]]; B5 = [self.b_s5t[5]]
        self.dma("sp", Bre, p["o_b_re"][0].rearrange("(q gi) p c -> (gi p) q c", gi=2), [], B4)
        self.dma("sp", Bim, p["o_b_im"][0].rearrange("(q gi) p c -> (gi p) q c", gi=2), [], B4)
        qrb = qr.unsqueeze(2).to_broadcast([128, 16, 16]); qib = qi.unsqueeze(2).to_broadcast([128, 16, 16])
        self.tt(t1, Bre, qrb, ALU.mult, B4 + A0, B5)
        self.tt(t2, Bim, qib, ALU.mult, B4 + A0, B5)
        self.tt(Bbr, t1, t2, ALU.subtract, B5, B4)
        self.tt(t1, Bim, qrb, ALU.mult, B4 + A0, B5)
        self.tt(t2, Bre, qib, ALU.mult, B4 + A0, B5)
        self.tt(Bbi, t1, t2, ALU.add, B5, B4)
        self.cp(ar, apw[:, 0, 8, :], [bap], [bc[0]])
        self.cp(ai, apw[:, 1, 8, :], [bap], [bc[1]])
        self.memset(c[:, 2, :], 0.0, [bc[2]])
        self.memset(c[:, 3, :], 0.0, [bc[3]])
        self.deng = "dve"
        yield
        CTf = self.bigf[:].rearrange("p a b -> p (a b)").rearrange("p (t q c) -> p t q c", t=2, q=16)
        Zc = self.xT[:].rearrange("p a b -> p (a b)").rearrange("p (z c) -> p z c", c=128)
        for part in range(2):
            for q4 in range(4):
                pb, bpb = self.bank()
                for k in range(4):
                    q = q4 * 4 + k
                    z = part * 16 + q
                    self.tr(pb[:, k * 128:(k + 1) * 128], Zc[:, z, :], [self.b_xT[z // 4]], [bpb])
                self.act(CTf[:, part, q4 * 4:q4 * 4 + 4, :], pb.rearrange("p (k c) -> p k c", c=128), AF.Copy, [bpb],
                         [self.b_bf[part * 4 + q4]], scale=(1.0 if part == 0 else -1.0))
        CTb = self.s5t[:, 2:4, :].rearrange("p a b -> p (a b)").bitcast(BF16).rearrange("p (t q c) -> p t q c", t=2, q=16)
        bCTb = self.b_s5t[2:4]
        self.cp(CTb, CTf, self.b_bf, bCTb)
        self.cp(self.ident_bf[:], self.ident[:], [self.b_const], [self.b_const])
        stg_flat = self.bigbf[:].rearrange("p a b -> p (a b)")
        stg = [stg_flat[:, 0:4096], stg_flat[:, 4096:8192]]
        bstg = [self.b_bb[0:8], self.b_bb[8:16]]
        Zt = self.s5t[:, 0:2, :].rearrange("p a b -> p (a b)").bitcast(BF16).rearrange("p (t q c) -> p t q c", t=2, q=16)
        bZ = self.b_s5t[0:2]
        W1v = self.s5_d[:, 0:256 * 128].rearrange("p (cc s x) -> p cc s x", cc=4, s=8)
        si = 0
        ktf = self.ktap[:].rearrange("p a b -> p (a b)")
        for tau in range(8):
            s = 7 - tau
            apr_b = apw[:, 0, tau, :].unsqueeze(2).to_broadcast([128, 16, 16])
            api_b = apw[:, 1, tau, :].unsqueeze(2).to_broadcast([128, 16, 16])
            self.tt(t1, Bbr, apr_b, ALU.mult, B4 + [bap], B5)
            self.tt(t2, Bbi, api_b, ALU.mult, B4 + [bap], B5)
            self.tt(ABr, t1, t2, ALU.subtract, B5, B5)
            self.tt(t1, Bbr, api_b, ALU.mult, B4 + [bap], B5)
            self.tt(t2, Bbi, apr_b, ALU.mult, B4 + [bap], B5)
            self.tt(ABi, t1, t2, ALU.add, B5, B5)
            if tau == 0:
                bZj = [[Buf() for _ in range(4)] for _ in range(2)]
                self.memset(Zt, 0.0, bZ + bZj[0] + bZj[1])
            n = 0
            for part in range(2):
                AB = ABr if part == 0 else ABi
                for gi in range(2):
                    for j in range(4):
                        dst = Zt[64 * gi:64 * gi + 64, part, j::4, 32 * j + 16 * gi:32 * j + 16 * gi + 16]
                        src = AB[64 * gi:64 * gi + 64, j::4, :]
                        if n % 2 == 0:
                            self.cp(dst, src, B5, [bZj[part][j]])
                        else:
                            self.act(dst, src, AF.Copy, B5, [bZj[part][j]])
                        n += 1
            sl = stg[si % 2]; bsl = bstg[si % 2]; si += 1
            for cc in range(4):
                for hb in range(2):
                    pb, bpb = self.bank()
                    pbb = pb.bitcast(BF16)
                    for k in range(4):
                        jj = hb * 2 + k // 2
                        part = k % 2
                        q = cc * 4 + jj
                        self.tr(pbb[:, k * 128:(k + 1) * 128], Zt[:, part, q, :], [bZj[part][q % 4]], [bpb], ident=self.ident_bf)
                    o0 = (cc * 8 + hb * 4) * 128
                    if hb == 0:
                        self.act(sl[:, o0:o0 + 512], pbb[:, 0:512], AF.Copy, [bpb], [bsl[cc * 2 + hb]])
                    else:
                        self.cp(sl[:, o0:o0 + 512], pbb[:, 0:512], [bpb], [bsl[cc * 2 + hb]])
            self.dma("sp", W1v[:, :, s, :], sl.rearrange("p (cc x) -> p cc x", cc=4), bsl, [self.b_s5d])
            for cc in range(4):
                pb, bpb = self.bank()
                n = 0
                for jj in range(4):
                    q = cc * 4 + jj
                    for part in range(2):
                        self.mm(pb[:, 0:128], Zt[:, part, q, :], CTb[:, part, q, :], n == 0, n == 7,
                                [bZj[part][q % 4]] + bCTb, [bpb])
                        n += 1
                blk = cc * 8 + tau
                self.cp(ktf[:, blk * 128:(blk + 1) * 128], pb[:, 0:128], [bpb], [self.b_ktap])
        W2v = self.s5_d[:, 288 * 128:544 * 128].rearrange("p (cc l t x) -> p l t cc x", cc=4, l=8, t=2)
        Cc = self.s5t[:, 4, 0:512].rearrange("p (t q c) -> p t q c", t=2, q=16)
        for part in range(2):
            for gi in range(2):
                for j in range(4):
                    self.cp(Cc[64 * gi:64 * gi + 64, part, j::4, :],
                            CTf[64 * gi:64 * gi + 64, part, j::4, 32 * j + 16 * gi:32 * j + 16 * gi + 16],
                            self.b_bf, B4)
        w1_ = v3(self.s5t[:, 5, 0:256]); w2_ = v3(self.s5t[:, 5, 256:512])
        oc = self.s5t[:, 5, 512:1024].rearrange("p (t q c) -> p t q c", t=2, q=16)
        bW = [[[Buf() for _ in range(4)] for _ in range(2)] for _ in range(2)]
        for k in range(2):
            self.memset(stg[k], 0.0, bstg[k] + bW[k][0] + bW[k][1])
        for l in range(8):
            apr_b = apw[:, 0, l + 1, :].unsqueeze(2).to_broadcast([128, 16, 16])
            api_b = apw[:, 1, l + 1, :].unsqueeze(2).to_broadcast([128, 16, 16])
            sl = stg[si % 2]; bsl = bstg[si % 2]; bWs = bW[si % 2]; si += 1
            sl4 = sl.rearrange("p (t q c) -> p t q c", t=2, q=16)
            self.tt(w1_, Cc[:, 0], apr_b, ALU.mult, B4 + [bap], B5)
            self.tt(w2_, Cc[:, 1], api_b, ALU.mult, B4 + [bap], B5)
            self.tt(oc[:, 0], w1_, w2_, ALU.add, B5, B5)
            self.tt(w1_, Cc[:, 1], apr_b, ALU.mult, B4 + [bap], B5)
            self.tt(w2_, Cc[:, 0], api_b, ALU.mult, B4 + [bap], B5)
            self.tt(oc[:, 1], w1_, w2_, ALU.subtract, B5, B5)
            n = 0
            for part in range(2):
                for gi in range(2):
                    for j in range(4):
                        dst = sl4[64 * gi:64 * gi + 64, part, j::4, 32 * j + 16 * gi:32 * j + 16 * gi + 16]
                        src = oc[64 * gi:64 * gi + 64, part, j::4, :]
                        if n % 2 == 0:
                            self.cp(dst, src, B5, [bWs[part][j]])
                        else:
                            self.act(dst, src, AF.Copy, B5, [bWs[part][j]])
                        n += 1
            slv = sl.rearrange("p (t cc x) -> p t cc x", t=2, cc=4)
            for part in range(2):
                self.dma("sp", W2v[:, l, part], slv[:, part], bWs[part], [self.b_s5d])
        if self.dbg:
            self.dump("s5d", self.s5_d, [self.b_s5d], BF16)
            self.dump("apw", self.apw[:], [self.b_apw])
            self.dump("c8T", self.c8T[:], [self.b_tab])
            self.dump("s8T", self.s8T[:], [self.b_tab])
            self.dump("r8T", self.r8T[:], [self.b_tab])
            self.dump("ctf", self.bigf[:], self.b_bf)

    def scr_next(self):
        i = self.scr_rr % 8
        self.scr_rr = i + 1
        return self.scr[:].rearrange("p a b -> p (a b)")[:, i * 256:(i + 1) * 256], self.hb_scr[i]

    def scr_full(self):
        i0 = ((self.scr_rr + 1) // 2 * 2) % 8
        self.scr_rr = (i0 + 2) % 8
        return self.scr[:, i0 // 2, :], [self.hb_scr[i0], self.hb_scr[i0 + 1]]

    def load_x(self, ti):
        t0 = ti * T
        self.dma("sp", self.s5t[:, 0:4, :], self.x_d[t0:t0 + T, :].rearrange("(t p) d -> p t d", p=128), [],
                 self.hb_s5t[0:4])

    def tile(self, ti):
        t0 = ti * T
        HS = (0, 1)
        xtok = self.bigf[:].rearrange("p a b -> p (a b)").rearrange("p (t d) -> p t d", t=4)
        allbf = [b for h in HS for b in self.hb_bf[h]]
        xin = self.s5t[:, 0:4, :]
        if ti == 0:
            self.load_x(0)
        for h in HS:
            for dc in range(8):
                pb, bpb = self.bank()
                for k in range(2):
                    tb = 2 * h + k
                    self.tr(pb[:, k * 128:(k + 1) * 128], xin[:, tb, dc * 128:(dc + 1) * 128], [self.hb_s5t[tb]], [bpb])
                dst = self.xT[:, dc, h * 256:(h + 1) * 256]
                if dc % 2 == 0:
                    self.act(dst, pb[:, 0:256], AF.Copy, [bpb], [self.hb_xT[h][dc]])
                else:
                    self.cp(dst, pb[:, 0:256], [bpb], [self.hb_xT[h][dc]])
        prefetched = False
        for ph in self.phases:
            if ph == "l0mix":
                self.l0_mixer(ti)
            elif ph == "l1mix":
                self.l1_mixer(ti)
                if ti + 1 < self.ntiles:
                    self.load_x(ti + 1)
                    prefetched = True
            elif ph in ("ca0", "ca1"):
                self.cross_attn(int(ph[2]))
            elif ph in ("ffn0", "ffn1"):
                self.ffn(int(ph[3]))
        if ti + 1 < self.ntiles and not prefetched:
            self.load_x(ti + 1)
        otok = xtok
        for tb in range(4):
            h = tb // 2
            cs = slice(tb * 128, (tb + 1) * 128)
            banks = []
            sm = self.small
            ssq = sm[:, 32 + 2 * (tb % 2):34 + 2 * (tb % 2)]
            bss = self.hb_small[h][4 + tb % 2]
            for half in range(2):
                pb, bpb = self.bank()
                for j in range(4):
                    dc = half * 4 + j
                    self.tr(pb[:, j * 128:(j + 1) * 128], self.xT[:, dc, cs], [self.hb_xT[h][dc]], [bpb])
                banks.append((pb, bpb))
                if self.final_norm:
                    s_ = self.scr[:, (self.scr_rr // 2) % 4, :]
                    k0 = ((self.scr_rr // 2) % 4) * 2
                    self.scr_rr = (k0 + 2) % 8
                    bs_ = [self.hb_scr[k0], self.hb_scr[k0 + 1]]
                    self.act(s_, pb, AF.Square, [bpb], bs_ + [bss], accum=ssq[:, half:half + 1])
            if self.final_norm:
                rs = sm[:, 36 + (tb % 2):37 + (tb % 2)]
                brs = self.hb_small[h][6 + tb % 2]
                self.tt(rs, ssq[:, 0:1], ssq[:, 1:2], ALU.add, [bss], [brs])
                self.act(rs, rs, AF.Sqrt, [brs, self.b_const], [brs], bias=self.epsb[:], scale=1.0 / D)
                self.recip(rs, rs, [brs], [brs])
                for half in range(2):
                    pb, bpb = banks[half]
                    self.stt(otok[:, tb, half * 512:(half + 1) * 512], pb, rs, self.fg[:, half * 512:(half + 1) * 512],
                             ALU.mult, ALU.mult, [bpb, brs, self.b_const],
                             [self.hb_bf[0][2 * tb + half], self.hb_bf[1][2 * tb + half]])
            else:
                for half in range(2):
                    pb, bpb = banks[half]
                    self.cp(otok[:, tb, half * 512:(half + 1) * 512], pb, [bpb],
                            [self.hb_bf[0][2 * tb + half], self.hb_bf[1][2 * tb + half]])
        self.dma("sp", self.out_d[t0:t0 + T, :].rearrange("(t p) d -> p t d", p=128), otok, allbf, [self.b_out])

    def rmsnorm(self, h, gcol):
        cs = slice(h * 256, (h + 1) * 256)
        self.rms_generic(lambda dc: self.xT[:, dc, cs], self.hb_xT[h], 256, gcol,
                         lambda dc: self.xn[:, dc, cs], self.hb_xn[h],
                         rstd=self.rstd[:, cs], brstd=self.hb_rstd[h], sqbufs=self.hb_sq)

    def resid_add(self, h, dc, pb, bpb):
        x = self.xT[:, dc, h * 256:(h + 1) * 256]
        self.tt(x, x, pb[:, 0:256], ALU.add, [self.hb_xT[h][dc], bpb], [self.hb_xT[h][dc]])

    def l0_mixer(self, ti):
        p = self.p
        bb, bf = self.bigbf, self.bigf
        HS = (0, 1)
        CS = [slice(0, 256), slice(256, 512)]
        for h in HS:
            self.rmsnorm(h, G_E)
        w0, bw0 = self.load_w(p["e_w_in"][0], 0, 1024)
        for h in HS:
            bbb, bbf, bxn = self.hb_bb[h], self.hb_bf[h], self.hb_xn[h]
            for fc in range(4):
                pb, bpb = self.bank()
                for dc in range(8):
                    self.mm(pb[:, 0:256], w0[:, dc, fc * 128:(fc + 1) * 128], self.xn[:, dc, CS[h]], dc == 0, dc == 7,
                            [bw0, bxn[dc]], [bpb])
                self.act(bf[:, fc, CS[h]], pb[:, 0:256], AF.Gelu, [bpb], [bbf[fc]])
            for k in range(2):
                tb = 2 * h + k
                pb, bpb = self.bank()
                for dc in range(8):
                    self.mm(pb, self.xn[:, dc, tb * 128:(tb + 1) * 128], w0[:, dc, 512:1024], dc == 0, dc == 7,
                            [bw0, bxn[dc]], [bpb])
                i0 = ((self.scr_rr + 1) // 2 * 2) % 8
                self.scr_rr = (i0 + 2) % 8
                s_ = self.scr[:, i0 // 2, :]
                bs_ = [self.hb_scr[i0], self.hb_scr[i0 + 1]]
                self.act(s_, pb, AF.Gelu, [bpb], bs_)
                sm = self.small
                st6 = sm[:, 8 * k:8 * k + 6]; mv = sm[:, 16 + 4 * k:16 + 4 * k + 2]
                rv = sm[:, 16 + 4 * k + 2:16 + 4 * k + 3]
                bsm = self.hb_small[0][k]
                self.S.op("dve", lambda hh, st6=st6, s_=s_: hh.bn_stats(st6, s_), reads=bs_, writes=[bsm])
                self.S.op("dve", lambda hh, st6=st6, mv=mv: hh.bn_aggr(mv, st6), reads=[bsm], writes=[bsm])
                self.act(rv, mv[:, 1:2], AF.Sqrt, [bsm, self.b_const], [bsm], bias=self.epsb[:], scale=1.0)
                self.recip(rv, rv, [bsm], [bsm])
                self.ts(bb[:, 8 + tb, :], s_, mv[:, 0:1], rv, ALU.subtract, ALU.mult, bs_ + [bsm],
                        [self.hb_bb[0][8 + tb], self.hb_bb[1][8 + tb]])
        w1, bw1 = self.load_w(p["e_w_in"][0], 1024, 1024)
        for h in HS:
            bxn = self.hb_xn[h]
            for fc in range(4):
                pa, bpa = self.bank()
                for dc in range(8):
                    self.mm(pa[:, 0:256], w1[:, dc, fc * 128:(fc + 1) * 128], self.xn[:, dc, CS[h]], dc == 0, dc == 7,
                            [bw1, bxn[dc]], [bpa])
                pg, bpg = self.bank()
                for dc in range(8):
                    self.mm(pg[:, 0:256], w1[:, dc, 512 + fc * 128:512 + (fc + 1) * 128], self.xn[:, dc, CS[h]],
                            dc == 0, dc == 7, [bw1, bxn[dc]], [bpg])
                s_, bs_ = self.scr_next()
                self.act(s_, pg[:, 0:256], AF.Sigmoid, [bpg], [bs_])
                self.tt(self.hbuf[:, fc, 30 + h * 256:30 + (h + 1) * 256], pa[:, 0:256], s_, ALU.mult, [bpa, bs_],
                        [self.hb_h[fc][1 + h]])
        for h in HS:
            bbb, bbf = self.hb_bb[h], self.hb_bf[h]
            for g in range(4):
                pb, bpb = self.bank()
                for k in range(2):
                    tb = 2 * h + k
                    o = pb[:, k * 128:(k + 1) * 128]
                    self.mm(o, bb[:, 8 + tb, g * 128:(g + 1) * 128], self.gw[:, g, :], True, True,
                            [self.hb_bb[0][8 + tb], self.hb_bb[1][8 + tb], self.b_const], [bpb])
                s_, bs_ = self.scr_next()
                self.tt(s_.rearrange("p (k i) -> p k i", k=2), pb[:, 0:256].rearrange("p (k i) -> p k i", k=2),
                        self.bsbc[:, g, :].unsqueeze(1).to_broadcast([128, 2, 128]), ALU.add, [bpb, self.b_const], [bs_])
                self.tt(bb[:, g, CS[h]], bf[:, g, CS[h]], s_, ALU.mult, [bbf[g], bs_], [bbb[g]])
        for half in range(2):
            dg, bdg = self.load_slab(self.diag_d[half].rearrange("p (a b) -> p a b", b=128), 62, 128, q="sp",
                                     R=[self.b_diag])
            if half == 0:
                for f2 in range(2):
                    fc = half * 2 + f2
                    base = f2 * 31
                    pb, bpb = self.bank()
                    for k in range(31):
                        self.mm(pb, dg[:, base + k, :], self.hbuf[:, fc, k:k + 512], k == 0, k == 30,
                                [bdg] + self.hb_h[fc], [bpb])
                    self.act(bf[:, 4 + fc, :], pb, AF.Identity, [bpb, self.b_const],
                             [self.hb_bf[0][4 + fc], self.hb_bf[1][4 + fc]],
                             bias=self.gv[:, G_CB + fc:G_CB + fc + 1], scale=1.0)
                    self.cp(self.hbuf[:, fc, 0:30], self.hbuf[:, fc, 512:542], [self.hb_h[fc][2]], [self.hb_h[fc][0]])
            else:
                for h in HS:
                    for f2 in range(2):
                        fc = half * 2 + f2
                        base = f2 * 31
                        pb, bpb = self.bank()
                        Rh = [bdg, self.hb_h[fc][0], self.hb_h[fc][1]] if h == 0 else [bdg, self.hb_h[fc][1], self.hb_h[fc][2]]
                        for k in range(31):
                            self.mm(pb[:, 0:256], dg[:, base + k, :], self.hbuf[:, fc, h * 256 + k:h * 256 + k + 256],
                                    k == 0, k == 30, Rh, [bpb])
                        self.act(bf[:, 4 + fc, CS[h]], pb[:, 0:256], AF.Identity, [bpb, self.b_const], [self.hb_bf[h][4 + fc]],
                                 bias=self.gv[:, G_CB + fc:G_CB + fc + 1], scale=1.0)
                        if h == 1:
                            self.cp(self.hbuf[:, fc, 0:30], self.hbuf[:, fc, 512:542], [self.hb_h[fc][2]], [self.hb_h[fc][0]])
                    self.l0_ln(h)
        wo, bwo = self.load_w(p["e_w_out"][0], 0, 1024)
        for h in HS:
            bbb = self.hb_bb[h]
            for dc in range(8):
                pb, bpb = self.bank()
                for kc in range(8):
                    self.mm(pb[:, 0:256], wo[:, kc, dc * 128:(dc + 1) * 128], bb[:, kc, CS[h]], kc == 0, kc == 7,
                            [bwo, bbb[kc]], [bpb])
                self.resid_add(h, dc, pb, bpb)

    def l0_ln(self, h):
        bb, bf = self.bigbf, self.bigf
        CS = [slice(0, 256), slice(256, 512)]
        bbb, bbf = self.hb_bb[h], self.hb_bf[h]
        pm, bpm = self.bank()
        for fc in range(4):
            s_, bs_ = self.scr_next()
            sb_ = s_.bitcast(BF16)[:, 0:256]
            self.act(sb_, bf[:, 4 + fc, CS[h]], AF.Copy, [bbf[4 + fc]], [bs_])
            self.mm(pm[:, 0:256], self.ones_bf[:], sb_, fc == 0, fc == 3, [bs_, self.b_const], [bpm])
        pq, bpq = self.bank()
        for fc in range(4):
            s_, bs_ = self.scr_next()
            sb_ = s_.bitcast(BF16)[:, 0:256]
            self.act(sb_, bf[:, 4 + fc, CS[h]], AF.Square, [bbf[4 + fc]], [bs_])
            self.mm(pq[:, 0:256], self.ones_bf[:], sb_, fc == 0, fc == 3, [bs_, self.b_const], [bpq])
        mean = self.aux[:, 0, CS[h]]; rs = self.aux[:, 1, CS[h]]
        bmean, brs = self.hb_aux[h]
        self.ts(mean, pm[:, 0:256], 1.0 / 512, None, ALU.mult, None, [bpm], [bmean])
        self.tt(rs, mean, mean, ALU.mult, [bmean], [brs])
        self.stt(rs, pq[:, 0:256], 1.0 / 512, rs, ALU.mult, ALU.subtract, [bpq, brs], [brs])
        self.act(rs, rs, AF.Sqrt, [brs, self.b_const], [brs], bias=self.epsb[:], scale=1.0)
        self.recip(rs, rs, [brs], [brs])
        for fc in range(4):
            s_, bs_ = self.scr_next()
            self.tt(s_, bf[:, 4 + fc, CS[h]], mean, ALU.subtract, [bbf[4 + fc], bmean], [bs_])
            self.tt(s_, s_, rs, ALU.mult, [bs_, brs], [bs_])
            self.act(bb[:, 4 + fc, CS[h]], s_, AF.Silu, [bs_, self.b_const], [bbb[4 + fc]],
                     bias=self.gv[:, G_LB + fc:G_LB + fc + 1], scale=self.gv[:, G_LG + fc:G_LG + fc + 1])

    def cross_attn(self, li):
        p = self.p
        bb = self.bigbf
        HS = (0, 1)
        CS = [slice(0, 256), slice(256, 512)]
        for h in HS:
            self.rmsnorm(h, G_CA + 8 * li)
        wq, bwq = self.load_w(p["ca_wq"][li], 0, 1024)
        for h in HS:
            bbb, bxn = self.hb_bb[h], self.hb_xn[h]
            for oc in range(8):
                pb, bpb = self.bank()
                for dc in range(8):
                    self.mm(pb[:, 0:256], wq[:, dc, oc * 128:(oc + 1) * 128], self.xn[:, dc, CS[h]], dc == 0, dc == 7,
                            [bwq, bxn[dc]], [bpb])
                self.act(bb[:, oc, CS[h]], pb[:, 0:256], AF.Copy, [bpb], [bbb[oc]], scale=1.0 / 16)
        for h in HS:
            bbb = self.hb_bb[h]
            for hd in range(4):
                pc = 16 + 2 * (hd % 2)
                for mc in range(2):
                    pb, bpb = self.bank()
                    for j in range(2):
                        self.mm(pb[:, 0:256], self.KT[:, li, 2 * hd + j, mc * 128:(mc + 1) * 128], bb[:, 2 * hd + j, CS[h]],
                                j == 0, j == 1, [self.b_KT, bbb[2 * hd + j]], [bpb])
                    self.act(bb[:, pc + mc, CS[h]], pb[:, 0:256], AF.Exp, [bpb], [bbb[pc + mc]])
                pd, bpd = self.bank()
                for mc in range(2):
                    self.mm(pd[:, 0:256], self.ones_bf[:], bb[:, pc + mc, CS[h]], mc == 0, mc == 1,
                            [bbb[pc + mc], self.b_const], [bpd])
                rden = self.aux[:, hd % 2, CS[h]]; brd = self.hb_aux[h][hd % 2]
                self.recip(rden, pd[:, 0:256], [bpd], [brd])
                for j in range(2):
                    po, bpo = self.bank()
                    for mc in range(2):
                        self.mm(po[:, 0:256], self.V[:, li, mc, (2 * hd + j) * 128:(2 * hd + j + 1) * 128], bb[:, pc + mc, CS[h]],
                                mc == 0, mc == 1, [self.b_V, bbb[pc + mc]], [bpo])
                    self.tt(bb[:, 8 + 2 * hd + j, CS[h]], po[:, 0:256], rden, ALU.mult, [bpo, brd], [bbb[8 + 2 * hd + j]])
        wo, bwo = self.load_w(p["ca_wo"][li], 0, 1024)
        for h in HS:
            bbb = self.hb_bb[h]
            for dc in range(8):
                pb, bpb = self.bank()
                for kc in range(8):
                    self.mm(pb[:, 0:256], wo[:, kc, dc * 128:(dc + 1) * 128], bb[:, 8 + kc, CS[h]], kc == 0, kc == 7,
                            [bwo, bbb[8 + kc]], [bpb])
                self.resid_add(h, dc, pb, bpb)

    def ffn(self, li):
        p = self.p
        bb = self.bigbf
        HS = (0, 1)
        CS = [slice(0, 256), slice(256, 512)]
        for h in HS:
            self.rmsnorm(h, G_FFN + 8 * li)
        for s in range(6):
            nh = 4 if s < 5 else 2
            i = self.ring_rr
            self.ring_rr = (i + 1) % NSLOT
            slot = self.ring[i][:, 0:8192].rearrange("p (a b) -> p a b", b=1024)
            bsl = self.b_ring[i]
            c0 = s * 512
            for k, wname in enumerate(("ffn_w_gate", "ffn_w_up")):
                src = p[wname][li][:, c0:c0 + nh * 128].rearrange("(c p) f -> p c f", p=128)
                self.dma("pool", slot[:, :, k * 512:k * 512 + nh * 128], src, [], [bsl])
            if True:
                for h in HS:
                    bbb, bxn = self.hb_bb[h], self.hb_xn[h]
                    for j in range(nh):
                        hc = s * 4 + j
                        pg, bpg = self.bank()
                        for dc in range(8):
                            self.mm(pg[:, 0:256], slot[:, dc, j * 128:(j + 1) * 128], self.xn[:, dc, CS[h]], dc == 0, dc == 7,
                                    [bsl, bxn[dc]], [bpg])
                        pu, bpu = self.bank()
                        for dc in range(8):
                            self.mm(pu[:, 0:256], slot[:, dc, 512 + j * 128:512 + (j + 1) * 128], self.xn[:, dc, CS[h]],
                                    dc == 0, dc == 7, [bsl, bxn[dc]], [bpu])
                        s_, bs_ = self.scr_next()
                        self.act(s_, pg[:, 0:256], AF.Silu, [bpg], [bs_])
                        self.tt(bb[:, hc, CS[h]], s_, pu[:, 0:256], ALU.mult, [bs_, bpu], [bbb[hc]])
            else:
                for j in range(nh):
                    hc = s * 4 + j
                    pg, bpg = self.bank()
                    for dc in range(8):
                        self.mm(pg, slot[:, dc, j * 128:(j + 1) * 128], self.xn[:, dc, :], dc == 0, dc == 7,
                                [bsl, self.hb_xn[0][dc], self.hb_xn[1][dc]], [bpg])
                    pu, bpu = self.bank()
                    for dc in range(8):
                        self.mm(pu, slot[:, dc, 512 + j * 128:512 + (j + 1) * 128], self.xn[:, dc, :], dc == 0, dc == 7,
                                [bsl, self.hb_xn[0][dc], self.hb_xn[1][dc]], [bpu])
                    s_, bs_ = self.scr_full()
                    self.act(s_, pg, AF.Silu, [bpg], bs_)
                    self.tt(bb[:, hc, :], s_, pu, ALU.mult, bs_ + [bpu], [self.hb_bb[0][hc], self.hb_bb[1][hc]])
        for s in range(3):
            ndc = 3 if s < 2 else 2
            d0 = s * 3
            wd, bwd = self.load_w(p["ffn_w_down"][li], d0 * 128, ndc * 128)
            if True:
                for h in HS:
                    bbb = self.hb_bb[h]
                    for j in range(ndc):
                        dc = d0 + j
                        pb, bpb = self.bank()
                        for hc in range(HC):
                            self.mm(pb[:, 0:256], wd[:, hc, j * 128:(j + 1) * 128], bb[:, hc, CS[h]], hc == 0, hc == HC - 1,
                                    [bwd, bbb[hc]], [bpb])
                        self.resid_add(h, dc, pb, bpb)
            else:
                for j in range(ndc):
                    dc = d0 + j
                    pb, bpb = self.bank()
                    for hc in range(HC):
                        self.mm(pb, wd[:, hc, j * 128:(j + 1) * 128], bb[:, hc, :], hc == 0, hc == HC - 1,
                                [bwd, self.hb_bb[0][hc], self.hb_bb[1][hc]], [bpb])
                    bx = [self.hb_xT[0][dc], self.hb_xT[1][dc]]
                    self.tt(self.xT[:, dc, :], self.xT[:, dc, :], pb, ALU.add, bx + [bpb], bx)

    def l1_mixer(self, ti):
        p = self.p
        bb, bf = self.bigbf, self.bigf
        HS = (0, 1)
        CS = [slice(0, 256), slice(256, 512)]
        c, bc = self.s5c, self.b_s5c
        both = lambda lst, k: [lst[0][k], lst[1][k]]
        for h in HS:
            self.rmsnorm(h, G_O)
        w_, bw = self.load_w(p["o_w_in"][0], 0, 512)
        for h in HS:
            bxn = self.hb_xn[h]
            for cc in range(4):
                pb, bpb = self.bank()
                for dc in range(8):
                    self.mm(pb[:, 0:256], w_[:, dc, cc * 128:(cc + 1) * 128], self.xn[:, dc, CS[h]], dc == 0, dc == 7,
                            [bw, bxn[dc]], [bpb])
                self.act(bf[:, cc, CS[h]], pb[:, 0:256], AF.Copy, [bpb], [self.hb_bf[h][cc]])
                self.cp(bb[:, cc, CS[h]], pb[:, 0:256], [bpb], [self.hb_bb[h][cc]])
        self.ps_rr = 0
        py = [(self.psum[:, i, :], self.b_ps[i]) for i in range(4)]
        pre = self.psum[:, 4:6, :].rearrange("p a b -> p (a b)"); bpre = [self.b_ps[4], self.b_ps[5]]
        pim = self.psum[:, 6:8, :].rearrange("p a b -> p (a b)"); bpim = [self.b_ps[6], self.b_ps[7]]
        u3 = [bb[:, cc, :].rearrange("p (n s) -> p n s", s=8) for cc in range(4)]
        for cc in range(4):
            sw, bsw = self.load_slab(self.s5_d[:, cc * 8192:(cc + 1) * 8192].rearrange("p (a b) -> p a b", b=128),
                                     64, 128, q="sp", R=[self.b_s5d])
            for jj in range(4):
                q = cc * 4 + jj
                for part in range(2):
                    o = (pre if part == 0 else pim)[:, q * 64:(q + 1) * 64]
                    bo = (bpre if part == 0 else bpim)[q // 8]
                    for s in range(8):
                        self.mm(o, sw[:, (s * 4 + jj) * 2 + part, :], u3[cc][:, :, s], s == 0, s == 7,
                                [bsw] + both(self.hb_bb, cc), [bo])
        cosf = self.c8T[:].rearrange("p q l -> p (q l)")
        sinf = self.s8T[:].rearrange("p q l -> p (q l)")
        rTf = self.r8T[:].rearrange("p q l -> p (q l)")
        tabR = [self.b_tab]
        t = [self.s5t[:, i, :] for i in range(6)]
        bt = self.hb_s5t
        xs_re = bb[:, 8:10, :].rearrange("p a b -> p (a b)"); bxr = both(self.hb_bb, 8) + both(self.hb_bb, 9)
        xs_im = bb[:, 10:12, :].rearrange("p a b -> p (a b)"); bxi = both(self.hb_bb, 10) + both(self.hb_bb, 11)
        ar = c[:, 0, :]; ai = c[:, 1, :]; xpr = c[:, 2, :]; xpi = c[:, 3, :]
        u1 = c[:, 4, :]; u2 = c[:, 5, :]
        self.tt(t[0], pre, cosf, ALU.mult, bpre + tabR, [bt[0]])
        self.tt(t[1], pim, sinf, ALU.mult, bpim + tabR, [bt[1]])
        self.tt(t[0], t[0], t[1], ALU.add, [bt[0], bt[1]], [bt[0]])
        self.tt(t[2], pim, cosf, ALU.mult, bpim + tabR, [bt[2]])
        self.tt(t[3], pre, sinf, ALU.mult, bpre + tabR, [bt[3]])
        self.tt(t[2], t[2], t[3], ALU.subtract, [bt[2], bt[3]], [bt[2]])
        if ti > 0:
            w_re0 = t[0].rearrange("p (q l) -> p q l", l=64)[:, :, 0:1]
            w_im0 = t[2].rearrange("p (q l) -> p q l", l=64)[:, :, 0:1]
            self.tt(u1, ar, xpr, ALU.mult, [bc[0], bc[2]], [bc[4]])
            self.tt(u2, ai, xpi, ALU.mult, [bc[1], bc[3]], [bc[5]])
            self.tt(u1, u1, u2, ALU.subtract, [bc[4], bc[5]], [bc[4]])
            self.tt(w_re0, w_re0, u1.unsqueeze(2), ALU.add, [bt[0], bc[4]], [bt[0]])
            self.tt(u1, ar, xpi, ALU.mult, [bc[0], bc[3]], [bc[4]])
            self.tt(u2, ai, xpr, ALU.mult, [bc[1], bc[2]], [bc[5]])
            self.tt(u1, u1, u2, ALU.add, [bc[4], bc[5]], [bc[4]])
            self.tt(w_im0, w_im0, u1.unsqueeze(2), ALU.add, [bt[2], bc[4]], [bt[2]])
        self.S.op("dve", lambda hh, o=t[1], a=rTf, b=t[0]: hh.tensor_tensor_scan(o, a, b, 0.0, ALU.mult, ALU.add),
                  reads=[bt[0]] + tabR, writes=[bt[1]])
        self.S.op("dve", lambda hh, o=t[3], a=rTf, b=t[2]: hh.tensor_tensor_scan(o, a, b, 0.0, ALU.mult, ALU.add),
                  reads=[bt[2]] + tabR, writes=[bt[3]])
        self.tt(t[0], t[1], cosf, ALU.mult, [bt[1]] + tabR, [bt[0]])
        self.tt(t[2], t[3], sinf, ALU.mult, [bt[3]] + tabR, [bt[2]])
        self.tt(t[4], t[0], t[2], ALU.subtract, [bt[0], bt[2]], [bt[4]])
        self.tt(t[0], t[1], sinf, ALU.mult, [bt[1]] + tabR, [bt[0]])
        self.tt(t[2], t[3], cosf, ALU.mult, [bt[3]] + tabR, [bt[2]])
        self.tt(t[5], t[0], t[2], ALU.add, [bt[0], bt[2]], [bt[5]])
        x_re3 = t[4].rearrange("p (q l) -> p q l", l=64)
        x_im3 = t[5].rearrange("p (q l) -> p q l", l=64)
        xs_re3 = xs_re.rearrange("p (q l) -> p q l", l=64)
        xs_im3 = xs_im.rearrange("p (q l) -> p q l", l=64)
        self.act(xs_re3[:, :, 0:1], xpr.unsqueeze(2), AF.Copy, [bc[2]], bxr)
        self.act(xs_im3[:, :, 0:1], xpi.unsqueeze(2), AF.Copy, [bc[3]], bxi)
        self.act(xs_re3[:, :, 1:64], x_re3[:, :, 0:63], AF.Copy, [bt[4]], bxr)
        self.act(xs_im3[:, :, 1:64], x_im3[:, :, 0:63], AF.Copy, [bt[5]], bxi)
        self.cp(xpr.unsqueeze(2), x_re3[:, :, 63:64], [bt[4]], [bc[2]])
        self.cp(xpi.unsqueeze(2), x_im3[:, :, 63:64], [bt[5]], [bc[3]])
        kt, bkt = self.ktap, self.b_ktap
        for cc in range(4):
            o3 = py[cc][0].rearrange("p (n l) -> p n l", l=8)
            bo = py[cc][1]
            Ru = [bkt] + both(self.hb_bb, cc)
            self.mm(py[cc][0], kt[:, cc * 8, :], bb[:, cc, :], True, False, Ru, [bo])
            for l in range(1, 8):
                for tau in range(1, l + 1):
                    self.mm(o3[:, :, l], kt[:, cc * 8 + tau, :], u3[cc][:, :, l - tau], False, False, Ru, [bo])
        for cc in range(4):
            sw, bsw = self.load_slab(self.s5_d[:, (288 + cc * 64) * 128:(288 + (cc + 1) * 64) * 128]
                                     .rearrange("p (a b) -> p a b", b=128), 64, 128, q="sp", R=[self.b_s5d])
            o3 = py[cc][0].rearrange("p (n l) -> p n l", l=8)
            bo = py[cc][1]
            for l in range(8):
                for jj in range(4):
                    q = cc * 4 + jj
                    for part in range(2):
                        xs = xs_re if part == 0 else xs_im
                        bx = bxr if part == 0 else bxi
                        self.mm(o3[:, :, l], sw[:, (l * 2 + part) * 4 + jj, :], xs[:, q * 64:(q + 1) * 64], False,
                                (l == 7 and jj == 3 and part == 1), [bsw] + bx, [bo])
        for h in HS:
            for cc in range(4):
                s_, bs_ = self.scr_next()
                self.stt(s_, bf[:, cc, CS[h]], self.gv[:, G_OD + cc:G_OD + cc + 1], py[cc][0][:, CS[h]], ALU.mult, ALU.add,
                         [self.hb_bf[h][cc], py[cc][1], self.b_const], [bs_])
                self.act(bb[:, 4 + cc, CS[h]], s_, AF.Gelu, [bs_], [self.hb_bb[h][4 + cc]])
        wo, bwo = self.load_slab(p["o_w_out"][0].rearrange("(c p) f -> p c f", p=128), 4, 2048)
        for h in HS:
            bbb = self.hb_bb[h]
            for dc in range(8):
                pa, bpa = self.bank()
                for kc in range(4):
                    self.mm(pa[:, 0:256], wo[:, kc, dc * 128:(dc + 1) * 128], bb[:, 4 + kc, CS[h]], kc == 0, kc == 3,
                            [bwo, bbb[4 + kc]], [bpa])
                pg, bpg = self.bank()
                for kc in range(4):
                    self.mm(pg[:, 0:256], wo[:, kc, 1024 + dc * 128:1024 + (dc + 1) * 128], bb[:, 4 + kc, CS[h]],
                            kc == 0, kc == 3, [bwo, bbb[4 + kc]], [bpg])
                s_, bs_ = self.scr_next()
                self.act(s_, pg[:, 0:256], AF.Sigmoid, [bpg], [bs_])
                self.tt(s_, pa[:, 0:256], s_, ALU.mult, [bpa, bs_], [bs_])
                x = self.xT[:, dc, CS[h]]
                self.tt(x, x, s_, ALU.add, [self.hb_xT[h][dc], bs_], [self.hb_xT[h][dc]])


ALL_PHASES = ("l0mix", "ca0", "ffn0", "l1mix", "ca1", "ffn1")


def build_nc(phases=ALL_PHASES, ntiles=SEQ // T, final_norm=True, dbg=False):
    nc = bass.Bass("TRN2", target_bir_lowering=False)
    kb = KB(nc, phases, ntiles, final_norm)
    kb.dbg = dbg
    kb.build()
    return nc


def make_in_maps(inputs):
    ident = np.eye(128, dtype=np.float32)
    maps = []
    for b in range(8):
        m = {"x": np.ascontiguousarray(inputs["x"][b]), "mem": np.ascontiguousarray(inputs["mem"][b]),
             "c_ident": ident}
        for k in PARAM_SHAPES:
            m[k] = np.ascontiguousarray(np.asarray(inputs[k], dtype=np.float32))
        maps.append(m)
    return maps


def kernel(**inputs):
    inputs = {k: np.asarray(v) for k, v in inputs.items()}
    nc = build_nc()
    res = run_bass_kernel_spmd(nc, make_in_maps(inputs), core_ids=list(range(8)))
    return np.stack([res.results[b]["out"] for b in range(8)], axis=0).astype(np.float32)
```
